# Optimizing a Trainium2 kernel written in Bass

```python
import jax, jax.numpy as jnp
from jax import lax
import numpy as np

D_MODEL = 1024
BATCH = 2
SEQ = 8192
DEPTH = 4

N_MIXERS = 4
N_HEADS = 8
HEAD_DIM = D_MODEL // N_HEADS
D_FF = 4 * D_MODEL
ROPE_THETA = 10000.0
NORM_EPS = 1e-6
NEG_INF = -1e30

NSA_KV_GROUPS = 2
NSA_HEADS_PER_GROUP = N_HEADS // NSA_KV_GROUPS
NSA_CMP_BLOCK = 32
NSA_CMP_STRIDE = 16
NSA_SEL_BLOCK = 64
NSA_SEL_TOPK = 16
NSA_WINDOW = 512
NSA_QBLOCK = 64
NSA_FORCE_BONUS = 1000.0
NSA_IN_DIM = N_HEADS * HEAD_DIM + 6 * NSA_KV_GROUPS * HEAD_DIM + 3 * N_HEADS

SB_QBLOCK = 128

CONV_WIDTH = 31

MOBA_BLOCK = 256
MOBA_TOPK = 3
MOBA_QBLOCK = 32

N_NSA = (DEPTH + 3) // 4
N_SB = (DEPTH + 2) // 4
N_CONV = (DEPTH + 1) // 4
N_MOBA = DEPTH // 4

kernel_name = 'hybrid_nsa_stickbreak_conformer_moba_trunk'


def _rms_norm(x, g):
    xf = x.astype(jnp.float32)
    y = xf * lax.rsqrt(jnp.mean(xf * xf, axis=-1, keepdims=True) + NORM_EPS)
    return (y * g.astype(jnp.float32)).astype(x.dtype)


def _rope_tables(seq):
    half = HEAD_DIM // 2
    inv_freq = ROPE_THETA ** (-jnp.arange(half, dtype=jnp.float32) / half)
    ang = jnp.arange(seq, dtype=jnp.float32)[:, None] * inv_freq[None, :]
    return jnp.cos(ang), jnp.sin(ang)


def _apply_rope(x, cos, sin):
    xf = x.astype(jnp.float32)
    x1, x2 = jnp.split(xf, 2, axis=-1)
    c = cos[None, :, None, :]
    s = sin[None, :, None, :]
    return jnp.concatenate([x1 * c - x2 * s, x1 * s + x2 * c], axis=-1).astype(x.dtype)


def _masked_softmax(logits, mask):
    z = jnp.where(mask, logits.astype(jnp.float32), NEG_INF)
    return jnp.where(mask, jax.nn.softmax(z, axis=-1), 0.0)


def _nsa_mixer(h, w_in, q_gain, k_gain, cmp_pos, w_cmp, w_out, cos, sin):
    B, S, _ = h.shape
    H, G, HG, DH = N_HEADS, NSA_KV_GROUPS, NSA_HEADS_PER_GROUP, HEAD_DIM
    L, STR, SB, W, QB = NSA_CMP_BLOCK, NSA_CMP_STRIDE, NSA_SEL_BLOCK, NSA_WINDOW, NSA_QBLOCK
    sizes = [H * DH] + [G * DH] * 6
    q, kc, vc, ks, vs, kw, vw, gates = jnp.split(h @ w_in, np.cumsum(sizes).tolist(), axis=-1)
    q = _apply_rope(_rms_norm(q.reshape(B, S, H, DH), q_gain), cos, sin)
    kc, ks, kw = [_apply_rope(_rms_norm(k.reshape(B, S, G, DH), k_gain[i]), cos, sin)
                  for i, k in enumerate((kc, ks, kw))]
    vc, vs, vw = [v.reshape(B, S, G, DH) for v in (vc, vs, vw)]
    gates = jax.nn.sigmoid(gates.astype(jnp.float32)).reshape(B, S, 3, G, HG)
    n_cmp = (S - L) // STR + 1
    cmp_start = jnp.arange(n_cmp) * STR
    cidx = cmp_start[:, None] + jnp.arange(L)[None, :]
    kc = jnp.einsum('bnlgd,lde->bnge', kc[:, cidx] + cmp_pos[0][None, None, :, None, :], w_cmp[0])
    vc = jnp.einsum('bnlgd,lde->bnge', vc[:, cidx] + cmp_pos[1][None, None, :, None, :], w_cmp[1])
    cmp_end = cmp_start + L - 1
    n_sel = S // SB
    sel_start = jnp.arange(n_sel) * SB
    overlap = ((cmp_start[:, None] < sel_start[None, :] + SB)
               & (cmp_start[:, None] + L > sel_start[None, :])).astype(jnp.float32)
    topk = min(NSA_SEL_TOPK, n_sel)
    ks_blk = ks.reshape(B, n_sel, SB, G, DH).transpose(0, 3, 1, 2, 4)
    vs_blk = vs.reshape(B, n_sel, SB, G, DH).transpose(0, 3, 1, 2, 4)
    kw_pad = jnp.pad(kw, ((0, 0), (W, 0), (0, 0), (0, 0)))
    vw_pad = jnp.pad(vw, ((0, 0), (W, 0), (0, 0), (0, 0)))
    scale = DH ** -0.5
    b_ix = jnp.arange(B)[:, None, None, None]
    g_ix = jnp.arange(G)[None, None, :, None]
    blk = jnp.arange(n_sel)
    in_blk = jnp.arange(SB)
    win_off = jnp.arange(W + QB) - W

    def chunk(c0):
        t = c0 + jnp.arange(QB)
        qc = lax.dynamic_slice_in_dim(q, c0, QB, axis=1).reshape(B, QB, G, HG, DH)
        p_c = _masked_softmax(jnp.einsum('bqghd,bngd->bqghn', qc, kc) * scale,
                              (cmp_end[None, :] <= t[:, None])[None, :, None, None, :])
        o_c = jnp.einsum('bqghn,bngd->bqghd', p_c.astype(vc.dtype), vc)
        imp = jnp.einsum('bqgn,nj->bqgj', p_c.sum(axis=3), overlap)
        cur = c0 // SB
        forced = (blk == 0) | (blk == cur) | (blk == cur - 1)
        imp = jnp.where(blk <= cur, imp + NSA_FORCE_BONUS * forced, -1.0)
        _, idx = lax.top_k(imp, topk)
        k_sel = ks_blk[b_ix, g_ix, idx].reshape(B, QB, G, topk * SB, DH)
        v_sel = vs_blk[b_ix, g_ix, idx].reshape(B, QB, G, topk * SB, DH)
        pos = (idx[..., None] * SB + in_blk).reshape(B, QB, G, 1, topk * SB)
        p_s = _masked_softmax(jnp.einsum('bqghd,bqgmd->bqghm', qc, k_sel) * scale,
                              pos <= t[None, :, None, None, None])
        o_s = jnp.einsum('bqghm,bqgmd->bqghd', p_s.astype(v_sel.dtype), v_sel)
        pos_w = c0 + win_off
        diff = t[:, None] - pos_w[None, :]
        m_w = ((pos_w[None, :] >= 0) & (diff >= 0) & (diff < W))[None, :, None, None, :]
        kwc = lax.dynamic_slice_in_dim(kw_pad, c0, W + QB, axis=1)
        vwc = lax.dynamic_slice_in_dim(vw_pad, c0, W + QB, axis=1)
        p_w = _masked_softmax(jnp.einsum('bqghd,bkgd->bqghk', qc, kwc) * scale, m_w)
        o_w = jnp.einsum('bqghk,bkgd->bqghd', p_w.astype(vwc.dtype), vwc)
        g = lax.dynamic_slice_in_dim(gates, c0, QB, axis=1)[..., None]
        o = g[:, :, 0] * o_c + g[:, :, 1] * o_s + g[:, :, 2] * o_w
        return o.reshape(B, QB, H * DH).astype(h.dtype)

    out = lax.map(chunk, jnp.arange(S // QB) * QB)
    out = jnp.transpose(out, (1, 0, 2, 3)).reshape(B, S, H * DH)
    return out @ w_out


def _stick_breaking_mixer(h, w_in, w_out):
    B, S, _ = h.shape
    H, DH, QB = N_HEADS, HEAD_DIM, SB_QBLOCK
    q, k, v = [a.reshape(B, S, H, DH) for a in jnp.split(h @ w_in, 3, axis=-1)]
    scale = DH ** -0.5
    s_idx = jnp.arange(S)

    def chunk(c0):
        t = c0 + jnp.arange(QB)
        qc = lax.dynamic_slice_in_dim(q, c0, QB, axis=1)
        z = jnp.einsum('bqhd,bshd->bhqs', qc, k).astype(jnp.float32) * scale
        mask = (s_idx[None, :] < t[:, None])[None, None]
        log_beta = jax.nn.log_sigmoid(z)
        log_one_minus = jnp.where(mask, jax.nn.log_sigmoid(-z), 0.0)
        between = lax.cumsum(log_one_minus, axis=3, reverse=True) - log_one_minus
        a = jnp.where(mask, jnp.exp(log_beta + between), 0.0)
        o = jnp.einsum('bhqs,bshd->bqhd', a.astype(v.dtype), v)
        return o.reshape(B, QB, H * DH)

    out = lax.map(chunk, jnp.arange(S // QB) * QB)
    out = jnp.transpose(out, (1, 0, 2, 3)).reshape(B, S, H * DH)
    return out @ w_out


def _conformer_conv_mixer(h, w_in, dw_w, dw_b, ln_g, ln_b, w_out):
    a, b = jnp.split(h @ w_in, 2, axis=-1)
    u = a * jax.nn.sigmoid(b)
    u = lax.conv_general_dilated(u, dw_w[:, None, :], window_strides=(1,),
                                 padding=[(CONV_WIDTH - 1, 0)],
                                 dimension_numbers=('NWC', 'WIO', 'NWC'),
                                 feature_group_count=D_MODEL) + dw_b
    uf = u.astype(jnp.float32)
    mu = jnp.mean(uf, axis=-1, keepdims=True)
    var = jnp.mean(jnp.square(uf - mu), axis=-1, keepdims=True)
    un = (uf - mu) * lax.rsqrt(var + NORM_EPS) * ln_g.astype(jnp.float32) + ln_b.astype(jnp.float32)
    return jax.nn.silu(un).astype(h.dtype) @ w_out


def _moba_mixer(h, w_in, q_gain, k_gain, w_out, cos, sin):
    B, S, _ = h.shape
    H, DH, BS, QB = N_HEADS, HEAD_DIM, MOBA_BLOCK, MOBA_QBLOCK
    q, k, v = [a.reshape(B, S, H, DH) for a in jnp.split(h @ w_in, 3, axis=-1)]
    q = _apply_rope(_rms_norm(q, q_gain), cos, sin)
    k = _apply_rope(_rms_norm(k, k_gain), cos, sin)
    n_blk = -(-S // BS)
    pad = n_blk * BS - S
    kp = jnp.pad(k, ((0, 0), (0, pad), (0, 0), (0, 0)))
    vp = jnp.pad(v, ((0, 0), (0, pad), (0, 0), (0, 0)))
    k_blocks = kp.reshape(B, n_blk, BS, H, DH)
    k_mean = jnp.mean(k_blocks.astype(jnp.float32), axis=2)
    kb_t = k_blocks.transpose(0, 3, 1, 2, 4)
    vb_t = vp.reshape(B, n_blk, BS, H, DH).transpose(0, 3, 1, 2, 4)
    topk = min(MOBA_TOPK, n_blk)
    scale = DH ** -0.5
    b_ix = jnp.arange(B)[:, None, None, None]
    h_ix = jnp.arange(H)[None, None, :, None]
    blk = jnp.arange(n_blk)

    def chunk(c0):
        t = c0 + jnp.arange(QB)
        qc = lax.dynamic_slice_in_dim(q, c0, QB, axis=1)
        cur = c0 // BS
        gate = jnp.einsum('bqhd,bnhd->bqhn', qc.astype(jnp.float32), k_mean)
        gate = jnp.where(blk < cur, gate, NEG_INF)
        _, idx = lax.top_k(gate, topk)
        valid = idx < cur
        k_sel = kb_t[b_ix, h_ix, idx].reshape(B, QB, H, topk * BS, DH)
        v_sel = vb_t[b_ix, h_ix, idx].reshape(B, QB, H, topk * BS, DH)
        s_past = jnp.einsum('bqhd,bqhmd->bqhm', qc, k_sel) * scale
        m_past = jnp.broadcast_to(valid[..., None], (B, QB, H, topk, BS)).reshape(B, QB, H, topk * BS)
        own0 = cur * BS
        k_own = lax.dynamic_slice_in_dim(kp, own0, BS, axis=1)
        v_own = lax.dynamic_slice_in_dim(vp, own0, BS, axis=1)
        s_own = jnp.einsum('bqhd,bkhd->bqhk', qc, k_own) * scale
        m_own = jnp.broadcast_to(((own0 + jnp.arange(BS))[None, :] <= t[:, None])[None, :, None, :],
                                 (B, QB, H, BS))
        p = _masked_softmax(jnp.concatenate([s_past, s_own], axis=-1),
                            jnp.concatenate([m_past, m_own], axis=-1))
        p = p.astype(v.dtype)
        o = (jnp.einsum('bqhm,bqhmd->bqhd', p[..., :topk * BS], v_sel)
             + jnp.einsum('bqhk,bkhd->bqhd', p[..., topk * BS:], v_own))
        return o.reshape(B, QB, H * DH)

    out = lax.map(chunk, jnp.arange(S // QB) * QB)
    out = jnp.transpose(out, (1, 0, 2, 3)).reshape(B, S, H * DH)
    return out @ w_out


def setup_inputs(seed: int = 0) -> dict:
    key = jax.random.key(seed)
    keys = iter(jax.random.split(key, 32))

    def w(shape, fan_in):
        return jax.random.normal(next(keys), shape, jnp.float32) * (fan_in ** -0.5)

    def gain(shape):
        return 1.0 + 0.05 * jax.random.normal(next(keys), shape, jnp.float32)

    def small(shape, s):
        return s * jax.random.normal(next(keys), shape, jnp.float32)

    D, DH, L = D_MODEL, HEAD_DIM, NSA_CMP_BLOCK
    return {
        'x': jax.random.normal(next(keys), (BATCH, SEQ, D), jnp.float32),
        'attn_norm': gain((DEPTH, D)),
        'mlp_norm': gain((DEPTH, D)),
        'mlp_w_up': w((DEPTH, D, D_FF), D),
        'mlp_w_down': w((DEPTH, D_FF, D), D_FF),
        'nsa_w_in': w((N_NSA, D, NSA_IN_DIM), D),
        'nsa_q_norm': gain((N_NSA, DH)),
        'nsa_k_norm': gain((N_NSA, 3, DH)),
        'nsa_cmp_pos': small((N_NSA, 2, L, DH), 0.1),
        'nsa_w_cmp': w((N_NSA, 2, L, DH, DH), L * DH),
        'nsa_w_out': w((N_NSA, N_HEADS * DH, D), N_HEADS * DH),
        'sb_w_in': w((N_SB, D, 3 * N_HEADS * DH), D),
        'sb_w_out': w((N_SB, N_HEADS * DH, D), N_HEADS * DH),
        'conv_w_in': w((N_CONV, D, 2 * D), D),
        'conv_dw_w': w((N_CONV, CONV_WIDTH, D), CONV_WIDTH),
        'conv_dw_b': small((N_CONV, D), 0.02),
        'conv_ln_g': gain((N_CONV, D)),
        'conv_ln_b': small((N_CONV, D), 0.02),
        'conv_w_out': w((N_CONV, D, D), D),
        'moba_w_in': w((N_MOBA, D, 3 * N_HEADS * DH), D),
        'moba_q_norm': gain((N_MOBA, DH)),
        'moba_k_norm': gain((N_MOBA, DH)),
        'moba_w_out': w((N_MOBA, N_HEADS * DH, D), N_HEADS * DH),
    }


def reference(x, attn_norm, mlp_norm, mlp_w_up, mlp_w_down,
              nsa_w_in, nsa_q_norm, nsa_k_norm, nsa_cmp_pos, nsa_w_cmp, nsa_w_out,
              sb_w_in, sb_w_out,
              conv_w_in, conv_dw_w, conv_dw_b, conv_ln_g, conv_ln_b, conv_w_out,
              moba_w_in, moba_q_norm, moba_k_norm, moba_w_out):
    cos, sin = _rope_tables(x.shape[1])
    for i in range(DEPTH):
        m, j = i % N_MIXERS, i // N_MIXERS
        h = _rms_norm(x, attn_norm[i])
        if m == 0:
            y = _nsa_mixer(h, nsa_w_in[j], nsa_q_norm[j], nsa_k_norm[j], nsa_cmp_pos[j],
                           nsa_w_cmp[j], nsa_w_out[j], cos, sin)
        elif m == 1:
            y = _stick_breaking_mixer(h, sb_w_in[j], sb_w_out[j])
        elif m == 2:
            y = _conformer_conv_mixer(h, conv_w_in[j], conv_dw_w[j], conv_dw_b[j],
                                      conv_ln_g[j], conv_ln_b[j], conv_w_out[j])
        else:
            y = _moba_mixer(h, moba_w_in[j], moba_q_norm[j], moba_k_norm[j], moba_w_out[j], cos, sin)
        x = x + y.astype(x.dtype)
        hm = _rms_norm(x, mlp_norm[i])
        x = x + jnp.square(jax.nn.relu(hm @ mlp_w_up[i])) @ mlp_w_down[i]
    return x
```

```python
import numpy as np
import ml_dtypes
from contextlib import ExitStack
import concourse.bass as bass
import concourse.mybir as mybir
from concourse.bass_utils import run_bass_kernel_spmd

F32 = mybir.dt.float32
BF16 = mybir.dt.bfloat16
AF = mybir.ActivationFunctionType
ALU = mybir.AluOpType
bf = ml_dtypes.bfloat16

D = 1024
DC = 8
H = 8
DH = 128
CH = 512
NCH = 4
NT = NCH * CH
DFF = 4096
EPS = 1e-6
NEG = -30000.0
SCALE = DH ** -0.5


class Prog:
    NDMA = 24

    def __init__(self, nc):
        self.nc = nc
        self.ops = []
        self.lastw = {}
        self.readers = {}
        self.bar = set()
        self.last_eng = {}
        self.dma_since = []

    def op(self, eng, fn, reads=(), writes=(), dma=False):
        idx = len(self.ops)
        deps = set(self.bar)
        for k in reads:
            if k in self.lastw:
                deps.add(self.lastw[k])
        for k in writes:
            if k in self.lastw:
                deps.add(self.lastw[k])
            deps.update(self.readers.get(k, ()))
        for k in reads:
            self.readers.setdefault(k, []).append(idx)
        for k in writes:
            self.lastw[k] = idx
            self.readers[k] = []
        self.ops.append(dict(eng=eng, fn=fn, deps=deps, dma=dma))
        self.last_eng[eng] = idx
        if dma:
            self.dma_since.append(idx)
        return idx

    def barrier(self):
        self.bar = set(self.last_eng.values()) | set(self.dma_since)
        self.dma_since = []
        self.lastw = {}
        self.readers = {}

    def emit(self, stack):
        nc = self.nc
        ops = self.ops
        n = len(ops)
        needed = [False] * n
        for i, o in enumerate(ops):
            nd = set()
            for d in o['deps']:
                if ops[d]['eng'] == 'pe' and o['eng'] == 'pe' and not ops[d]['dma']:
                    continue
                nd.add(d)
                needed[d] = True
            o['deps'] = nd
        engs = ['pe', 'act', 'dve', 'pool', 'sp']
        esem = {e: stack.enter_context(nc.semaphore('s_' + e)) for e in engs}
        dsem = [stack.enter_context(nc.semaphore('d_%d' % i)) for i in range(self.NDMA)]
        ecount = {e: 0 for e in engs}
        dcount = [0] * self.NDMA
        rr = {'sp': 0, 'pool': 0}
        NSP = 16
        for i, o in enumerate(ops):
            if o['dma']:
                if o['eng'] == 'pool':
                    s = NSP + rr['pool'] % (self.NDMA - NSP)
                    rr['pool'] += 1
                else:
                    s = rr['sp'] % NSP
                    rr['sp'] += 1
                o['prev'] = (dsem[s], dcount[s]) if dcount[s] > 0 else None
                dcount[s] += 16
                o['sig'] = (dsem[s], dcount[s])
            else:
                if needed[i]:
                    ecount[o['eng']] += 1
                    o['sig'] = (esem[o['eng']], ecount[o['eng']])
                else:
                    o['sig'] = None
        final_d = [(dsem[s], dcount[s]) for s in range(self.NDMA) if dcount[s] > 0]
        block = stack.enter_context(nc.Block())

        def run(ename, e):
            waited = {}

            def w(sem, val):
                k = id(sem)
                if waited.get(k, 0) >= val:
                    return
                waited[k] = val
                e.wait_ge(sem, val)
            for o in ops:
                if o['eng'] != ename:
                    continue
                for d in sorted(o['deps']):
                    sg = ops[d]['sig']
                    w(sg[0], sg[1])
                if o['dma'] and o['prev'] is not None:
                    w(*o['prev'])
                ins = o['fn'](e)
                if o['dma']:
                    ins.then_inc(o['sig'][0], 16)
                elif o['sig'] is not None:
                    ins.then_inc(o['sig'][0], 1)
            if ename == 'sp':
                for sem, val in final_d:
                    w(sem, val)

        @block.tensor
        def _(e):
            run('pe', e)

        @block.scalar
        def _(e):
            run('act', e)

        @block.vector
        def _(e):
            run('dve', e)

        @block.gpsimd
        def _(e):
            run('pool', e)

        @block.sync
        def _(e):
            run('sp', e)


class Cfg:
    def __init__(self, R=4, B=2):
        self.R = R
        self.B = B
        self.S = NCH * R * CH
        self.NC = R * B
        self.NKT = self.S // 128
        self.KM = [4 * R, 8 * R, 12 * R, 16 * R]
        self.GMIN = [0, R, 2 * R, 3 * R]

    def gchunks(self, j):
        R = self.R
        return [j, 2 * R - 1 - j, 2 * R + j, 4 * R - 1 - j]


class K:
    def __init__(self, cfg, st):
        self.cfg = cfg
        self.st = st
        self.nc = bass.Bass("TRN2", target_bir_lowering=False)
        self.P = Prog(self.nc)
        self.dram = {}
        self.in_specs = {}
        self.out_specs = {}
        self.arena = None
        self.off = 0
        self.hiwater = 0
        self.uid = 0

    def start(self, words):
        self.arena = self.st.enter_context(self.nc.sbuf_tensor("arena", [128, words], F32))
        self.words = words
        self.ps = [self.st.enter_context(self.nc.psum_tensor("ps%d" % i, [128, 512], F32)) for i in range(8)]

    def din(self, name, shape, dt):
        t = self.nc.dram_tensor(name, list(shape), dt, kind="ExternalInput").ap()
        self.dram[name] = t
        self.in_specs[name] = (tuple(shape), dt)
        return t

    def dout(self, name, shape, dt):
        t = self.nc.dram_tensor('o_' + name, list(shape), dt, kind="ExternalOutput").ap()
        self.dram[name] = t
        self.out_specs[name] = (tuple(shape), dt)
        return t

    def f32(self, n, parts=128):
        a = self.arena[0:parts, self.off:self.off + n]
        self.off += n
        self.hiwater = max(self.hiwater, self.off)
        assert self.off <= self.words, ("arena overflow", self.off, self.words)
        return a

    def b16(self, n, parts=128):
        w = (n + 1) // 2
        a = self.arena[0:parts, self.off:self.off + w].bitcast(BF16)
        self.off += w
        self.hiwater = max(self.hiwater, self.off)
        assert self.off <= self.words, ("arena overflow", self.off, self.words)
        return a[:, 0:n]

    def mark(self):
        return self.off

    def reset(self, m):
        self.off = m

    def key(self, s):
        self.uid += 1
        return "%s#%d" % (s, self.uid)

    def dma(self, out, in_, reads=(), writes=(), eng='sp'):
        self.P.op(eng, lambda e, out=out, in_=in_: e.dma_start(out=out, in_=in_), reads, writes, dma=True)

    def mm(self, out, lhsT, rhs, start=True, stop=True, reads=(), writes=(), skip=False):
        self.P.op('pe', lambda e, out=out, lhsT=lhsT, rhs=rhs, start=start, stop=stop, skip=skip:
                  e.matmul(out, lhsT, rhs, start=start, stop=stop, skip_group_check=skip), reads, writes)

    def act(self, out, in_, func, reads=(), writes=(), bias=None, scale=None):
        kw = {}
        if bias is not None:
            kw['bias'] = bias
        if scale is not None:
            kw['scale'] = scale
        self.P.op('act', lambda e, out=out, in_=in_, func=func, kw=kw: e.activation(out=out, in_=in_, func=func, **kw),
                  reads, writes)

    def tt(self, out, in0, in1, op, reads=(), writes=(), eng='dve'):
        self.P.op(eng, lambda e, out=out, in0=in0, in1=in1, op=op: e.tensor_tensor(out=out, in0=in0, in1=in1, op=op),
                  reads, writes)

    def ts(self, out, in0, s1, s2, op0, op1=None, reads=(), writes=(), eng='dve'):
        if op1 is None:
            self.P.op(eng, lambda e, out=out, in0=in0, s1=s1, op0=op0:
                      e.tensor_scalar(out=out, in0=in0, scalar1=s1, scalar2=None, op0=op0), reads, writes)
        else:
            self.P.op(eng, lambda e, out=out, in0=in0, s1=s1, s2=s2, op0=op0, op1=op1:
                      e.tensor_scalar(out=out, in0=in0, scalar1=s1, scalar2=s2, op0=op0, op1=op1), reads, writes)

    def stt(self, out, in0, scalar, in1, op0, op1, reads=(), writes=(), eng='dve'):
        self.P.op(eng, lambda e, out=out, in0=in0, scalar=scalar, in1=in1, op0=op0, op1=op1:
                  e.scalar_tensor_tensor(out=out, in0=in0, scalar=scalar, in1=in1, op0=op0, op1=op1), reads, writes)

    def copy(self, out, in_, reads=(), writes=(), eng='dve'):
        self.P.op(eng, lambda e, out=out, in_=in_: e.tensor_copy(out=out, in_=in_), reads, writes)

    def recip(self, out, in_, reads=(), writes=()):
        self.P.op('dve', lambda e, out=out, in_=in_: e.reciprocal(out=out, in_=in_), reads, writes)

    def memset(self, ap, val, writes=(), eng='pool'):
        self.P.op(eng, lambda e, ap=ap, val=val: e.memset(ap, val), (), writes)

    def finish(self):
        self.P.emit(self.st)
        return self.nc


def v3(ap, a):
    return ap.rearrange("p (a b) -> p a b", a=a)


def load_consts(k):
    cin = k.din('cst', [128, 5 * 128], BF16)
    c = k.b16(5 * 128)
    k.dma(c, cin[:, :], writes=['cst'])
    k.ident = c[:, 0:128]
    k.onesD = c[:, 128:256]
    k.onesH = c[:, 256:384]
    k.ones = c[:, 384:512]
    k.Rm = c[:, 512:640]
    k.bank_rr = {}


def host_consts():
    c = np.zeros((128, 5 * 128), np.float32)
    c[:, 0:128] = np.eye(128)
    c[:, 128:256] = 1.0 / 1024
    c[:, 256:384] = 1.0 / 128
    c[:, 384:512] = 1.0
    Rm = np.zeros((128, 128), np.float32)
    for dd in range(64):
        Rm[dd + 64, dd] = -1.0
        Rm[dd, dd + 64] = 1.0
    c[:, 512:640] = Rm
    return c.astype(bf)


def nextbank(k, role, banks):
    i = k.bank_rr.get(role, 0)
    k.bank_rr[role] = i + 1
    return banks[i % len(banks)]


def load_x(k):
    xin = k.din('xT', [D, NT], F32)
    k.xT = v3(k.f32(DC * NT), DC)
    for dc in range(DC):
        k.dma(k.xT[:, dc, :], xin[dc * 128:(dc + 1) * 128, :], writes=[('x', dc, c) for c in range(NCH)])


def store_x(k):
    xo = k.dout('xT_out', [D, NT], F32)
    for dc in range(DC):
        k.dma(xo[dc * 128:(dc + 1) * 128, :], k.xT[:, dc, :], reads=[('x', dc, c) for c in range(NCH)],
              writes=[('xo', dc)])


def load_vec(k, name, n):
    t = k.f32(n)
    k.dma(t, k.din(name, [128, n], F32)[:, :], writes=[name])
    return t


def rmsnorm(k, gname, hn):
    g = load_vec(k, gname, DC)
    sq = [v3(k.b16(DC * CH), DC) for _ in range(2)]
    rs = [k.f32(CH) for _ in range(2)]
    for c in range(NCH):
        s = c % 2
        cs = slice(c * CH, (c + 1) * CH)
        k.act(sq[s], k.xT[:, :, cs], AF.Square, reads=[('x', dc, c) for dc in range(DC)], writes=[('sq', s)])
        pb = 6 + s
        for dc in range(DC):
            k.mm(k.ps[pb][:, :], k.onesD, sq[s][:, dc, :], start=(dc == 0), stop=(dc == DC - 1),
                 reads=[('sq', s), 'cst'], writes=[('ps', pb)])
        k.act(rs[s], k.ps[pb][:, :], AF.Sqrt, bias=EPS, reads=[('ps', pb)], writes=[('rs', s)])
        k.recip(rs[s], rs[s], reads=[('rs', s)], writes=[('rs', s)])
        for dc in range(DC):
            k.stt(hn[:, dc, cs], k.xT[:, dc, cs], g[:, dc:dc + 1], rs[s], ALU.mult, ALU.mult,
                  reads=[('x', dc, c), ('rs', s), gname], writes=[('hn', dc, c)])


def proj_fm(k, wname, blocks, hn, consumer, banks=(0, 1, 2, 3)):
    wv = k.dram[wname].rearrange("(dc p) m -> p dc m", p=128)
    slots = [v3(k.b16(DC * 512), DC) for _ in range(2)]
    for bi, (c0, ncol) in enumerate(blocks):
        s = bi % 2
        k.dma(slots[s][:, :, 0:ncol], wv[:, :, c0:c0 + ncol], writes=[('wblk', wname, s)], eng='pool')
        for c in range(NCH):
            for mi in range((ncol + 127) // 128):
                mw = min(128, ncol - mi * 128)
                pb = nextbank(k, 'proj', banks)
                for dc in range(DC):
                    k.mm(k.ps[pb][0:mw, :], slots[s][:, dc, mi * 128:mi * 128 + mw], hn[:, dc, c * CH:(c + 1) * CH],
                         start=(dc == 0), stop=(dc == DC - 1),
                         reads=[('wblk', wname, s), ('hn', dc, c)], writes=[('ps', pb)])
                consumer(c0 + mi * 128, c, pb)


def proj_tm(k, wname, c0, ncol, hn, consumer, banks=(0, 1, 2, 3)):
    wv = k.dram[wname].rearrange("(dc p) m -> p dc m", p=128)
    wt = v3(k.b16(DC * ncol), DC)
    kk = k.key('wtm')
    k.dma(wt, wv[:, :, c0:c0 + ncol], writes=[kk], eng='pool')
    for tt in range(NT // 128):
        c = tt // 4
        pb = nextbank(k, 'proj', banks)
        for dc in range(DC):
            k.mm(k.ps[pb][:, 0:ncol], hn[:, dc, tt * 128:(tt + 1) * 128], wt[:, dc, :],
                 start=(dc == 0), stop=(dc == DC - 1), reads=[kk, ('hn', dc, c)], writes=[('ps', pb)])
        consumer(tt, pb)


def out_proj_residual(k, wname, ob, okeyf):
    wv = k.dram[wname].rearrange("(dc p) m -> p dc m", p=128)
    wt = v3(k.b16(DC * D), DC)
    kk = k.key('wout')
    k.dma(wt, wv, writes=[kk], eng='pool')
    for c in range(NCH):
        cs = slice(c * CH, (c + 1) * CH)
        for dco in range(DC):
            pb = nextbank(k, 'op', (4, 5))
            for h in range(DC):
                k.mm(k.ps[pb][:, :], wt[:, h, dco * 128:(dco + 1) * 128], ob[:, h, cs],
                     start=(h == 0), stop=(h == DC - 1), reads=[kk] + okeyf(h, c), writes=[('ps', pb)])
            k.tt(k.xT[:, dco, cs], k.xT[:, dco, cs], k.ps[pb][:, :], ALU.add,
                 reads=[('ps', pb), ('x', dco, c)], writes=[('x', dco, c)])


def mlp(k, li):
    m = k.mark()
    hn = v3(k.b16(DC * NT), DC)
    rmsnorm(k, 'g_mlp%d' % li, hn)
    wu = k.din('w_up%d' % li, [D, DFF], F32).rearrange("(dc p) m -> p dc m", p=128)
    wd = k.din('w_dn%d' % li, [DFF, D], F32).rearrange("(fc p) m -> p fc m", p=128)
    ups = [v3(k.b16(DC * 512), DC) for _ in range(2)]
    dns = [v3(k.b16(4 * D), 4) for _ in range(2)]
    rl = [k.b16(CH) for _ in range(2)]
    h1 = [v3(k.b16(4 * CH), 4) for _ in range(2)]
    it = 0
    for fb in range(DFF // 512):
        s = fb % 2
        k.dma(ups[s], wu[:, :, fb * 512:(fb + 1) * 512], writes=[('wup', s)], eng='pool')
        k.dma(dns[s], wd[:, fb * 4:(fb + 1) * 4, :], writes=[('wdn', s)], eng='pool')
        for c in range(NCH):
            cs = slice(c * CH, (c + 1) * CH)
            hs = it % 2
            it += 1
            for fc in range(4):
                pb = nextbank(k, 'mlpu', (0, 1))
                for dc in range(DC):
                    k.mm(k.ps[pb][:, :], ups[s][:, dc, fc * 128:(fc + 1) * 128], hn[:, dc, cs],
                         start=(dc == 0), stop=(dc == DC - 1), reads=[('wup', s), ('hn', dc, c)], writes=[('ps', pb)])
                r = nextbank(k, 'rl', (0, 1))
                k.act(rl[r], k.ps[pb][:, :], AF.Relu, reads=[('ps', pb)], writes=[('rl', r)])
                k.tt(h1[hs][:, fc, :], rl[r], rl[r], ALU.mult, reads=[('rl', r)], writes=[('h1', hs, fc)], eng='pool')
            for dco in range(DC):
                pb = nextbank(k, 'mlpd', (2, 3))
                for fc in range(4):
                    k.mm(k.ps[pb][:, :], dns[s][:, fc, dco * 128:(dco + 1) * 128], h1[hs][:, fc, :],
                         start=(fc == 0), stop=(fc == 3), reads=[('wdn', s), ('h1', hs, fc)], writes=[('ps', pb)])
                k.tt(k.xT[:, dco, cs], k.xT[:, dco, cs], k.ps[pb][:, :], ALU.add,
                     reads=[('ps', pb), ('x', dco, c)], writes=[('x', dco, c)])
    k.P.barrier()
    k.reset(m)


def normrope_setup(k):
    k.cosT = k.f32(NT)
    k.sinT = k.f32(NT)
    k.dma(k.cosT, k.din('cosT', [128, NT], F32)[:, :], writes=['cos'])
    k.dma(k.sinT, k.din('sinT', [128, NT], F32)[:, :], writes=['sin'])
    k.nr_xg = [k.b16(CH) for _ in range(2)]
    k.nr_sq = [k.b16(CH) for _ in range(2)]
    k.nr_rs = [k.f32(CH) for _ in range(2)]
    k.nr_t1 = [k.f32(CH) for _ in range(2)]
    k.nr_t2 = [k.f32(CH) for _ in range(2)]


def normrope(k, pb, c, gain, gkey, out, okeys):
    s = nextbank(k, 'nr', (0, 1))
    cs = slice(c * CH, (c + 1) * CH)
    xg, sq, rs, t1, t2 = k.nr_xg[s], k.nr_sq[s], k.nr_rs[s], k.nr_t1[s], k.nr_t2[s]
    k.act(xg, k.ps[pb][:, :], AF.Identity, scale=gain, reads=[('ps', pb), gkey], writes=[('nrxg', s)])
    k.act(sq, k.ps[pb][:, :], AF.Square, reads=[('ps', pb)], writes=[('nrsq', s)])
    b1 = nextbank(k, 'nrss', (4, 5))
    b2 = nextbank(k, 'nrrot', (6, 7))
    k.mm(k.ps[b1][:, :], k.onesH, sq, reads=[('nrsq', s), 'cst'], writes=[('ps', b1)])
    k.mm(k.ps[b2][:, :], k.Rm, xg, reads=[('nrxg', s), 'cst'], writes=[('ps', b2)])
    k.act(rs, k.ps[b1][:, :], AF.Sqrt, bias=EPS, reads=[('ps', b1)], writes=[('nrrs', s)])
    k.recip(rs, rs, reads=[('nrrs', s)], writes=[('nrrs', s)])
    k.tt(t1, xg, k.cosT[:, cs], ALU.mult, reads=[('nrxg', s), 'cos'], writes=[('nrt1', s)])
    k.tt(t2, k.ps[b2][:, :], k.sinT[:, cs], ALU.mult, reads=[('ps', b2), 'sin'], writes=[('nrt2', s)])
    k.tt(t1, t1, t2, ALU.add, reads=[('nrt1', s), ('nrt2', s)], writes=[('nrt1', s)])
    k.tt(out, t1, rs, ALU.mult, reads=[('nrt1', s), ('nrrs', s)], writes=okeys)


def qkv_pre(k, li, moba):
    m = k.mark()
    hn = v3(k.b16(DC * NT), DC)
    rmsnorm(k, 'g_an%d' % li, hn)
    k.din('w_in%d' % li, [D, 3 * D], F32)
    qo = k.dout('qT', [D, NT], BF16)
    ko = k.dout('kT', [D, NT], BF16)
    vo = k.dout('v', [NT, D], BF16)
    stage = [k.b16(CH) for _ in range(3)]
    if moba:
        normrope_setup(k)
        gq = load_vec(k, 'moba_qn', 1)
        gk = load_vec(k, 'moba_kn', 1)
        kmo = k.dout('kmean', [D, 2 * NCH], F32)
        kms = k.f32(H * 2 * NCH)
        kmf = [k.f32(CH) for _ in range(2)]

    def cons(col0, c, pb):
        s = nextbank(k, 'stg', (0, 1, 2))
        cs = slice(c * CH, (c + 1) * CH)
        isq = col0 < D
        m_ = (col0 % D) // 128
        dst = (qo if isq else ko)[m_ * 128:(m_ + 1) * 128, cs]
        if not moba:
            k.act(stage[s], k.ps[pb][:, :], AF.Copy, scale=(SCALE if isq else 1.0), reads=[('ps', pb)], writes=[('stg', s)])
        else:
            normrope(k, pb, c, gq[:, 0:1] if isq else gk[:, 0:1], 'moba_qn' if isq else 'moba_kn', stage[s], [('stg', s)])
            if not isq:
                f = nextbank(k, 'kmf', (0, 1))
                k.copy(kmf[f], stage[s], reads=[('stg', s)], writes=[('kmf', f)])
                for b2 in range(2):
                    col = m_ * 2 * NCH + c * 2 + b2
                    k.P.op('dve', lambda e, o=kms[:, col:col + 1], i=kmf[f][:, b2 * 256:(b2 + 1) * 256]:
                           e.reduce_sum(out=o, in_=i, axis=mybir.AxisListType.X), [('kmf', f)], [('kms', col)])
        k.dma(dst, stage[s], reads=[('stg', s)], writes=[k.key('o')])

    proj_fm(k, 'w_in%d' % li, [(0, 512), (512, 512), (1024, 512), (1536, 512)], hn, cons)
    vst = [k.b16(512) for _ in range(2)]
    for half in range(2):
        def consv(tt, pb, half=half):
            s = nextbank(k, 'vst', (0, 1))
            k.copy(vst[s], k.ps[pb][:, :], reads=[('ps', pb)], writes=[('vst', s)])
            k.dma(vo[tt * 128:(tt + 1) * 128, half * 512:(half + 1) * 512], vst[s], reads=[('vst', s)], writes=[k.key('o')])
        proj_tm(k, 'w_in%d' % li, 2 * D + half * 512, 512, hn, consv)
    if moba:
        k.ts(kms, kms, 1.0 / 256, None, ALU.mult, reads=[('kms', i) for i in range(H * 2 * NCH)], writes=['kmsall'])
        for h in range(H):
            k.dma(kmo[h * 128:(h + 1) * 128, :], kms[:, h * 2 * NCH:(h + 1) * 2 * NCH], reads=['kmsall'], writes=[k.key('o')])
    k.P.barrier()
    k.reset(m)


def conv_pre(k):
    m = k.mark()
    hn = v3(k.b16(DC * NT), DC)
    rmsnorm(k, 'g_an2', hn)
    wv = k.din('conv_w_in', [D, 2 * D], F32).rearrange("(dc p) m -> p dc m", p=128)
    uo = k.dout('uT', [D, NT], BF16)
    wa = [v3(k.b16(DC * 512), DC) for _ in range(2)]
    wb = [v3(k.b16(DC * 512), DC) for _ in range(2)]
    sg = [k.f32(CH) for _ in range(2)]
    us = [k.b16(CH) for _ in range(3)]
    for pi in range(2):
        k.dma(wa[pi], wv[:, :, pi * 512:(pi + 1) * 512], writes=[('wa', pi)], eng='pool')
        k.dma(wb[pi], wv[:, :, D + pi * 512:D + (pi + 1) * 512], writes=[('wb', pi)], eng='pool')
        for c in range(NCH):
            cs = slice(c * CH, (c + 1) * CH)
            for mi in range(4):
                dc = pi * 4 + mi
                pa = nextbank(k, 'cva', (0, 1))
                pbb = nextbank(k, 'cvb', (2, 3))
                for d2 in range(DC):
                    k.mm(k.ps[pa][:, :], wa[pi][:, d2, mi * 128:(mi + 1) * 128], hn[:, d2, cs], start=(d2 == 0), stop=(d2 == DC - 1),
                         reads=[('wa', pi), ('hn', d2, c)], writes=[('ps', pa)])
                for d2 in range(DC):
                    k.mm(k.ps[pbb][:, :], wb[pi][:, d2, mi * 128:(mi + 1) * 128], hn[:, d2, cs], start=(d2 == 0), stop=(d2 == DC - 1),
                         reads=[('wb', pi), ('hn', d2, c)], writes=[('ps', pbb)])
                s = nextbank(k, 'sg', (0, 1))
                u = nextbank(k, 'us', (0, 1, 2))
                k.act(sg[s], k.ps[pbb][:, :], AF.Sigmoid, reads=[('ps', pbb)], writes=[('sg', s)])
                k.tt(us[u], k.ps[pa][:, :], sg[s], ALU.mult, reads=[('ps', pa), ('sg', s)], writes=[('us', u)])
                k.dma(uo[dc * 128:(dc + 1) * 128, cs], us[u], reads=[('us', u)], writes=[k.key('o')])
    k.P.barrier()
    k.reset(m)


def conv_core(k):
    m = k.mark()
    HW = 32 + CH
    uin = k.din('uE', [D, NCH, HW], BF16)
    ue = v3(k.b16(DC * NCH * HW), DC * NCH)
    for dc in range(DC):
        k.dma(ue[:, dc * NCH:(dc + 1) * NCH, :], uin[dc * 128:(dc + 1) * 128, :, :], writes=[('ue', dc)])
    wdw = v3(load_vec(k, 'conv_dw', DC * 31), DC)
    bdw = load_vec(k, 'conv_dwb', DC)
    lg = load_vec(k, 'conv_lng', DC)
    lb = load_vec(k, 'conv_lnb', DC)
    ucb = v3(k.b16(DC * NT), DC)
    Dg = [v3(k.b16(31 * 128), 31) for _ in range(2)]
    for dc in range(DC):
        s = dc % 2
        for kk in range(31):
            k.ts(Dg[s][:, kk, :], k.ident, wdw[:, dc, kk:kk + 1], None, ALU.mult, reads=['cst', 'conv_dw'], writes=[('dg', s, kk)])
        for c in range(NCH):
            pb = nextbank(k, 'cv', (0, 1, 2))
            for kk in range(31):
                k.mm(k.ps[pb][:, :], Dg[s][:, kk, :], ue[:, dc * NCH + c, 2 + kk:2 + kk + CH], start=(kk == 0), stop=(kk == 30),
                     reads=[('dg', s, kk), ('ue', dc)], writes=[('ps', pb)])
            k.act(ucb[:, dc, c * CH:(c + 1) * CH], k.ps[pb][:, :], AF.Identity, bias=bdw[:, dc:dc + 1],
                  reads=[('ps', pb), 'conv_dwb'], writes=[('ucb', dc, c)])
    sq = [v3(k.b16(DC * CH), DC) for _ in range(2)]
    mu = [k.f32(CH) for _ in range(2)]
    va = [k.f32(CH) for _ in range(2)]
    nm = [k.f32(CH) for _ in range(2)]
    tn = [k.f32(CH) for _ in range(2)]
    for c in range(NCH):
        s = c % 2
        cs = slice(c * CH, (c + 1) * CH)
        k.tt(sq[s], ucb[:, :, cs], ucb[:, :, cs], ALU.mult, reads=[('ucb', dc, c) for dc in range(DC)], writes=[('csq', s)], eng='pool')
        for dc in range(DC):
            k.mm(k.ps[6][:, :], k.onesD, ucb[:, dc, cs], start=(dc == 0), stop=(dc == DC - 1), reads=[('ucb', dc, c), 'cst'], writes=[('ps', 6)])
        for dc in range(DC):
            k.mm(k.ps[7][:, :], k.onesD, sq[s][:, dc, :], start=(dc == 0), stop=(dc == DC - 1), reads=[('csq', s), 'cst'], writes=[('ps', 7)])
        k.copy(mu[s], k.ps[6][:, :], reads=[('ps', 6)], writes=[('mu', s)])
        k.tt(va[s], mu[s], mu[s], ALU.mult, reads=[('mu', s)], writes=[('va', s)])
        k.tt(va[s], k.ps[7][:, :], va[s], ALU.subtract, reads=[('ps', 7), ('va', s)], writes=[('va', s)])
        k.act(va[s], va[s], AF.Sqrt, bias=EPS, reads=[('va', s)], writes=[('va', s)])
        k.recip(va[s], va[s], reads=[('va', s)], writes=[('va', s)])
        k.stt(nm[s], mu[s], -1.0, va[s], ALU.mult, ALU.mult, reads=[('mu', s), ('va', s)], writes=[('nm', s)])
        for dc in range(DC):
            t = nextbank(k, 'tn', (0, 1))
            k.tt(tn[t], ucb[:, dc, cs], va[s], ALU.mult, reads=[('ucb', dc, c), ('va', s)], writes=[('tn', t)])
            k.tt(tn[t], tn[t], nm[s], ALU.add, reads=[('tn', t), ('nm', s)], writes=[('tn', t)])
            k.act(ucb[:, dc, cs], tn[t], AF.Silu, scale=lg[:, dc:dc + 1], bias=lb[:, dc:dc + 1],
                  reads=[('tn', t), 'conv_lng', 'conv_lnb'], writes=[('ucb', dc, c)])
    return m, ucb


def load_q(k):
    qin = k.din('qT', [D, NT], BF16)
    q = v3(k.b16(H * NT), H)
    for h in range(H):
        k.dma(q[:, h, :], qin[h * 128:(h + 1) * 128, :], writes=[('q', h, c) for c in range(NCH)])
    return q


def kv_loader(k, ktname='kT_full', vname='v_full'):
    cfg = k.cfg
    kin = k.din(ktname, [D, cfg.S], BF16)
    vin = k.din(vname, [H, 128, cfg.NKT, 128], BF16)
    kb = [k.b16(cfg.S) for _ in range(2)]
    vb = [v3(k.b16(cfg.NKT * 128), cfg.NKT) for _ in range(2)]
    state = {'i': 0}

    def load(h, nkt):
        s = state['i'] % 2
        state['i'] += 1
        k.dma(kb[s][:, 0:nkt * 128], kin[h * 128:(h + 1) * 128, 0:nkt * 128], writes=[('kb', s)])
        k.dma(vb[s][:, 0:nkt, :], vin[h, :, 0:nkt, :], writes=[('vb', s)])
        return s
    return kb, vb, load


def sb_core(k):
    cfg = k.cfg
    R = cfg.R
    m = k.mark()
    q = load_q(k)
    mq = k.mark()
    kb, vb, kvload = kv_loader(k)
    NU = 4 * R
    mkin = k.din('sbmask', [128, NCH, NU, CH], BF16)
    mk = v3(k.b16(NU * CH), NU)
    cc = k.din('sbc', [128, 256], BF16)
    cst = k.b16(256)
    k.dma(cst, cc[:, :], writes=['sbc'])
    negtri = cst[:, 0:128]
    negone1 = cst[0:1, 128:256]
    onescol = k.ones[:, 0:1]
    ef = [k.f32(CH) for _ in range(2)]
    sp = [k.b16(CH) for _ in range(3)]
    at = [k.b16(CH) for _ in range(3)]
    cr = [k.b16(CH, parts=1) for _ in range(2)]
    it = 0
    for lc in range(NCH):
        KM = cfg.KM[lc]
        k.dma(mk, mkin[:, lc, :, :], writes=['mk'])
        cs = slice(lc * CH, (lc + 1) * CH)
        for h in range(H):
            s = kvload(h, KM)
            ob = 4 + (it % 2)
            it += 1
            tiles = list(range(KM - 1, -1, -1))
            n = len(tiles)
            zb = [None] * n
            sps = [None] * n
            ats = [None] * n

            def stA(i):
                kt = tiles[i]
                zb[i] = nextbank(k, 'z', (0, 1, 2))
                k.mm(k.ps[zb[i]][:, :], kb[s][:, kt * 128:(kt + 1) * 128], q[:, h, cs],
                     reads=[('kb', s), ('q', h, lc)], writes=[('ps', zb[i])])
                e = nextbank(k, 'ef', (0, 1))
                sps[i] = nextbank(k, 'sp', (0, 1, 2))
                k.act(ef[e], k.ps[zb[i]][:, :], AF.Exp, reads=[('ps', zb[i])], writes=[('ef', e)])
                k.act(sp[sps[i]], ef[e], AF.Ln, bias=1.0, reads=[('ef', e)], writes=[('sp', sps[i])])
                u = kt - 4 * cfg.GMIN[lc]
                if u >= 0:
                    k.tt(sp[sps[i]], sp[sps[i]], mk[:, u, :], ALU.mult, reads=[('sp', sps[i]), 'mk'], writes=[('sp', sps[i])])

            def stB(i):
                z = k.ps[zb[i]]
                k.mm(z[:, :], negtri, sp[sps[i]], start=False, stop=True, skip=True,
                     reads=[('sp', sps[i]), 'sbc', ('ps', zb[i])], writes=[('ps', zb[i])])
                if i > 0:
                    c_ = (i - 1) % 2
                    k.mm(z[:, :], negone1, cr[c_], start=False, stop=True, skip=True,
                         reads=[('cr', c_), 'sbc', ('ps', zb[i])], writes=[('ps', zb[i])])
                k.mm(k.ps[3][0:1, :], onescol, sp[sps[i]], start=(i == 0), stop=True, skip=True,
                     reads=[('sp', sps[i]), 'cst'], writes=[('ps', 3)])
                if i < n - 1:
                    k.copy(cr[i % 2], k.ps[3][0:1, :], reads=[('ps', 3)], writes=[('cr', i % 2)])

            def stC(i):
                kt = tiles[i]
                ats[i] = nextbank(k, 'at', (0, 1, 2))
                a = at[ats[i]]
                k.act(a, k.ps[zb[i]][:, :], AF.Exp, reads=[('ps', zb[i])], writes=[('at', ats[i])])
                u = kt - 4 * cfg.GMIN[lc]
                if u >= 0:
                    k.tt(a, a, mk[:, u, :], ALU.mult, reads=[('at', ats[i]), 'mk'], writes=[('at', ats[i])])
                k.mm(k.ps[ob][:, :], vb[s][:, kt, :], a, start=(i == 0), stop=(i == n - 1),
                     reads=[('vb', s), ('at', ats[i])], writes=[('ps', ob)])

            for i in range(n + 2):
                if i < n:
                    stA(i)
                if 0 <= i - 1 < n:
                    stB(i - 1)
                if 0 <= i - 2 < n:
                    stC(i - 2)
            k.act(q[:, h, cs], k.ps[ob][:, :], AF.Copy, reads=[('ps', ob)], writes=[('q', h, lc)])
    return m, mq, q


def moba_core(k):
    cfg = k.cfg
    NB = cfg.S // 256
    m = k.mark()
    q = load_q(k)
    mq = k.mark()
    kb, vb, kvload = kv_loader(k)
    kown = k.din('kT_own', [D, NT], BF16)
    vown = k.din('v_own', [NT, D], BF16)
    kmin = k.din('kmean_full', [D, NB], F32)
    kmf = v3(k.f32(H * NB), H)
    kmb = v3(k.b16(H * NB), H)
    for h in range(H):
        k.dma(kmf[:, h, :], kmin[h * 128:(h + 1) * 128, :], writes=[('kmf', h)])
    k.copy(kmb, kmf, reads=[('kmf', h) for h in range(H)], writes=['kmb'])
    mvin = k.din('mvalid', [128, NCH, 4 * NB], F32)
    mv = v3(k.f32(NCH * 4 * NB), NCH)
    k.dma(mv, mvin[:, :, :], writes=['mv'])
    esin = k.din('esel', [NB, NB * 128], BF16)
    esel = k.b16(NB * 128, parts=NB)
    k.dma(esel, esin[:, :], writes=['esel'])
    omin = k.din('ownmask', [128, 4, CH], BF16)
    om = v3(k.b16(4 * CH), 4)
    k.dma(om, omin[:, :, :], writes=['om'])
    identf = k.f32(128)
    k.copy(identf, k.ident, reads=['cst'], writes=['identf'])
    gm = k.f32(4 * NB)
    m8 = k.f32(32)
    t1 = k.f32(4 * NB)
    mb = k.f32(4 * NB)
    mT = [k.b16(CH, parts=NB) for _ in range(2)]
    pt = [k.b16(CH) for _ in range(3)]
    kl = [k.b16(CH) for _ in range(2)]
    vl = [v3(k.b16(4 * 128), 4) for _ in range(2)]
    rd = k.f32(CH)
    it = 0
    for lc in range(NCH):
        KM = cfg.KM[lc]
        cs = slice(lc * CH, (lc + 1) * CH)
        for h in range(H):
            s = kvload(h, KM)
            l = it % 2
            k.dma(kl[l], kown[h * 128:(h + 1) * 128, cs], writes=[('kl', l)])
            k.dma(vl[l], vown[cs, h * 128:(h + 1) * 128].rearrange("(a p) d -> p a d", p=128), writes=[('vl', l)])
            ob = 4 + (it % 2)
            db = 6 + (it % 2)
            it += 1
            for u in range(4):
                k.mm(k.ps[3][:, u * NB:(u + 1) * NB], q[:, h, lc * CH + u * 128:lc * CH + (u + 1) * 128], kmb[:, h, :],
                     start=(u == 0), stop=True, skip=(u > 0), reads=[('q', h, lc), 'kmb'], writes=[('ps', 3)])
            k.tt(gm, k.ps[3][:, 0:4 * NB], mv[:, lc, :], ALU.add, reads=[('ps', 3), 'mv'], writes=['gm'])
            for u in range(4):
                k.P.op('dve', lambda e, o=m8[:, u * 8:(u + 1) * 8], i=gm[:, u * NB:(u + 1) * NB]: e.max(out=o, in_=i), ['gm'], [('m8', u)])
            thr = v3(m8, 4)[:, :, 2:3]
            k.ts(thr, thr, -1e29, None, ALU.max, reads=[('m8', u) for u in range(4)], writes=['thr'])
            for u in range(4):
                k.ts(t1[:, u * NB:(u + 1) * NB], gm[:, u * NB:(u + 1) * NB], m8[:, u * 8 + 2:u * 8 + 3], None, ALU.is_ge,
                     reads=['gm', 'thr'], writes=[('t1', u)])
            k.ts(mb, t1, -NEG, NEG, ALU.mult, ALU.add, reads=[('t1', u) for u in range(4)], writes=['mb'])
            for u in range(4):
                k.P.op('pe', lambda e, o=k.ps[2][0:NB, u * 128:(u + 1) * 128], i=mb[:, u * NB:(u + 1) * NB]:
                       e.transpose(o, i, identf), ['mb', 'identf'], [('ps', 2)])
            mt = mT[it % 2]
            mk_ = ('mT', it % 2)
            k.copy(mt, k.ps[2][0:NB, :], reads=[('ps', 2)], writes=[mk_])
            for kt in range(KM):
                zb = nextbank(k, 'z', (0, 1))
                nb_ = kt // 2
                k.mm(k.ps[zb][:, :], kb[s][:, kt * 128:(kt + 1) * 128], q[:, h, cs], start=True, stop=False,
                     reads=[('kb', s), ('q', h, lc)], writes=[('ps', zb)])
                k.mm(k.ps[zb][:, :], esel[:, nb_ * 128:(nb_ + 1) * 128], mt, start=False, stop=True,
                     reads=['esel', mk_], writes=[('ps', zb)])
                p = nextbank(k, 'pt', (0, 1, 2))
                k.act(pt[p], k.ps[zb][:, :], AF.Exp, scale=SCALE, reads=[('ps', zb)], writes=[('pt', p)])
                k.mm(k.ps[ob][:, :], vb[s][:, kt, :], pt[p], start=(kt == 0), stop=False, reads=[('vb', s), ('pt', p)], writes=[('ps', ob)])
                k.mm(k.ps[db][:, :], k.ones, pt[p], start=(kt == 0), stop=False, reads=['cst', ('pt', p)], writes=[('ps', db)])
            for i2 in range(4):
                zb = nextbank(k, 'z', (0, 1))
                k.mm(k.ps[zb][:, :], kl[l][:, i2 * 128:(i2 + 1) * 128], q[:, h, cs], start=True, stop=False,
                     reads=[('kl', l), ('q', h, lc)], writes=[('ps', zb)])
                k.mm(k.ps[zb][:, :], k.ident, om[:, i2, :], start=False, stop=True, reads=['cst', 'om'], writes=[('ps', zb)])
                p = nextbank(k, 'pt', (0, 1, 2))
                k.act(pt[p], k.ps[zb][:, :], AF.Exp, scale=SCALE, reads=[('ps', zb)], writes=[('pt', p)])
                k.mm(k.ps[ob][:, :], vl[l][:, i2, :], pt[p], start=False, stop=(i2 == 3), reads=[('vl', l), ('pt', p)], writes=[('ps', ob)])
                k.mm(k.ps[db][:, :], k.ones, pt[p], start=False, stop=(i2 == 3), reads=['cst', ('pt', p)], writes=[('ps', db)])
            k.recip(rd, k.ps[db][:, :], reads=[('ps', db)], writes=['rd'])
            k.tt(q[:, h, cs], k.ps[ob][:, :], rd, ALU.mult, reads=[('ps', ob), 'rd'], writes=[('q', h, lc)])
    return m, mq, q


NSA_IN = 2584


def nsa_pre(k):
    m = k.mark()
    hn = v3(k.b16(DC * NT), DC)
    rmsnorm(k, 'g_an0', hn)
    k.din('w_in0', [D, NSA_IN], F32)
    qo = k.dout('qT', [D, NT], BF16)
    fm_out = {1024: k.dout('kcT', [256, NT], BF16), 1280: k.dout('vcT', [256, NT], BF16),
              1536: k.dout('ksT', [256, NT], BF16), 2048: k.dout('kwT', [256, NT], BF16)}
    vso = k.dout('vs', [NT, 256], BF16)
    vwo = k.dout('vw', [NT, 256], BF16)
    go = k.dout('gT', [24, NT], F32)
    normrope_setup(k)
    gq = load_vec(k, 'nsa_qn', 1)
    gk = load_vec(k, 'nsa_kn', 3)
    stage = [k.b16(CH) for _ in range(3)]
    gst = [k.f32(CH, parts=24) for _ in range(2)]

    def cons(col0, c, pb):
        cs = slice(c * CH, (c + 1) * CH)
        if col0 == 2560:
            s = nextbank(k, 'gst', (0, 1))
            k.act(gst[s], k.ps[pb][0:24, :], AF.Sigmoid, reads=[('ps', pb)], writes=[('gst', s)])
            k.dma(go[:, cs], gst[s], reads=[('gst', s)], writes=[k.key('o')])
            return
        s = nextbank(k, 'stg', (0, 1, 2))
        if col0 < 1024:
            normrope(k, pb, c, gq[:, 0:1], 'nsa_qn', stage[s], [('stg', s)])
            dst = qo[col0:col0 + 128, cs]
        else:
            base = max(b for b in fm_out if b <= col0)
            dst = fm_out[base][col0 - base:col0 - base + 128, cs]
            if base == 1280:
                k.act(stage[s], k.ps[pb][:, :], AF.Copy, reads=[('ps', pb)], writes=[('stg', s)])
            else:
                gi = {1024: 0, 1536: 1, 2048: 2}[base]
                normrope(k, pb, c, gk[:, gi:gi + 1], 'nsa_kn', stage[s], [('stg', s)])
        k.dma(dst, stage[s], reads=[('stg', s)], writes=[k.key('o')])

    proj_fm(k, 'w_in0', [(0, 512), (512, 512), (1024, 512), (1536, 256), (2048, 256), (2560, 24)], hn, cons)
    vst = [k.b16(256) for _ in range(2)]
    for (c0, dst) in ((1792, vso), (2304, vwo)):
        def consv(tt, pb, dst=dst):
            s = nextbank(k, 'vst', (0, 1))
            k.copy(vst[s], k.ps[pb][:, 0:256], reads=[('ps', pb)], writes=[('vst', s)])
            k.dma(dst[tt * 128:(tt + 1) * 128, :], vst[s], reads=[('vst', s)], writes=[k.key('o')])
        proj_tm(k, 'w_in0', c0, 256, hn, consv)
    k.P.barrier()
    k.reset(m)


def nsa_core(k, ob):
    cfg = k.cfg
    S = cfg.S
    NSEL = S // 64
    NCMP = S // 16 - 1
    NCC = (NCMP + 127) // 128
    NQT = NT // 128
    m = k.mark()
    qin = k.din('qT', [D, NT], BF16)
    qsb = k.b16(NQT * H * 128).rearrange("p (a b c) -> p a b c", a=NQT, b=H)
    for h in range(H):
        k.dma(qsb[:, :, h, :], qin[h * 128:(h + 1) * 128, :].rearrange("p (a t) -> p a t", t=128), writes=[('q', h)])
    gin = k.din('gT', [24, NT], F32)
    gb = k.b16(NT, parts=24)
    k.dma(gb, gin[:, :], writes=['gb'], eng='pool')
    selg = k.b16(24 * 128, parts=24)
    k.dma(selg, k.din('selg', [24, 24 * 128], BF16)[:, :], writes=['selg'])
    G = k.b16(S, parts=NSEL)
    k.dma(G, k.din('G', [NSEL, S], BF16)[:, :], writes=['G'])
    Ov = v3(k.b16(NCC * NSEL), NCC)
    k.dma(Ov, k.din('Ov', [128, NCC, NSEL], BF16)[:, :, :], writes=['Ov'])
    tri = k.b16(256)
    k.dma(tri, k.din('tri', [128, 256], BF16)[:, :], writes=['tri'])
    triT = tri[:, 0:128]
    tri2 = tri[:, 128:256]
    McT = k.b16(NQT * NCC * 128).rearrange("p (a b c) -> p a b c", a=NQT, b=NCC)
    k.dma(McT, k.din('McT', [128, NQT, NCC, 128], BF16)[:, :, :, :], writes=['McT'])
    val = v3(k.b16(NQT * NSEL), NQT)
    bon = v3(k.b16(NQT * NSEL), NQT)
    pneg = v3(k.b16(NQT * NSEL), NQT)
    k.dma(val, k.din('sval', [128, NQT, NSEL], BF16)[:, :, :], writes=['val'])
    k.dma(bon, k.din('sbon', [128, NQT, NSEL], BF16)[:, :, :], writes=['bon'])
    k.dma(pneg, k.din('spneg', [128, NQT, NSEL], BF16)[:, :, :], writes=['pneg'])
    pcr = v3(k.b16(NCH * CH, parts=1), NCH)
    k.dma(pcr, k.din('pcrow', [1, NCH, CH], BF16)[:, :, :], writes=['pcr'])
    identf = k.f32(128)
    k.copy(identf, k.ident, reads=['cst'], writes=['identf'])
    posT = v3(load_vec(k, 'cmp_posT', 2 * 32), 2)
    kcmpT = [k.b16(NCC * 128) for _ in range(2)]
    vcmp = [v3(k.b16(NCC * 128), NCC) for _ in range(2)]
    onescol = k.ones[:, 0:1]
    ones1 = k.ones[0:1, :]
    m2 = k.mark()
    rawin = [k.din('kcT_full', [256, S], BF16), k.din('vcT_full', [256, S], BF16)]
    wcin = k.din('w_cmp', [2, 32, 128, 128], F32)
    raw = [k.b16(S) for _ in range(2)]
    wc = [v3(k.b16(32 * 128), 32) for _ in range(2)]
    tmp = [k.b16(NCC * 128) for _ in range(4)]
    for t_ in range(4):
        k.memset(tmp[t_], 0.0, writes=[('tmp', t_)])
    it = 0
    for kv in range(2):
        k.dma(wc[kv], wcin[kv].rearrange("l d e -> d l e"), writes=[('wc', kv)], eng='pool')
        for grp in range(2):
            r = it % 2
            it += 1
            k.dma(raw[r], rawin[kv][grp * 128:(grp + 1) * 128, :], writes=[('raw', r)])
            pb = 2 + r
            for l in range(32):
                t_ = nextbank(k, 'tmp', (0, 1, 2, 3))
                k.ts(tmp[t_][:, 0:NCMP], raw[r][:, l:l + 16 * (NCMP - 1) + 1:16], posT[:, kv, l:l + 1], None, ALU.add,
                     reads=[('raw', r), 'cmp_posT'], writes=[('tmp', t_)])
                if kv == 0:
                    k.mm(k.ps[pb][:, 0:NCC * 128], wc[kv][:, l, :], tmp[t_], start=(l == 0), stop=(l == 31),
                         reads=[('wc', kv), ('tmp', t_)], writes=[('ps', pb)])
                else:
                    for c in range(NCC):
                        k.mm(k.ps[pb][:, c * 128:(c + 1) * 128], tmp[t_][:, c * 128:(c + 1) * 128], wc[kv][:, l, :],
                             start=(l == 0 and c == 0), stop=True, skip=not (l == 0 and c == 0),
                             reads=[('wc', kv), ('tmp', t_)], writes=[('ps', pb)])
            if kv == 0:
                k.copy(kcmpT[grp], k.ps[pb][:, 0:NCC * 128], reads=[('ps', pb)], writes=[('kcmp', grp)])
            else:
                k.copy(vcmp[grp], v3(k.ps[pb][:, 0:NCC * 128], NCC), reads=[('ps', pb)], writes=[('vcmp', grp)])
    k.P.barrier()
    k.reset(m2)
    ksin = k.din('ksT_full', [256, S], BF16)
    vsin = k.din('vs_full', [2, 128, cfg.NKT, 128], BF16)
    kslin = k.din('ksT_own', [256, NT], BF16)
    vslin = k.din('vs_own', [NT, 256], BF16)
    kwpin = k.din('kw_pack', [256, NCH, 2 * CH], BF16)
    vwpin = k.din('vw_pack', [2, 128, NCH, 8, 128], BF16)
    ks = k.b16(S)
    vs = v3(k.b16(cfg.NKT * 128), cfg.NKT)
    ksl = k.b16(NT)
    vsl = v3(k.b16(NQT * 128), NQT)
    kwp = v3(k.b16(NCH * 2 * CH), NCH)
    vwp = k.b16(NCH * 8 * 128).rearrange("p (a b c) -> p a b c", a=NCH, b=8)
    Pc = [k.b16(CH) for _ in range(NCC)]
    pt = [k.b16(CH) for _ in range(3)]
    OB = [k.f32(CH) for _ in range(3)]
    rr_ = k.f32(CH)
    rd4 = k.f32(4)
    imp = k.f32(NSEL)
    imp2 = k.f32(NSEL)
    imp3 = k.f32(NSEL)
    m8 = k.f32(16)
    mbs = k.f32(NSEL)
    mT = k.b16(128, parts=NSEL)
    osum = k.f32(CH)
    obank = [2, 3]
    dbank = [4, 5]
    br = 0

    def softmax_tile(zb, vT, o_b, d_b, first, last):
        p = nextbank(k, 'pt', (0, 1, 2))
        k.act(pt[p], k.ps[zb][:, :], AF.Exp, scale=SCALE, reads=[('ps', zb)], writes=[('pt', p)])
        k.mm(k.ps[o_b][:, :], vT[0], pt[p], start=first, stop=last, reads=[vT[1], ('pt', p)], writes=[('ps', o_b)])
        k.mm(k.ps[d_b][:, :], k.ones, pt[p], start=first, stop=last, reads=['cst', ('pt', p)], writes=[('ps', d_b)])
        return p

    def add_mask(zb, lhsT, rhs, rkeys, last):
        for hg in range(4):
            k.mm(k.ps[zb][:, hg * 128:(hg + 1) * 128], lhsT, rhs, start=False, stop=True, skip=True,
                 reads=rkeys, writes=[('ps', zb)])

    def finish_branch(o_b, d_b, dst):
        k.ts(rr_, k.ps[d_b][:, :], 1e-30, None, ALU.max, reads=[('ps', d_b)], writes=['rr'])
        k.recip(rr_, rr_, reads=['rr'], writes=['rr'])
        k.tt(dst[0], k.ps[o_b][:, :], rr_, ALU.mult, reads=[('ps', o_b), 'rr'], writes=[dst[1]])

    for grp in range(2):
        gs = slice(grp * 128, (grp + 1) * 128)
        k.dma(ks, ksin[gs, :], writes=['ks'])
        k.dma(vs, vsin[grp], writes=['vs'])
        k.dma(ksl, kslin[gs, :], writes=['ksl'])
        k.dma(vsl, vslin[:, gs].rearrange("(a p) d -> p a d", p=128), writes=['vsl'])
        k.dma(kwp, kwpin[gs, :, :], writes=['kwp'])
        k.dma(vwp, vwpin[grp], writes=['vwp'])
        for qt in range(NQT):
            lc, u = qt // 4, qt % 4
            Q = qsb[:, qt, grp * 4:(grp + 1) * 4, :].rearrange("p a b -> p (a b)")
            qk = [('q', grp * 4 + hg) for hg in range(4)]
            o_b, d_b = obank[br % 2], dbank[br % 2]
            br += 1
            for c in range(NCC):
                zb = nextbank(k, 'z', (0, 1))
                k.mm(k.ps[zb][:, :], kcmpT[grp][:, c * 128:(c + 1) * 128], Q, start=True, stop=True,
                     reads=[('kcmp', grp)] + qk, writes=[('ps', zb)])
                add_mask(zb, k.ident, McT[:, qt, c, :], ['cst', 'McT'], True)
                k.act(Pc[c], k.ps[zb][:, :], AF.Exp, scale=SCALE, reads=[('ps', zb)], writes=[('Pc', c)])
                k.mm(k.ps[o_b][:, :], vcmp[grp][:, c, :], Pc[c], start=(c == 0), stop=(c == NCC - 1),
                     reads=[('vcmp', grp), ('Pc', c)], writes=[('ps', o_b)])
                k.mm(k.ps[d_b][:, :], k.ones, Pc[c], start=(c == 0), stop=(c == NCC - 1), reads=['cst', ('Pc', c)], writes=[('ps', d_b)])
            for hg in range(4):
                for c in range(NCC):
                    first = (hg == 0 and c == 0)
                    k.mm(k.ps[6][:, hg * NSEL:(hg + 1) * NSEL], Pc[c][:, hg * 128:(hg + 1) * 128], Ov[:, c, :],
                         start=first, stop=True, skip=not first, reads=[('Pc', c), 'Ov'], writes=[('ps', 6)])
            for hg in range(4):
                for c in range(NCC):
                    first = (hg == 0 and c == 0)
                    k.mm(k.ps[7][:, hg:hg + 1], Pc[c][:, hg * 128:(hg + 1) * 128], onescol,
                         start=first, stop=True, skip=not first, reads=[('Pc', c), 'cst'], writes=[('ps', 7)])
            k.ts(rd4, k.ps[7][:, 0:4], 1e-30, None, ALU.max, reads=[('ps', 7)], writes=['rd4'])
            k.recip(rd4, rd4, reads=['rd4'], writes=['rd4'])
            k.ts(imp, k.ps[6][:, 0:NSEL], rd4[:, 0:1], None, ALU.mult, reads=[('ps', 6), 'rd4'], writes=['imp'])
            for hg in range(1, 4):
                k.stt(imp, k.ps[6][:, hg * NSEL:(hg + 1) * NSEL], rd4[:, hg:hg + 1], imp, ALU.mult, ALU.add,
                      reads=[('ps', 6), 'rd4', 'imp'], writes=['imp'])
            k.tt(imp2, imp, val[:, qt, :], ALU.mult, reads=['imp', 'val'], writes=['imp2'])
            k.tt(imp2, imp2, bon[:, qt, :], ALU.add, reads=['imp2', 'bon'], writes=['imp2'])
            k.P.op('dve', lambda e: e.max(out=m8[:, 0:8], in_=imp2), ['imp2'], ['m8a'])
            k.P.op('dve', lambda e: e.match_replace(out=imp3, in_to_replace=m8[:, 0:8], in_values=imp2, imm_value=-1e9),
                   ['imp2', 'm8a'], ['imp3'])
            k.P.op('dve', lambda e: e.max(out=m8[:, 8:16], in_=imp3), ['imp3'], ['m8b'])
            k.ts(mbs, imp2, m8[:, 15:16], -NEG, ALU.is_ge, ALU.mult, reads=['imp2', 'm8b'], writes=['mbs'])
            k.stt(mbs, mbs, NEG, pneg[:, qt, :], ALU.add, ALU.add, reads=['mbs', 'pneg'], writes=['mbs'])
            k.P.op('pe', lambda e: e.transpose(k.ps[7][0:NSEL, 128:256], mbs, identf), ['mbs', 'identf'], [('ps', 7)])
            k.copy(mT, k.ps[7][0:NSEL, 128:256], reads=[('ps', 7)], writes=['mT'])
            finish_branch(o_b, d_b, (OB[0], ('OB', 0)))
            o_b, d_b = obank[br % 2], dbank[br % 2]
            br += 1
            KM = cfg.KM[lc]
            for kt in range(KM):
                zb = nextbank(k, 'z', (0, 1))
                k.mm(k.ps[zb][:, :], ks[:, kt * 128:(kt + 1) * 128], Q, start=True, stop=True, reads=['ks'] + qk, writes=[('ps', zb)])
                add_mask(zb, G[:, kt * 128:(kt + 1) * 128], mT, ['G', 'mT'], True)
                softmax_tile(zb, (vs[:, kt, :], 'vs'), o_b, d_b, kt == 0, False)
            zb = nextbank(k, 'z', (0, 1))
            k.mm(k.ps[zb][:, :], ksl[:, qt * 128:(qt + 1) * 128], Q, start=True, stop=True, reads=['ksl'] + qk, writes=[('ps', zb)])
            add_mask(zb, k.ident, triT, ['cst', 'tri'], True)
            softmax_tile(zb, (vsl[:, qt, :], 'vsl'), o_b, d_b, False, True)
            finish_branch(o_b, d_b, (OB[1], ('OB', 1)))
            o_b, d_b = obank[br % 2], dbank[br % 2]
            br += 1
            for i in range(5):
                w = u + i
                zb = nextbank(k, 'z', (0, 1))
                k.mm(k.ps[zb][:, :], kwp[:, lc, w * 128:(w + 1) * 128], Q, start=True, stop=True, reads=['kwp'] + qk, writes=[('ps', zb)])
                if i == 0:
                    add_mask(zb, k.ident, tri2, ['cst', 'tri'], False)
                if i == 4:
                    add_mask(zb, k.ident, triT, ['cst', 'tri'], False)
                if w < 4:
                    k.mm(k.ps[zb][:, :], ones1, pcr[:, lc, :], start=False, stop=True, skip=True, reads=['cst', 'pcr'], writes=[('ps', zb)])
                softmax_tile(zb, (vwp[:, lc, w, :], 'vwp'), o_b, d_b, i == 0, i == 4)
            finish_branch(o_b, d_b, (OB[2], ('OB', 2)))
            for b in range(3):
                for hg in range(4):
                    r0 = (b * 8 + grp * 4 + hg) * 128
                    k.mm(k.ps[6][:, hg * 128:(hg + 1) * 128], selg[:, r0:r0 + 128], gb[:, qt * 128:(qt + 1) * 128],
                         start=(hg == 0), stop=True, skip=(hg > 0), reads=['selg', 'gb'], writes=[('ps', 6)])
                k.tt(OB[b], OB[b], k.ps[6][:, :], ALU.mult, reads=[('OB', b), ('ps', 6)], writes=[('OB', b)])
            k.tt(osum, OB[0], OB[1], ALU.add, reads=[('OB', 0), ('OB', 1)], writes=['osum'])
            k.tt(ob[:, grp * 4:(grp + 1) * 4, qt * 128:(qt + 1) * 128], v3(osum, 4), v3(OB[2], 4), ALU.add,
                 reads=['osum', ('OB', 2)], writes=[('ob', grp * 4 + hg, lc) for hg in range(4)])
    k.P.barrier()
    k.reset(m)


ARENA_WORDS = 53000


def attn_tail(k, wname, m, mq, q, li, nxt):
    k.P.barrier()
    k.reset(mq)
    mo = k.mark()
    out_proj_residual(k, wname, q, lambda h, c: [('q', h, c)])
    k.P.barrier()
    k.reset(m)
    mlp(k, li)
    if nxt is not None:
        nxt(k)
    store_x(k)


def build_launch(li, cfg):
    st = ExitStack()
    k = K(cfg, st)
    k.start(ARENA_WORDS)
    load_consts(k)
    if li == 0:
        load_x(k)
        nsa_pre(k)
    elif li == 1:
        ob = v3(k.b16(H * NT), H)
        nsa_core(k, ob)
        load_x(k)
        mo = k.mark()
        k.din('w_out0', [D, D], F32)
        out_proj_residual(k, 'w_out0', ob, lambda h, c: [('ob', h, c)])
        k.P.barrier()
        k.reset(mo)
        mlp(k, 0)
        qkv_pre(k, 1, False)
        store_x(k)
    elif li == 2:
        load_x(k)
        m, mq, q = sb_core(k)
        k.din('w_out1', [D, D], F32)
        attn_tail(k, 'w_out1', m, mq, q, 1, conv_pre)
    elif li == 3:
        load_x(k)
        m, ucb = conv_core(k)
        k.din('w_out2', [D, D], F32)
        out_proj_residual(k, 'w_out2', ucb, lambda h, c: [('ucb', h, c)])
        k.P.barrier()
        k.reset(m)
        mlp(k, 2)
        qkv_pre(k, 3, True)
        store_x(k)
    elif li == 4:
        load_x(k)
        m, mq, q = moba_core(k)
        k.din('w_out3', [D, D], F32)
        attn_tail(k, 'w_out3', m, mq, q, 3, None)
    nc = k.finish()
    return nc, k, st


def pp(v, n):
    return np.ascontiguousarray(np.asarray(v, np.float32).reshape(n, 128).T)


def kernel_impl(inputs, cfg, runner, upto=5, dbg=None):
    R, B, S, NC = cfg.R, cfg.B, cfg.S, cfg.NC
    NKT = cfg.NKT
    x = np.asarray(inputs['x'], np.float32)
    gch = [cfg.gchunks(c % R) for c in range(NC)]
    bat = [c // R for c in range(NC)]
    pos = [np.concatenate([np.arange(g * CH, (g + 1) * CH) for g in gch[c]]) for c in range(NC)]
    cst = host_consts()
    P = {n: np.asarray(v, np.float32) for n, v in inputs.items() if n != 'x'}

    def run(li, in_maps):
        nc, k, st = build_launch(li, cfg)
        for im in in_maps:
            assert set(im.keys()) == set(k.in_specs.keys()), (sorted(set(im.keys()) ^ set(k.in_specs.keys())))
            for n, (shp, dt) in k.in_specs.items():
                want = bf if dt == BF16 else np.float32
                a = im[n]
                if a.dtype != want:
                    a = a.astype(want)
                assert tuple(a.shape) == tuple(shp), (n, a.shape, shp)
                im[n] = np.ascontiguousarray(a)
        res = runner(nc, in_maps)
        st.close()
        return [{n[2:]: v for n, v in r.items()} for r in res]

    def to_global(res, name, axis):
        out = []
        for b in range(B):
            parts = [None] * (NCH * R)
            for c in range(NC):
                if bat[c] != b:
                    continue
                a = np.asarray(res[c][name])
                for lc, g in enumerate(gch[c]):
                    sl = [slice(None)] * a.ndim
                    sl[axis] = slice(lc * CH, (lc + 1) * CH)
                    parts[g] = a[tuple(sl)]
            out.append(np.concatenate(parts, axis=axis))
        return out

    def vlay(vg, nh):
        return np.ascontiguousarray(vg.reshape(NKT, 128, nh, 128).transpose(2, 1, 0, 3))

    inv_freq = (np.float32(10000.0) ** (-np.arange(64, dtype=np.float32) / np.float32(64))).astype(np.float32)
    cosT, sinT = [], []
    for c in range(NC):
        ang = pos[c].astype(np.float32)[:, None] * inv_freq[None, :]
        cosT.append(np.ascontiguousarray(np.concatenate([np.cos(ang), np.cos(ang)], axis=1).T.astype(np.float32)))
        sinT.append(np.ascontiguousarray(np.concatenate([np.sin(ang), np.sin(ang)], axis=1).T.astype(np.float32)))

    def mlpw(li):
        return {'g_mlp%d' % li: pp(P['mlp_norm'][li], DC), 'w_up%d' % li: P['mlp_w_up'][li], 'w_dn%d' % li: P['mlp_w_down'][li]}

    xT = [np.ascontiguousarray(x[bat[c], pos[c], :].T) for c in range(NC)]
    ims = [dict(cst=cst, xT=xT[c], g_an0=pp(P['attn_norm'][0], DC), w_in0=P['nsa_w_in'][0], cosT=cosT[c], sinT=sinT[c],
                nsa_qn=pp(P['nsa_q_norm'][0], 1), nsa_kn=np.ascontiguousarray(P['nsa_k_norm'][0].T)) for c in range(NC)]
    r0 = run(0, ims)
    if dbg is not None:
        dbg['r0'] = r0
    if upto <= 1:
        return r0
    NSEL = S // 64
    NCMP = S // 16 - 1
    NCC = (NCMP + 127) // 128
    NQT = NT // 128
    kcf, vcf, ksf, kwf = [to_global(r0, n, 1) for n in ('kcT', 'vcT', 'ksT', 'kwT')]
    vsf, vwf = [to_global(r0, n, 0) for n in ('vs', 'vw')]
    jj = np.arange(NSEL)
    Gm = (np.arange(S)[None, :] // 64 == jj[:, None]).astype(np.float32)
    nn = np.arange(NCC * 128)
    ovl = ((16 * nn[:, None] < 64 * jj[None, :] + 64) & (16 * nn[:, None] + 32 > 64 * jj[None, :]) & (nn[:, None] < NCMP)).astype(np.float32)
    Ov = np.ascontiguousarray(ovl.reshape(NCC, 128, NSEL).transpose(1, 0, 2))
    ss, tt = np.meshgrid(np.arange(128), np.arange(128), indexing='ij')
    tri = np.concatenate([np.where(ss <= tt, 0.0, NEG), np.where(ss > tt, 0.0, NEG)], axis=1).astype(np.float32)
    selg = np.zeros((24, 24 * 128), np.float32)
    for r_ in range(24):
        selg[r_, r_ * 128:(r_ + 1) * 128] = 1.0
    ims = []
    for c in range(NC):
        b = bat[c]
        McT = np.zeros((128, NQT, NCC, 128), np.float32)
        sval = np.zeros((128, NQT, NSEL), np.float32)
        sbon = np.zeros((128, NQT, NSEL), np.float32)
        spneg = np.zeros((128, NQT, NSEL), np.float32)
        pcrow = np.zeros((1, NCH, CH), np.float32)
        kwp = np.zeros((256, NCH, 2 * CH), np.float32)
        vwp = np.zeros((2, 128, NCH, 8, 128), np.float32)
        for lc, g in enumerate(gch[c]):
            if g == 0:
                pcrow[0, lc, :] = NEG
            lo = (g - 1) * CH
            if g > 0:
                kwp[:, lc, 0:CH] = kwf[b][:, lo:lo + CH].astype(np.float32)
                vprev = vwf[b][lo:lo + CH].astype(np.float32)
            else:
                vprev = np.zeros((CH, 256), np.float32)
            kwp[:, lc, CH:] = kwf[b][:, g * CH:(g + 1) * CH].astype(np.float32)
            vboth = np.concatenate([vprev, vwf[b][g * CH:(g + 1) * CH].astype(np.float32)], axis=0)
            vwp[:, :, lc] = vboth.reshape(8, 128, 2, 128).transpose(2, 1, 0, 3)
            for u in range(4):
                qt = lc * 4 + u
                T = 4 * g + u
                tl = np.arange(128)
                tg = 128 * T + tl
                n_ = np.arange(NCC * 128).reshape(NCC, 128)
                vis = (16 * n_[:, :, None] + 31 <= tg[None, None, :]) & (n_[:, :, None] < NCMP)
                McT[:, qt, :, :] = np.where(vis, 0.0, NEG).transpose(1, 0, 2)
                cur = tg // 64
                le = jj[None, :] <= cur[:, None]
                forced = (jj[None, :] == 0) | (jj[None, :] == cur[:, None]) | (jj[None, :] == cur[:, None] - 1)
                sval[:, qt, :] = le
                sbon[:, qt, :] = np.where(le, 1000.0 * forced, -1.0)
                spneg[:, qt, :] = np.where(jj[None, :] >= 2 * T, NEG, 0.0)
        ksown = np.asarray(r0[c]['ksT'])
        ims.append(dict(cst=cst, qT=np.asarray(r0[c]['qT']), gT=np.asarray(r0[c]['gT']), selg=selg, G=Gm, Ov=Ov, tri=tri,
                        McT=McT, sval=sval, sbon=sbon, spneg=spneg, pcrow=pcrow,
                        cmp_posT=np.ascontiguousarray(P['nsa_cmp_pos'][0].transpose(2, 0, 1).reshape(128, 64)),
                        kcT_full=kcf[b], vcT_full=vcf[b], w_cmp=P['nsa_w_cmp'][0], ksT_full=ksf[b], vs_full=vlay(vsf[b], 2),
                        ksT_own=ksown, vs_own=np.asarray(r0[c]['vs']), kw_pack=kwp, vw_pack=vwp,
                        xT=xT[c], w_out0=P['nsa_w_out'][0], g_an1=pp(P['attn_norm'][1], DC), w_in1=P['sb_w_in'][0], **mlpw(0)))
    r1 = run(1, ims)
    if dbg is not None:
        dbg['r1'] = r1
    if upto <= 2:
        return r1
    ktf = to_global(r1, 'kT', 1)
    vf = to_global(r1, 'v', 0)
    js, s_ = np.meshgrid(np.arange(128), np.arange(128), indexing='ij')
    sbc = np.zeros((128, 256), np.float32)
    sbc[:, 0:128] = np.where(js >= s_, -1.0, 0.0)
    sbc[:, 128:256] = -1.0
    ims = []
    for c in range(NC):
        b = bat[c]
        sbm = np.zeros((128, NCH, 4 * R, CH), np.float32)
        for lc, g in enumerate(gch[c]):
            for i in range(4 * R):
                kt = 4 * cfg.GMIN[lc] + i
                sg_ = 128 * kt + np.arange(128)
                tg = CH * g + np.arange(CH)
                sbm[:, lc, i, :] = sg_[:, None] < tg[None, :]
        ims.append(dict(cst=cst, xT=np.asarray(r1[c]['xT_out']), qT=np.asarray(r1[c]['qT']), kT_full=ktf[b], v_full=vlay(vf[b], H),
                        sbmask=sbm, sbc=sbc, w_out1=P['sb_w_out'][0], g_an2=pp(P['attn_norm'][2], DC), conv_w_in=P['conv_w_in'][0], **mlpw(1)))
    r2 = run(2, ims)
    if dbg is not None:
        dbg['r2'] = r2
    if upto <= 3:
        return r2
    uf = to_global(r2, 'uT', 1)
    ims = []
    for c in range(NC):
        b = bat[c]
        uE = np.zeros((D, NCH, 32 + CH), np.float32)
        for lc, g in enumerate(gch[c]):
            if g > 0:
                uE[:, lc, 0:32] = uf[b][:, g * CH - 32:g * CH].astype(np.float32)
            uE[:, lc, 32:] = uf[b][:, g * CH:(g + 1) * CH].astype(np.float32)
        dw = P['conv_dw_w'][0]
        conv_dw = np.ascontiguousarray(dw.reshape(31, DC, 128).transpose(2, 1, 0).reshape(128, DC * 31))
        ims.append(dict(cst=cst, xT=np.asarray(r2[c]['xT_out']), uE=uE, conv_dw=conv_dw, conv_dwb=pp(P['conv_dw_b'][0], DC),
                        conv_lng=pp(P['conv_ln_g'][0], DC), conv_lnb=pp(P['conv_ln_b'][0], DC), w_out2=P['conv_w_out'][0],
                        g_an3=pp(P['attn_norm'][3], DC), w_in3=P['moba_w_in'][0], cosT=cosT[c], sinT=sinT[c],
                        moba_qn=pp(P['moba_q_norm'][0], 1), moba_kn=pp(P['moba_k_norm'][0], 1), **mlpw(2)))
    r3 = run(3, ims)
    if dbg is not None:
        dbg['r3'] = r3
    if upto <= 4:
        return r3
    NB = S // 256
    ktf = to_global(r3, 'kT', 1)
    vf = to_global(r3, 'v', 0)
    kmf = []
    for b in range(B):
        km = np.zeros((D, NB), np.float32)
        for c in range(NC):
            if bat[c] == b:
                a = np.asarray(r3[c]['kmean'])
                for lc, g in enumerate(gch[c]):
                    km[:, 2 * g:2 * g + 2] = a[:, 2 * lc:2 * lc + 2]
        kmf.append(km)
    esel = np.zeros((NB, NB * 128), np.float32)
    for n_ in range(NB):
        esel[n_, n_ * 128:(n_ + 1) * 128] = 1.0
    om = np.zeros((128, 4, CH), np.float32)
    for i2 in range(4):
        sl_ = 128 * i2 + np.arange(128)
        tl = np.arange(CH)
        ok = (i2 // 2 == tl[None, :] // 256) & (sl_[:, None] <= tl[None, :])
        om[:, i2, :] = np.where(ok, 0.0, NEG)
    ims = []
    for c in range(NC):
        b = bat[c]
        mv = np.zeros((128, NCH, 4 * NB), np.float32)
        for lc, g in enumerate(gch[c]):
            for u in range(4):
                cur = 2 * g + u // 2
                mv[:, lc, u * NB:(u + 1) * NB] = np.where(np.arange(NB) < cur, 0.0, -1e30)[None, :]
        ims.append(dict(cst=cst, xT=np.asarray(r3[c]['xT_out']), qT=np.asarray(r3[c]['qT']), kT_full=ktf[b], v_full=vlay(vf[b], H),
                        kT_own=np.asarray(r3[c]['kT']), v_own=np.asarray(r3[c]['v']), kmean_full=kmf[b], mvalid=mv, esel=esel,
                        ownmask=om, w_out3=P['moba_w_out'][0], **mlpw(3)))
    r4 = run(4, ims)
    if dbg is not None:
        dbg['r4'] = r4
    out = np.zeros((B, S, D), np.float32)
    for c in range(NC):
        out[bat[c], pos[c], :] = np.asarray(r4[c]['xT_out']).T
    return out


def hw_runner(nc, in_maps):
    res = run_bass_kernel_spmd(nc, in_maps, core_ids=list(range(len(in_maps))))
    return res.results


def kernel(**inputs):
    return kernel_impl(inputs, Cfg(4, 2), hw_runner)
```

```python
import numpy as np
import ml_dtypes
from contextlib import ExitStack
import concourse.bass as bass
import concourse.mybir as mybir
from concourse.bass_utils import run_bass_kernel_spmd

F32 = mybir.dt.float32
BF16 = mybir.dt.bfloat16
AF = mybir.ActivationFunctionType
ALU = mybir.AluOpType
bf = ml_dtypes.bfloat16

D = 1024
DC = 8
H = 8
DH = 128
CH = 512
NCH = 4
NT = NCH * CH
DFF = 4096
EPS = 1e-6
NEG = -30000.0
SCALE = DH ** -0.5


class Prog:
    NDMA = 24

    def __init__(self, nc):
        self.nc = nc
        self.ops = []
        self.lastw = {}
        self.readers = {}
        self.bar = set()
        self.last_eng = {}
        self.dma_since = []

    def op(self, eng, fn, reads=(), writes=(), dma=False):
        idx = len(self.ops)
        deps = set(self.bar)
        for k in reads:
            if k in self.lastw:
                deps.add(self.lastw[k])
        for k in writes:
            if k in self.lastw:
                deps.add(self.lastw[k])
            deps.update(self.readers.get(k, ()))
        for k in reads:
            self.readers.setdefault(k, []).append(idx)
        for k in writes:
            self.lastw[k] = idx
            self.readers[k] = []
        self.ops.append(dict(eng=eng, fn=fn, deps=deps, dma=dma))
        self.last_eng[eng] = idx
        if dma:
            self.dma_since.append(idx)
        return idx

    def barrier(self):
        self.bar = set(self.last_eng.values()) | set(self.dma_since)
        self.dma_since = []
        self.lastw = {}
        self.readers = {}

    def emit(self, stack):
        nc = self.nc
        ops = self.ops
        n = len(ops)
        needed = [False] * n
        for i, o in enumerate(ops):
            nd = set()
            for d in o['deps']:
                if ops[d]['eng'] == 'pe' and o['eng'] == 'pe' and not ops[d]['dma']:
                    continue
                nd.add(d)
                needed[d] = True
            o['deps'] = nd
        engs = ['pe', 'act', 'dve', 'pool', 'sp']
        esem = {e: stack.enter_context(nc.semaphore('s_' + e)) for e in engs}
        dsem = [stack.enter_context(nc.semaphore('d_%d' % i)) for i in range(self.NDMA)]
        ecount = {e: 0 for e in engs}
        dcount = [0] * self.NDMA
        rr = {'sp': 0, 'pool': 0}
        NSP = 16
        for i, o in enumerate(ops):
            if o['dma']:
                if o['eng'] == 'pool':
                    s = NSP + rr['pool'] % (self.NDMA - NSP)
                    rr['pool'] += 1
                else:
                    s = rr['sp'] % NSP
                    rr['sp'] += 1
                o['prev'] = (dsem[s], dcount[s]) if dcount[s] > 0 else None
                dcount[s] += 16
                o['sig'] = (dsem[s], dcount[s])
            else:
                if needed[i]:
                    ecount[o['eng']] += 1
                    o['sig'] = (esem[o['eng']], ecount[o['eng']])
                else:
                    o['sig'] = None
        final_d = [(dsem[s], dcount[s]) for s in range(self.NDMA) if dcount[s] > 0]
        block = stack.enter_context(nc.Block())

        def run(ename, e):
            waited = {}

            def w(sem, val):
                k = id(sem)
                if waited.get(k, 0) >= val:
                    return
                waited[k] = val
                e.wait_ge(sem, val)
            for o in ops:
                if o['eng'] != ename:
                    continue
                for d in sorted(o['deps']):
                    sg = ops[d]['sig']
                    w(sg[0], sg[1])
                if o['dma'] and o['prev'] is not None:
                    w(*o['prev'])
                ins = o['fn'](e)
                if o['dma']:
                    ins.then_inc(o['sig'][0], 16)
                elif o['sig'] is not None:
                    ins.then_inc(o['sig'][0], 1)
            if ename == 'sp':
                for sem, val in final_d:
                    w(sem, val)

        @block.tensor
        def _(e):
            run('pe', e)

        @block.scalar
        def _(e):
            run('act', e)

        @block.vector
        def _(e):
            run('dve', e)

        @block.gpsimd
        def _(e):
            run('pool', e)

        @block.sync
        def _(e):
            run('sp', e)


class Cfg:
    def __init__(self, R=4, B=2):
        self.R = R
        self.B = B
        self.S = NCH * R * CH
        self.NC = R * B
        self.NKT = self.S // 128
        self.KM = [4 * R, 8 * R, 12 * R, 16 * R]
        self.GMIN = [0, R, 2 * R, 3 * R]

    def gchunks(self, j):
        R = self.R
        return [j, 2 * R - 1 - j, 2 * R + j, 4 * R - 1 - j]


class K:
    def __init__(self, cfg, st):
        self.cfg = cfg
        self.st = st
        self.nc = bass.Bass("TRN2", target_bir_lowering=False)
        self.P = Prog(self.nc)
        self.dram = {}
        self.in_specs = {}
        self.out_specs = {}
        self.arena = None
        self.off = 0
        self.hiwater = 0
        self.uid = 0

    def start(self, words):
        self.arena = self.st.enter_context(self.nc.sbuf_tensor("arena", [128, words], F32))
        self.words = words
        self.ps = [self.st.enter_context(self.nc.psum_tensor("ps%d" % i, [128, 512], F32)) for i in range(8)]

    def din(self, name, shape, dt):
        t = self.nc.dram_tensor(name, list(shape), dt, kind="ExternalInput").ap()
        self.dram[name] = t
        self.in_specs[name] = (tuple(shape), dt)
        return t

    def dout(self, name, shape, dt):
        t = self.nc.dram_tensor('o_' + name, list(shape), dt, kind="ExternalOutput").ap()
        self.dram[name] = t
        self.out_specs[name] = (tuple(shape), dt)
        return t

    def f32(self, n, parts=128):
        a = self.arena[0:parts, self.off:self.off + n]
        self.off += n
        self.hiwater = max(self.hiwater, self.off)
        assert self.off <= self.words, ("arena overflow", self.off, self.words)
        return a

    def b16(self, n, parts=128):
        w = (n + 1) // 2
        a = self.arena[0:parts, self.off:self.off + w].bitcast(BF16)
        self.off += w
        self.hiwater = max(self.hiwater, self.off)
        assert self.off <= self.words, ("arena overflow", self.off, self.words)
        return a[:, 0:n]

    def mark(self):
        return self.off

    def reset(self, m):
        self.off = m

    def key(self, s):
        self.uid += 1
        return "%s#%d" % (s, self.uid)

    def dma(self, out, in_, reads=(), writes=(), eng='sp'):
        self.P.op(eng, lambda e, out=out, in_=in_: e.dma_start(out=out, in_=in_), reads, writes, dma=True)

    def mm(self, out, lhsT, rhs, start=True, stop=True, reads=(), writes=(), skip=False):
        self.P.op('pe', lambda e, out=out, lhsT=lhsT, rhs=rhs, start=start, stop=stop, skip=skip:
                  e.matmul(out, lhsT, rhs, start=start, stop=stop, skip_group_check=skip), reads, writes)

    def act(self, out, in_, func, reads=(), writes=(), bias=None, scale=None):
        kw = {}
        if bias is not None:
            kw['bias'] = bias
        if scale is not None:
            kw['scale'] = scale
        self.P.op('act', lambda e, out=out, in_=in_, func=func, kw=kw: e.activation(out=out, in_=in_, func=func, **kw),
                  reads, writes)

    def tt(self, out, in0, in1, op, reads=(), writes=(), eng='dve'):
        self.P.op(eng, lambda e, out=out, in0=in0, in1=in1, op=op: e.tensor_tensor(out=out, in0=in0, in1=in1, op=op),
                  reads, writes)

    def ts(self, out, in0, s1, s2, op0, op1=None, reads=(), writes=(), eng='dve'):
        if op1 is None:
            self.P.op(eng, lambda e, out=out, in0=in0, s1=s1, op0=op0:
                      e.tensor_scalar(out=out, in0=in0, scalar1=s1, scalar2=None, op0=op0), reads, writes)
        else:
            self.P.op(eng, lambda e, out=out, in0=in0, s1=s1, s2=s2, op0=op0, op1=op1:
                      e.tensor_scalar(out=out, in0=in0, scalar1=s1, scalar2=s2, op0=op0, op1=op1), reads, writes)

    def stt(self, out, in0, scalar, in1, op0, op1, reads=(), writes=(), eng='dve'):
        self.P.op(eng, lambda e, out=out, in0=in0, scalar=scalar, in1=in1, op0=op0, op1=op1:
                  e.scalar_tensor_tensor(out=out, in0=in0, scalar=scalar, in1=in1, op0=op0, op1=op1), reads, writes)

    def copy(self, out, in_, reads=(), writes=(), eng='dve'):
        self.P.op(eng, lambda e, out=out, in_=in_: e.tensor_copy(out=out, in_=in_), reads, writes)

    def recip(self, out, in_, reads=(), writes=()):
        self.P.op('dve', lambda e, out=out, in_=in_: e.reciprocal(out=out, in_=in_), reads, writes)

    def memset(self, ap, val, writes=(), eng='pool'):
        self.P.op(eng, lambda e, ap=ap, val=val: e.memset(ap, val), (), writes)

    def finish(self):
        self.P.emit(self.st)
        return self.nc


def v3(ap, a):
    return ap.rearrange("p (a b) -> p a b", a=a)


def load_consts(k):
    cin = k.din('cst', [128, 5 * 128], BF16)
    c = k.b16(5 * 128)
    k.dma(c, cin[:, :], writes=['cst'])
    k.ident = c[:, 0:128]
    k.onesD = c[:, 128:256]
    k.onesH = c[:, 256:384]
    k.ones = c[:, 384:512]
    k.Rm = c[:, 512:640]
    k.bank_rr = {}


def host_consts():
    c = np.zeros((128, 5 * 128), np.float32)
    c[:, 0:128] = np.eye(128)
    c[:, 128:256] = 1.0 / 1024
    c[:, 256:384] = 1.0 / 128
    c[:, 384:512] = 1.0
    Rm = np.zeros((128, 128), np.float32)
    for dd in range(64):
        Rm[dd + 64, dd] = -1.0
        Rm[dd, dd + 64] = 1.0
    c[:, 512:640] = Rm
    return c.astype(bf)


def nextbank(k, role, banks):
    i = k.bank_rr.get(role, 0)
    k.bank_rr[role] = i + 1
    return banks[i % len(banks)]


def load_x(k):
    xin = k.din('xT', [D, NT], F32)
    k.xT = v3(k.f32(DC * NT), DC)
    for dc in range(DC):
        k.dma(k.xT[:, dc, :], xin[dc * 128:(dc + 1) * 128, :], writes=[('x', dc, c) for c in range(NCH)])


def store_x(k):
    xo = k.dout('xT_out', [D, NT], F32)
    for dc in range(DC):
        k.dma(xo[dc * 128:(dc + 1) * 128, :], k.xT[:, dc, :], reads=[('x', dc, c) for c in range(NCH)],
              writes=[('xo', dc)])


def load_vec(k, name, n):
    t = k.f32(n)
    k.dma(t, k.din(name, [128, n], F32)[:, :], writes=[name])
    return t


def rmsnorm(k, gname, hn):
    g = load_vec(k, gname, DC)
    sq = [v3(k.b16(DC * CH), DC) for _ in range(2)]
    rs = [k.f32(CH) for _ in range(2)]
    for c in range(NCH):
        s = c % 2
        cs = slice(c * CH, (c + 1) * CH)
        k.act(sq[s], k.xT[:, :, cs], AF.Square, reads=[('x', dc, c) for dc in range(DC)], writes=[('sq', s)])
        pb = 6 + s
        for dc in range(DC):
            k.mm(k.ps[pb][:, :], k.onesD, sq[s][:, dc, :], start=(dc == 0), stop=(dc == DC - 1),
                 reads=[('sq', s), 'cst'], writes=[('ps', pb)])
        k.act(rs[s], k.ps[pb][:, :], AF.Sqrt, bias=EPS, reads=[('ps', pb)], writes=[('rs', s)])
        k.recip(rs[s], rs[s], reads=[('rs', s)], writes=[('rs', s)])
        for dc in range(DC):
            k.stt(hn[:, dc, cs], k.xT[:, dc, cs], g[:, dc:dc + 1], rs[s], ALU.mult, ALU.mult,
                  reads=[('x', dc, c), ('rs', s), gname], writes=[('hn', dc, c)])


def proj_fm(k, wname, blocks, hn, consumer, banks=(0, 1, 2, 3)):
    wv = k.dram[wname].rearrange("(dc p) m -> p dc m", p=128)
    slots = [v3(k.b16(DC * 512), DC) for _ in range(2)]
    for bi, (c0, ncol) in enumerate(blocks):
        s = bi % 2
        k.dma(slots[s][:, :, 0:ncol], wv[:, :, c0:c0 + ncol], writes=[('wblk', wname, s)], eng='pool')
        for c in range(NCH):
            for mi in range((ncol + 127) // 128):
                mw = min(128, ncol - mi * 128)
                pb = nextbank(k, 'proj', banks)
                for dc in range(DC):
                    k.mm(k.ps[pb][0:mw, :], slots[s][:, dc, mi * 128:mi * 128 + mw], hn[:, dc, c * CH:(c + 1) * CH],
                         start=(dc == 0), stop=(dc == DC - 1),
                         reads=[('wblk', wname, s), ('hn', dc, c)], writes=[('ps', pb)])
                consumer(c0 + mi * 128, c, pb)


def proj_tm(k, wname, c0, ncol, hn, consumer, banks=(0, 1, 2, 3)):
    wv = k.dram[wname].rearrange("(dc p) m -> p dc m", p=128)
    wt = v3(k.b16(DC * ncol), DC)
    kk = k.key('wtm')
    k.dma(wt, wv[:, :, c0:c0 + ncol], writes=[kk], eng='pool')
    for tt in range(NT // 128):
        c = tt // 4
        pb = nextbank(k, 'proj', banks)
        for dc in range(DC):
            k.mm(k.ps[pb][:, 0:ncol], hn[:, dc, tt * 128:(tt + 1) * 128], wt[:, dc, :],
                 start=(dc == 0), stop=(dc == DC - 1), reads=[kk, ('hn', dc, c)], writes=[('ps', pb)])
        consumer(tt, pb)


def out_proj_residual(k, wname, ob, okeyf):
    wv = k.dram[wname].rearrange("(dc p) m -> p dc m", p=128)
    wt = v3(k.b16(DC * D), DC)
    kk = k.key('wout')
    k.dma(wt, wv, writes=[kk], eng='pool')
    for c in range(NCH):
        cs = slice(c * CH, (c + 1) * CH)
        for dco in range(DC):
            pb = nextbank(k, 'op', (4, 5))
            for h in range(DC):
                k.mm(k.ps[pb][:, :], wt[:, h, dco * 128:(dco + 1) * 128], ob[:, h, cs],
                     start=(h == 0), stop=(h == DC - 1), reads=[kk] + okeyf(h, c), writes=[('ps', pb)])
            k.tt(k.xT[:, dco, cs], k.xT[:, dco, cs], k.ps[pb][:, :], ALU.add,
                 reads=[('ps', pb), ('x', dco, c)], writes=[('x', dco, c)])


def mlp(k, li):
    m = k.mark()
    hn = v3(k.b16(DC * NT), DC)
    rmsnorm(k, 'g_mlp%d' % li, hn)
    wu = k.din('w_up%d' % li, [D, DFF], F32).rearrange("(dc p) m -> p dc m", p=128)
    wd = k.din('w_dn%d' % li, [DFF, D], F32).rearrange("(fc p) m -> p fc m", p=128)
    ups = [v3(k.b16(DC * 512), DC) for _ in range(2)]
    dns = [v3(k.b16(4 * D), 4) for _ in range(2)]
    rl = [k.b16(CH) for _ in range(2)]
    h1 = [v3(k.b16(4 * CH), 4) for _ in range(2)]
    it = 0
    for fb in range(DFF // 512):
        s = fb % 2
        k.dma(ups[s], wu[:, :, fb * 512:(fb + 1) * 512], writes=[('wup', s)], eng='pool')
        k.dma(dns[s], wd[:, fb * 4:(fb + 1) * 4, :], writes=[('wdn', s)], eng='pool')
        for c in range(NCH):
            cs = slice(c * CH, (c + 1) * CH)
            hs = it % 2
            it += 1
            for fc in range(4):
                pb = nextbank(k, 'mlpu', (0, 1))
                for dc in range(DC):
                    k.mm(k.ps[pb][:, :], ups[s][:, dc, fc * 128:(fc + 1) * 128], hn[:, dc, cs],
                         start=(dc == 0), stop=(dc == DC - 1), reads=[('wup', s), ('hn', dc, c)], writes=[('ps', pb)])
                r = nextbank(k, 'rl', (0, 1))
                k.act(rl[r], k.ps[pb][:, :], AF.Relu, reads=[('ps', pb)], writes=[('rl', r)])
                k.tt(h1[hs][:, fc, :], rl[r], rl[r], ALU.mult, reads=[('rl', r)], writes=[('h1', hs, fc)], eng='pool')
            for dco in range(DC):
                pb = nextbank(k, 'mlpd', (2, 3))
                for fc in range(4):
                    k.mm(k.ps[pb][:, :], dns[s][:, fc, dco * 128:(dco + 1) * 128], h1[hs][:, fc, :],
                         start=(fc == 0), stop=(fc == 3), reads=[('wdn', s), ('h1', hs, fc)], writes=[('ps', pb)])
                k.tt(k.xT[:, dco, cs], k.xT[:, dco, cs], k.ps[pb][:, :], ALU.add,
                     reads=[('ps', pb), ('x', dco, c)], writes=[('x', dco, c)])
    k.P.barrier()
    k.reset(m)


def normrope_setup(k):
    k.cosT = k.f32(NT)
    k.sinT = k.f32(NT)
    k.dma(k.cosT, k.din('cosT', [128, NT], F32)[:, :], writes=['cos'])
    k.dma(k.sinT, k.din('sinT', [128, NT], F32)[:, :], writes=['sin'])
    k.nr_xg = [k.b16(CH) for _ in range(2)]
    k.nr_sq = [k.b16(CH) for _ in range(2)]
    k.nr_rs = [k.f32(CH) for _ in range(2)]
    k.nr_t1 = [k.f32(CH) for _ in range(2)]
    k.nr_t2 = [k.f32(CH) for _ in range(2)]


def normrope(k, pb, c, gain, gkey, out, okeys):
    s = nextbank(k, 'nr', (0, 1))
    cs = slice(c * CH, (c + 1) * CH)
    xg, sq, rs, t1, t2 = k.nr_xg[s], k.nr_sq[s], k.nr_rs[s], k.nr_t1[s], k.nr_t2[s]
    k.act(xg, k.ps[pb][:, :], AF.Identity, scale=gain, reads=[('ps', pb), gkey], writes=[('nrxg', s)])
    k.act(sq, k.ps[pb][:, :], AF.Square, reads=[('ps', pb)], writes=[('nrsq', s)])
    b1 = nextbank(k, 'nrss', (4, 5))
    b2 = nextbank(k, 'nrrot', (6, 7))
    k.mm(k.ps[b1][:, :], k.onesH, sq, reads=[('nrsq', s), 'cst'], writes=[('ps', b1)])
    k.mm(k.ps[b2][:, :], k.Rm, xg, reads=[('nrxg', s), 'cst'], writes=[('ps', b2)])
    k.act(rs, k.ps[b1][:, :], AF.Sqrt, bias=EPS, reads=[('ps', b1)], writes=[('nrrs', s)])
    k.recip(rs, rs, reads=[('nrrs', s)], writes=[('nrrs', s)])
    k.tt(t1, xg, k.cosT[:, cs], ALU.mult, reads=[('nrxg', s), 'cos'], writes=[('nrt1', s)])
    k.tt(t2, k.ps[b2][:, :], k.sinT[:, cs], ALU.mult, reads=[('ps', b2), 'sin'], writes=[('nrt2', s)])
    k.tt(t1, t1, t2, ALU.add, reads=[('nrt1', s), ('nrt2', s)], writes=[('nrt1', s)])
    k.tt(out, t1, rs, ALU.mult, reads=[('nrt1', s), ('nrrs', s)], writes=okeys)


def qkv_pre(k, li, moba):
    m = k.mark()
    hn = v3(k.b16(DC * NT), DC)
    rmsnorm(k, 'g_an%d' % li, hn)
    k.din('w_in%d' % li, [D, 3 * D], F32)
    qo = k.dout('qT', [D, NT], BF16)
    ko = k.dout('kT', [D, NT], BF16)
    vo = k.dout('v', [NT, D], BF16)
    stage = [k.b16(CH) for _ in range(3)]
    if moba:
        normrope_setup(k)
        gq = load_vec(k, 'moba_qn', 1)
        gk = load_vec(k, 'moba_kn', 1)
        kmo = k.dout('kmean', [D, 2 * NCH], F32)
        kms = k.f32(H * 2 * NCH)
        kmf = [k.f32(CH) for _ in range(2)]

    def cons(col0, c, pb):
        s = nextbank(k, 'stg', (0, 1, 2))
        cs = slice(c * CH, (c + 1) * CH)
        isq = col0 < D
        m_ = (col0 % D) // 128
        dst = (qo if isq else ko)[m_ * 128:(m_ + 1) * 128, cs]
        if not moba:
            k.act(stage[s], k.ps[pb][:, :], AF.Copy, scale=(SCALE if isq else 1.0), reads=[('ps', pb)], writes=[('stg', s)])
        else:
            normrope(k, pb, c, gq[:, 0:1] if isq else gk[:, 0:1], 'moba_qn' if isq else 'moba_kn', stage[s], [('stg', s)])
            if not isq:
                f = nextbank(k, 'kmf', (0, 1))
                k.copy(kmf[f], stage[s], reads=[('stg', s)], writes=[('kmf', f)])
                for b2 in range(2):
                    col = m_ * 2 * NCH + c * 2 + b2
                    k.P.op('dve', lambda e, o=kms[:, col:col + 1], i=kmf[f][:, b2 * 256:(b2 + 1) * 256]:
                           e.reduce_sum(out=o, in_=i, axis=mybir.AxisListType.X), [('kmf', f)], [('kms', col)])
        k.dma(dst, stage[s], reads=[('stg', s)], writes=[k.key('o')])

    proj_fm(k, 'w_in%d' % li, [(0, 512), (512, 512), (1024, 512), (1536, 512)], hn, cons)
    vst = [k.b16(512) for _ in range(2)]
    for half in range(2):
        def consv(tt, pb, half=half):
            s = nextbank(k, 'vst', (0, 1))
            k.copy(vst[s], k.ps[pb][:, :], reads=[('ps', pb)], writes=[('vst', s)])
            k.dma(vo[tt * 128:(tt + 1) * 128, half * 512:(half + 1) * 512], vst[s], reads=[('vst', s)], writes=[k.key('o')])
        proj_tm(k, 'w_in%d' % li, 2 * D + half * 512, 512, hn, consv)
    if moba:
        k.ts(kms, kms, 1.0 / 256, None, ALU.mult, reads=[('kms', i) for i in range(H * 2 * NCH)], writes=['kmsall'])
        for h in range(H):
            k.dma(kmo[h * 128:(h + 1) * 128, :], kms[:, h * 2 * NCH:(h + 1) * 2 * NCH], reads=['kmsall'], writes=[k.key('o')])
    k.P.barrier()
    k.reset(m)


def conv_pre(k):
    m = k.mark()
    hn = v3(k.b16(DC * NT), DC)
    rmsnorm(k, 'g_an2', hn)
    wv = k.din('conv_w_in', [D, 2 * D], F32).rearrange("(dc p) m -> p dc m", p=128)
    uo = k.dout('uT', [D, NT], BF16)
    wa = [v3(k.b16(DC * 512), DC) for _ in range(2)]
    wb = [v3(k.b16(DC * 512), DC) for _ in range(2)]
    sg = [k.f32(CH) for _ in range(2)]
    us = [k.b16(CH) for _ in range(3)]
    for pi in range(2):
        k.dma(wa[pi], wv[:, :, pi * 512:(pi + 1) * 512], writes=[('wa', pi)], eng='pool')
        k.dma(wb[pi], wv[:, :, D + pi * 512:D + (pi + 1) * 512], writes=[('wb', pi)], eng='pool')
        for c in range(NCH):
            cs = slice(c * CH, (c + 1) * CH)
            for mi in range(4):
                dc = pi * 4 + mi
                pa = nextbank(k, 'cva', (0, 1))
                pbb = nextbank(k, 'cvb', (2, 3))
                for d2 in range(DC):
                    k.mm(k.ps[pa][:, :], wa[pi][:, d2, mi * 128:(mi + 1) * 128], hn[:, d2, cs], start=(d2 == 0), stop=(d2 == DC - 1),
                         reads=[('wa', pi), ('hn', d2, c)], writes=[('ps', pa)])
                for d2 in range(DC):
                    k.mm(k.ps[pbb][:, :], wb[pi][:, d2, mi * 128:(mi + 1) * 128], hn[:, d2, cs], start=(d2 == 0), stop=(d2 == DC - 1),
                         reads=[('wb', pi), ('hn', d2, c)], writes=[('ps', pbb)])
                s = nextbank(k, 'sg', (0, 1))
                u = nextbank(k, 'us', (0, 1, 2))
                k.act(sg[s], k.ps[pbb][:, :], AF.Sigmoid, reads=[('ps', pbb)], writes=[('sg', s)])
                k.tt(us[u], k.ps[pa][:, :], sg[s], ALU.mult, reads=[('ps', pa), ('sg', s)], writes=[('us', u)])
                k.dma(uo[dc * 128:(dc + 1) * 128, cs], us[u], reads=[('us', u)], writes=[k.key('o')])
    k.P.barrier()
    k.reset(m)


def conv_core(k):
    m = k.mark()
    HW = 32 + CH
    uin = k.din('uE', [D, NCH, HW], BF16)
    ue = v3(k.b16(DC * NCH * HW), DC * NCH)
    for dc in range(DC):
        k.dma(ue[:, dc * NCH:(dc + 1) * NCH, :], uin[dc * 128:(dc + 1) * 128, :, :], writes=[('ue', dc)])
    wdw = v3(load_vec(k, 'conv_dw', DC * 31), DC)
    bdw = load_vec(k, 'conv_dwb', DC)
    lg = load_vec(k, 'conv_lng', DC)
    lb = load_vec(k, 'conv_lnb', DC)
    ucb = v3(k.b16(DC * NT), DC)
    Dg = [v3(k.b16(31 * 128), 31) for _ in range(2)]
    for dc in range(DC):
        s = dc % 2
        for kk in range(31):
            k.ts(Dg[s][:, kk, :], k.ident, wdw[:, dc, kk:kk + 1], None, ALU.mult, reads=['cst', 'conv_dw'], writes=[('dg', s, kk)])
        for c in range(NCH):
            pb = nextbank(k, 'cv', (0, 1, 2))
            for kk in range(31):
                k.mm(k.ps[pb][:, :], Dg[s][:, kk, :], ue[:, dc * NCH + c, 2 + kk:2 + kk + CH], start=(kk == 0), stop=(kk == 30),
                     reads=[('dg', s, kk), ('ue', dc)], writes=[('ps', pb)])
            k.act(ucb[:, dc, c * CH:(c + 1) * CH], k.ps[pb][:, :], AF.Identity, bias=bdw[:, dc:dc + 1],
                  reads=[('ps', pb), 'conv_dwb'], writes=[('ucb', dc, c)])
    sq = [v3(k.b16(DC * CH), DC) for _ in range(2)]
    mu = [k.f32(CH) for _ in range(2)]
    va = [k.f32(CH) for _ in range(2)]
    nm = [k.f32(CH) for _ in range(2)]
    tn = [k.f32(CH) for _ in range(2)]
    for c in range(NCH):
        s = c % 2
        cs = slice(c * CH, (c + 1) * CH)
        k.tt(sq[s], ucb[:, :, cs], ucb[:, :, cs], ALU.mult, reads=[('ucb', dc, c) for dc in range(DC)], writes=[('csq', s)], eng='pool')
        for dc in range(DC):
            k.mm(k.ps[6][:, :], k.onesD, ucb[:, dc, cs], start=(dc == 0), stop=(dc == DC - 1), reads=[('ucb', dc, c), 'cst'], writes=[('ps', 6)])
        for dc in range(DC):
            k.mm(k.ps[7][:, :], k.onesD, sq[s][:, dc, :], start=(dc == 0), stop=(dc == DC - 1), reads=[('csq', s), 'cst'], writes=[('ps', 7)])
        k.copy(mu[s], k.ps[6][:, :], reads=[('ps', 6)], writes=[('mu', s)])
        k.tt(va[s], mu[s], mu[s], ALU.mult, reads=[('mu', s)], writes=[('va', s)])
        k.tt(va[s], k.ps[7][:, :], va[s], ALU.subtract, reads=[('ps', 7), ('va', s)], writes=[('va', s)])
        k.act(va[s], va[s], AF.Sqrt, bias=EPS, reads=[('va', s)], writes=[('va', s)])
        k.recip(va[s], va[s], reads=[('va', s)], writes=[('va', s)])
        k.stt(nm[s], mu[s], -1.0, va[s], ALU.mult, ALU.mult, reads=[('mu', s), ('va', s)], writes=[('nm', s)])
        for dc in range(DC):
            t = nextbank(k, 'tn', (0, 1))
            k.tt(tn[t], ucb[:, dc, cs], va[s], ALU.mult, reads=[('ucb', dc, c), ('va', s)], writes=[('tn', t)])
            k.tt(tn[t], tn[t], nm[s], ALU.add, reads=[('tn', t), ('nm', s)], writes=[('tn', t)])
            k.act(ucb[:, dc, cs], tn[t], AF.Silu, scale=lg[:, dc:dc + 1], bias=lb[:, dc:dc + 1],
                  reads=[('tn', t), 'conv_lng', 'conv_lnb'], writes=[('ucb', dc, c)])
    return m, ucb


def load_q(k):
    qin = k.din('qT', [D, NT], BF16)
    q = v3(k.b16(H * NT), H)
    for h in range(H):
        k.dma(q[:, h, :], qin[h * 128:(h + 1) * 128, :], writes=[('q', h, c) for c in range(NCH)])
    return q


def kv_loader(k, ktname='kT_full', vname='v_full'):
    cfg = k.cfg
    kin = k.din(ktname, [D, cfg.S], BF16)
    vin = k.din(vname, [H, 128, cfg.NKT, 128], BF16)
    kb = [k.b16(cfg.S) for _ in range(2)]
    vb = [v3(k.b16(cfg.NKT * 128), cfg.NKT) for _ in range(2)]
    state = {'i': 0}

    def load(h, nkt):
        s = state['i'] % 2
        state['i'] += 1
        k.dma(kb[s][:, 0:nkt * 128], kin[h * 128:(h + 1) * 128, 0:nkt * 128], writes=[('kb', s)])
        k.dma(vb[s][:, 0:nkt, :], vin[h, :, 0:nkt, :], writes=[('vb', s)])
        return s
    return kb, vb, load


def sb_core(k):
    cfg = k.cfg
    R = cfg.R
    m = k.mark()
    q = load_q(k)
    mq = k.mark()
    kb, vb, kvload = kv_loader(k)
    NU = 4 * R
    mkin = k.din('sbmask', [128, NCH, NU, CH], BF16)
    mk = v3(k.b16(NU * CH), NU)
    cc = k.din('sbc', [128, 256], BF16)
    cst = k.b16(256)
    k.dma(cst, cc[:, :], writes=['sbc'])
    negtri = cst[:, 0:128]
    NCHAIN = 2
    ef = [[k.f32(CH) for _ in range(2)] for _ in range(NCHAIN)]
    sp = [[k.b16(CH) for _ in range(2)] for _ in range(NCHAIN)]
    at = [[k.b16(CH) for _ in range(2)] for _ in range(NCHAIN)]
    crt = [k.b16(CH, parts=33) for _ in range(2)]
    ZB = [(0, 1), (2, 3)]
    OBK = [5, 6]
    CP = [0, 32]
    for lc in range(NCH):
        KM = cfg.KM[lc]
        k.dma(mk, mkin[:, lc, :, :], writes=['mk'])
        cs = slice(lc * CH, (lc + 1) * CH)
        tiles = list(range(KM - 1, -1, -1))
        n = len(tiles)
        for hp in range(H // NCHAIN):
            hs = [hp * NCHAIN + ci for ci in range(NCHAIN)]
            sl = [kvload(h, KM) for h in hs]
            zb = [[None] * n for _ in range(NCHAIN)]

            def stA(ci, i):
                h, s = hs[ci], sl[ci]
                kt = tiles[i]
                zb[ci][i] = ZB[ci][i % 2]
                z = zb[ci][i]
                k.mm(k.ps[z][:, :], kb[s][:, kt * 128:(kt + 1) * 128], q[:, h, cs],
                     reads=[('kb', s), ('q', h, lc)], writes=[('ps', z)])
                e = i % 2
                k.act(ef[ci][e], k.ps[z][:, :], AF.Exp, reads=[('ps', z)], writes=[('ef', ci, e)])
                k.act(sp[ci][e], ef[ci][e], AF.Ln, bias=1.0, reads=[('ef', ci, e)], writes=[('sp', ci, e)])
                u = kt - 4 * cfg.GMIN[lc]
                if u >= 0:
                    k.tt(sp[ci][e], sp[ci][e], mk[:, u, :], ALU.mult, reads=[('sp', ci, e), 'mk'], writes=[('sp', ci, e)])

            def stB(ci, i):
                z = zb[ci][i]
                e = i % 2
                p0 = CP[ci]
                crow = k.ps[4][p0:p0 + 1, :]
                k.mm(k.ps[z][:, :], negtri, sp[ci][e], start=False, stop=True, skip=True,
                     reads=[('sp', ci, e), 'sbc', ('ps', z)], writes=[('ps', z)])
                if i > 0:
                    c_ = (i - 1) % 2
                    k.mm(k.ps[z][:, :], cst[p0:p0 + 1, 128:256], crt[c_][p0:p0 + 1, :], start=False, stop=True, skip=True,
                         reads=[('cr', ci, c_), 'sbc', ('ps', z)], writes=[('ps', z)])
                k.mm(crow, k.ones[:, 0:1], sp[ci][e], start=(i == 0), stop=True, skip=(i > 0),
                     reads=[('sp', ci, e), 'cst'], writes=[('psc', ci)])
                if i < n - 1:
                    k.copy(crt[i % 2][p0:p0 + 1, :], crow, reads=[('psc', ci)], writes=[('cr', ci, i % 2)])

            def stC(ci, i):
                h, s = hs[ci], sl[ci]
                kt = tiles[i]
                z = zb[ci][i]
                e = i % 2
                a = at[ci][e]
                k.act(a, k.ps[z][:, :], AF.Exp, reads=[('ps', z)], writes=[('at', ci, e)])
                u = kt - 4 * cfg.GMIN[lc]
                if u >= 0:
                    k.tt(a, a, mk[:, u, :], ALU.mult, reads=[('at', ci, e), 'mk'], writes=[('at', ci, e)])
                k.mm(k.ps[OBK[ci]][:, :], vb[s][:, kt, :], a, start=(i == 0), stop=(i == n - 1),
                     reads=[('vb', s), ('at', ci, e)], writes=[('ps', OBK[ci])])

            for i in range(n + 1):
                for ci in range(NCHAIN):
                    if i < n:
                        stA(ci, i)
                for ci in range(NCHAIN):
                    if 0 <= i - 1 < n:
                        stB(ci, i - 1)
                for ci in range(NCHAIN):
                    if 0 <= i - 1 < n:
                        stC(ci, i - 1)
            for ci in range(NCHAIN):
                k.act(q[:, hs[ci], cs], k.ps[OBK[ci]][:, :], AF.Copy, reads=[('ps', OBK[ci])], writes=[('q', hs[ci], lc)])
    return m, mq, q


def moba_core(k):
    cfg = k.cfg
    NB = cfg.S // 256
    m = k.mark()
    q = load_q(k)
    mq = k.mark()
    kb, vb, kvload = kv_loader(k)
    kown = k.din('kT_own', [D, NT], BF16)
    vown = k.din('v_own', [NT, D], BF16)
    kmin = k.din('kmean_full', [D, NB], F32)
    kmf = v3(k.f32(H * NB), H)
    kmb = v3(k.b16(H * NB), H)
    for h in range(H):
        k.dma(kmf[:, h, :], kmin[h * 128:(h + 1) * 128, :], writes=[('kmf', h)])
    k.copy(kmb, kmf, reads=[('kmf', h) for h in range(H)], writes=['kmb'])
    mvin = k.din('mvalid', [128, NCH, 4 * NB], F32)
    mv = v3(k.f32(NCH * 4 * NB), NCH)
    k.dma(mv, mvin[:, :, :], writes=['mv'])
    esin = k.din('esel', [NB, NB * 128], BF16)
    esel = k.b16(NB * 128, parts=NB)
    k.dma(esel, esin[:, :], writes=['esel'])
    omin = k.din('ownmask', [128, 4, CH], BF16)
    om = v3(k.b16(4 * CH), 4)
    k.dma(om, omin[:, :, :], writes=['om'])
    identf = k.f32(128)
    k.copy(identf, k.ident, reads=['cst'], writes=['identf'])
    gm = k.f32(4 * NB)
    m8 = k.f32(32)
    t1 = k.f32(4 * NB)
    mb = k.f32(4 * NB)
    mT = [k.b16(CH, parts=NB) for _ in range(2)]
    pt = [k.b16(CH) for _ in range(3)]
    kl = [k.b16(CH) for _ in range(2)]
    vl = [v3(k.b16(4 * 128), 4) for _ in range(2)]
    rd = k.f32(CH)
    dacc = k.f32(CH)
    dhi = k.b16(CH)
    dlo = k.b16(CH)
    it = 0
    for lc in range(NCH):
        KM = cfg.KM[lc]
        cs = slice(lc * CH, (lc + 1) * CH)
        for h in range(H):
            s = kvload(h, KM)
            l = it % 2
            k.dma(kl[l], kown[h * 128:(h + 1) * 128, cs], writes=[('kl', l)])
            k.dma(vl[l], vown[cs, h * 128:(h + 1) * 128].rearrange("(a p) d -> p a d", p=128), writes=[('vl', l)])
            ob = 4 + (it % 2)
            db = 6 + (it % 2)
            it += 1
            for u in range(4):
                k.mm(k.ps[3][:, u * NB:(u + 1) * NB], q[:, h, lc * CH + u * 128:lc * CH + (u + 1) * 128], kmb[:, h, :],
                     start=(u == 0), stop=True, skip=(u > 0), reads=[('q', h, lc), 'kmb'], writes=[('ps', 3)])
            k.tt(gm, k.ps[3][:, 0:4 * NB], mv[:, lc, :], ALU.add, reads=[('ps', 3), 'mv'], writes=['gm'])
            for u in range(4):
                k.P.op('dve', lambda e, o=m8[:, u * 8:(u + 1) * 8], i=gm[:, u * NB:(u + 1) * NB]: e.max(out=o, in_=i), ['gm'], [('m8', u)])
            thr = v3(m8, 4)[:, :, 2:3]
            k.ts(thr, thr, -1e29, None, ALU.max, reads=[('m8', u) for u in range(4)], writes=['thr'])
            for u in range(4):
                k.ts(t1[:, u * NB:(u + 1) * NB], gm[:, u * NB:(u + 1) * NB], m8[:, u * 8 + 2:u * 8 + 3], None, ALU.is_ge,
                     reads=['gm', 'thr'], writes=[('t1', u)])
            k.ts(mb, t1, -NEG, NEG, ALU.mult, ALU.add, reads=[('t1', u) for u in range(4)], writes=['mb'])
            for u in range(4):
                k.P.op('pe', lambda e, o=k.ps[2][0:NB, u * 128:(u + 1) * 128], i=mb[:, u * NB:(u + 1) * NB]:
                       e.transpose(o, i, identf), ['mb', 'identf'], [('ps', 2)])
            mt = mT[it % 2]
            mk_ = ('mT', it % 2)
            k.copy(mt, k.ps[2][0:NB, :], reads=[('ps', 2)], writes=[mk_])
            tiles = []
            for kt in range(KM):
                nb_ = kt // 2
                tiles.append((kb[s][:, kt * 128:(kt + 1) * 128], ('kb', s), esel[:, nb_ * 128:(nb_ + 1) * 128], mt, ['esel', mk_],
                              vb[s][:, kt, :], ('vb', s)))
            for i2 in range(4):
                tiles.append((kl[l][:, i2 * 128:(i2 + 1) * 128], ('kl', l), k.ident, om[:, i2, :], ['cst', 'om'],
                              vl[l][:, i2, :], ('vl', l)))
            nt_ = len(tiles)
            pend = None
            for i in range(nt_ + 1):
                if i < nt_:
                    kT_, kkey, ml, mr, mkeys, vT_, vkey = tiles[i]
                    zb = nextbank(k, 'z', (0, 1))
                    k.mm(k.ps[zb][:, :], kT_, q[:, h, cs], start=True, stop=False, reads=[kkey, ('q', h, lc)], writes=[('ps', zb)])
                    k.mm(k.ps[zb][:, :], ml, mr, start=False, stop=True, reads=mkeys, writes=[('ps', zb)])
                    p = nextbank(k, 'pt', (0, 1, 2))
                    k.act(pt[p], k.ps[zb][:, :], AF.Exp, scale=SCALE, reads=[('ps', zb)], writes=[('pt', p)])
                    cur = (i, p, vT_, vkey)
                else:
                    cur = None
                if pend is not None:
                    pi, pp_, pv_, pvk = pend
                    k.mm(k.ps[ob][:, :], pv_, pt[pp_], start=(pi == 0), stop=(pi == nt_ - 1), reads=[pvk, ('pt', pp_)], writes=[('ps', ob)])
                    if pi == 0:
                        k.copy(dacc, pt[pp_], reads=[('pt', pp_)], writes=['dacc'])
                    else:
                        k.tt(dacc, dacc, pt[pp_], ALU.add, reads=[('pt', pp_), 'dacc'], writes=['dacc'])
                pend = cur
            k.copy(dhi, dacc, reads=['dacc'], writes=['dhi'])
            k.tt(dlo, dacc, dhi, ALU.subtract, reads=['dacc', 'dhi'], writes=['dlo'])
            k.mm(k.ps[db][:, :], k.ones, dhi, start=True, stop=False, reads=['cst', 'dhi'], writes=[('ps', db)])
            k.mm(k.ps[db][:, :], k.ones, dlo, start=False, stop=True, reads=['cst', 'dlo'], writes=[('ps', db)])
            k.recip(rd, k.ps[db][:, :], reads=[('ps', db)], writes=['rd'])
            k.tt(q[:, h, cs], k.ps[ob][:, :], rd, ALU.mult, reads=[('ps', ob), 'rd'], writes=[('q', h, lc)])
    return m, mq, q


NSA_IN = 2584


def nsa_pre(k):
    m = k.mark()
    hn = v3(k.b16(DC * NT), DC)
    rmsnorm(k, 'g_an0', hn)
    k.din('w_in0', [D, NSA_IN], F32)
    qo = k.dout('qT', [D, NT], BF16)
    fm_out = {1024: k.dout('kcT', [256, NT], BF16), 1280: k.dout('vcT', [256, NT], BF16),
              1536: k.dout('ksT', [256, NT], BF16), 2048: k.dout('kwT', [256, NT], BF16)}
    vso = k.dout('vs', [NT, 256], BF16)
    vwo = k.dout('vw', [NT, 256], BF16)
    go = k.dout('gT', [24, NT], F32)
    normrope_setup(k)
    gq = load_vec(k, 'nsa_qn', 1)
    gk = load_vec(k, 'nsa_kn', 3)
    stage = [k.b16(CH) for _ in range(3)]
    gst = [k.f32(CH, parts=24) for _ in range(2)]

    def cons(col0, c, pb):
        cs = slice(c * CH, (c + 1) * CH)
        if col0 == 2560:
            s = nextbank(k, 'gst', (0, 1))
            k.act(gst[s], k.ps[pb][0:24, :], AF.Sigmoid, reads=[('ps', pb)], writes=[('gst', s)])
            k.dma(go[:, cs], gst[s], reads=[('gst', s)], writes=[k.key('o')])
            return
        s = nextbank(k, 'stg', (0, 1, 2))
        if col0 < 1024:
            normrope(k, pb, c, gq[:, 0:1], 'nsa_qn', stage[s], [('stg', s)])
            dst = qo[col0:col0 + 128, cs]
        else:
            base = max(b for b in fm_out if b <= col0)
            dst = fm_out[base][col0 - base:col0 - base + 128, cs]
            if base == 1280:
                k.act(stage[s], k.ps[pb][:, :], AF.Copy, reads=[('ps', pb)], writes=[('stg', s)])
            else:
                gi = {1024: 0, 1536: 1, 2048: 2}[base]
                normrope(k, pb, c, gk[:, gi:gi + 1], 'nsa_kn', stage[s], [('stg', s)])
        k.dma(dst, stage[s], reads=[('stg', s)], writes=[k.key('o')])

    proj_fm(k, 'w_in0', [(0, 512), (512, 512), (1024, 512), (1536, 256), (2048, 256), (2560, 24)], hn, cons)
    vst = [k.b16(256) for _ in range(2)]
    for (c0, dst) in ((1792, vso), (2304, vwo)):
        def consv(tt, pb, dst=dst):
            s = nextbank(k, 'vst', (0, 1))
            k.copy(vst[s], k.ps[pb][:, 0:256], reads=[('ps', pb)], writes=[('vst', s)])
            k.dma(dst[tt * 128:(tt + 1) * 128, :], vst[s], reads=[('vst', s)], writes=[k.key('o')])
        proj_tm(k, 'w_in0', c0, 256, hn, consv)
    k.P.barrier()
    k.reset(m)


def nsa_core(k, ob):
    cfg = k.cfg
    S = cfg.S
    NSEL = S // 64
    NCMP = S // 16 - 1
    NCC = (NCMP + 127) // 128
    NQT = NT // 128
    m = k.mark()
    qin = k.din('qT', [D, NT], BF16)
    qsb = k.b16(NQT * H * 128).rearrange("p (a b c) -> p a b c", a=NQT, b=H)
    for h in range(H):
        k.dma(qsb[:, :, h, :], qin[h * 128:(h + 1) * 128, :].rearrange("p (a t) -> p a t", t=128), writes=[('q', h)])
    gin = k.din('gT', [24, NT], F32)
    gb = k.b16(NT, parts=24)
    k.dma(gb, gin[:, :], writes=['gb'], eng='pool')
    selg = k.b16(24 * 128, parts=24)
    k.dma(selg, k.din('selg', [24, 24 * 128], BF16)[:, :], writes=['selg'])
    G = k.b16(S, parts=NSEL)
    k.dma(G, k.din('G', [NSEL, S], BF16)[:, :], writes=['G'])
    Ov = v3(k.b16(NCC * NSEL), NCC)
    k.dma(Ov, k.din('Ov', [128, NCC, NSEL], BF16)[:, :, :], writes=['Ov'])
    tri = k.b16(256)
    k.dma(tri, k.din('tri', [128, 256], BF16)[:, :], writes=['tri'])
    triT = tri[:, 0:128]
    tri2 = tri[:, 128:256]
    McT = k.b16(NQT * NCC * 128).rearrange("p (a b c) -> p a b c", a=NQT, b=NCC)
    k.dma(McT, k.din('McT', [128, NQT, NCC, 128], BF16)[:, :, :, :], writes=['McT'])
    val = v3(k.b16(NQT * NSEL), NQT)
    bon = v3(k.b16(NQT * NSEL), NQT)
    pneg = v3(k.b16(NQT * NSEL), NQT)
    k.dma(val, k.din('sval', [128, NQT, NSEL], BF16)[:, :, :], writes=['val'])
    k.dma(bon, k.din('sbon', [128, NQT, NSEL], BF16)[:, :, :], writes=['bon'])
    k.dma(pneg, k.din('spneg', [128, NQT, NSEL], BF16)[:, :, :], writes=['pneg'])
    pcr = v3(k.b16(NCH * 128, parts=1), NCH)
    k.dma(pcr, k.din('pcrow', [1, NCH, 128], BF16)[:, :, :], writes=['pcr'])
    identf = k.f32(128)
    k.copy(identf, k.ident, reads=['cst'], writes=['identf'])
    posT = v3(load_vec(k, 'cmp_posT', 2 * 32), 2)
    kcmpT = [k.b16(NCC * 128) for _ in range(2)]
    vcmp = [v3(k.b16(NCC * 128), NCC) for _ in range(2)]
    onescol = k.ones[:, 0:1]
    ones1 = k.ones[0:1, :]
    m2 = k.mark()
    rawin = [k.din('kcT_full', [256, S], BF16), k.din('vcT_full', [256, S], BF16)]
    wcin = k.din('w_cmp', [2, 32, 128, 128], F32)
    raw = [k.b16(S) for _ in range(2)]
    wc = [v3(k.b16(32 * 128), 32) for _ in range(2)]
    tmp = [k.b16(NCC * 128) for _ in range(4)]
    for t_ in range(4):
        k.memset(tmp[t_], 0.0, writes=[('tmp', t_)])
    it = 0
    for kv in range(2):
        k.dma(wc[kv], wcin[kv].rearrange("l d e -> d l e"), writes=[('wc', kv)], eng='pool')
        for grp in range(2):
            r = it % 2
            it += 1
            k.dma(raw[r], rawin[kv][grp * 128:(grp + 1) * 128, :], writes=[('raw', r)])
            pb = 2 + r
            for l in range(32):
                t_ = nextbank(k, 'tmp', (0, 1, 2, 3))
                k.ts(tmp[t_][:, 0:NCMP], raw[r][:, l:l + 16 * (NCMP - 1) + 1:16], posT[:, kv, l:l + 1], None, ALU.add,
                     reads=[('raw', r), 'cmp_posT'], writes=[('tmp', t_)])
                if kv == 0:
                    k.mm(k.ps[pb][:, 0:NCC * 128], wc[kv][:, l, :], tmp[t_], start=(l == 0), stop=(l == 31),
                         reads=[('wc', kv), ('tmp', t_)], writes=[('ps', pb)])
                else:
                    for c in range(NCC):
                        k.mm(k.ps[pb][:, c * 128:(c + 1) * 128], tmp[t_][:, c * 128:(c + 1) * 128], wc[kv][:, l, :],
                             start=(l == 0 and c == 0), stop=True, skip=not (l == 0 and c == 0),
                             reads=[('wc', kv), ('tmp', t_)], writes=[('ps', pb)])
            if kv == 0:
                k.copy(kcmpT[grp], k.ps[pb][:, 0:NCC * 128], reads=[('ps', pb)], writes=[('kcmp', grp)])
            else:
                k.copy(vcmp[grp], v3(k.ps[pb][:, 0:NCC * 128], NCC), reads=[('ps', pb)], writes=[('vcmp', grp)])
    k.P.barrier()
    k.reset(m2)
    ksin = k.din('ksT_full', [256, S], BF16)
    vsin = k.din('vs_full', [2, 128, cfg.NKT, 128], BF16)
    kslin = k.din('ksT_own', [256, NT], BF16)
    vslin = k.din('vs_own', [NT, 256], BF16)
    kwpin = k.din('kw_pack', [256, NCH, 2 * CH], BF16)
    vwpin = k.din('vw_pack', [2, 128, NCH, 8, 128], BF16)
    ks = k.b16(S)
    vs = v3(k.b16(cfg.NKT * 128), cfg.NKT)
    ksl = k.b16(NT)
    vsl = v3(k.b16(NQT * 128), NQT)
    kwp = v3(k.b16(NCH * 2 * CH), NCH)
    vwp = k.b16(NCH * 8 * 128).rearrange("p (a b c) -> p a b c", a=NCH, b=8)
    Pc = [k.b16(CH) for _ in range(NCC)]
    pt = [k.b16(CH) for _ in range(3)]
    OB = [k.f32(CH) for _ in range(3)]
    rr_ = k.f32(CH)
    rd4 = k.f32(4)
    imp = k.f32(NSEL)
    imp2 = k.f32(NSEL)
    imp3 = k.f32(NSEL)
    m8 = k.f32(16)
    mbs = k.f32(NSEL)
    mT = k.b16(128, parts=NSEL)
    osum = k.f32(CH)
    obank = [2, 3]
    dbank = [4, 5]
    br = 0

    dacc = k.f32(CH)
    dhi = k.b16(CH)
    dlo = k.b16(CH)

    def run_tiles(tiles, Q, qk, o_b, d_b):
        n = len(tiles)
        pend = None
        for i in range(n + 1):
            cur = None
            if i < n:
                kT_, kkey, masks, vT_, vkey, dst = tiles[i]
                zb = nextbank(k, 'z', (0, 1))
                k.mm(k.ps[zb][:, :], kT_, Q, start=True, stop=True, reads=[kkey] + qk, writes=[('ps', zb)])
                for (ml, mr, mkeys, mode) in masks:
                    if mode == 'hg4':
                        for hg in range(4):
                            k.mm(k.ps[zb][:, hg * 128:(hg + 1) * 128], ml, mr, start=False, stop=True, skip=True,
                                 reads=mkeys, writes=[('ps', zb)])
                    else:
                        k.mm(k.ps[zb][:, :], ml, mr, start=False, stop=True, skip=True, reads=mkeys, writes=[('ps', zb)])
                if dst is None:
                    p = nextbank(k, 'pt', (0, 1, 2))
                    dst = (pt[p], ('pt', p))
                k.act(dst[0], k.ps[zb][:, :], AF.Exp, scale=SCALE, reads=[('ps', zb)], writes=[dst[1]])
                cur = (i, dst, vT_, vkey)
            if pend is not None:
                pi, pd, pv_, pvk = pend
                k.mm(k.ps[o_b][:, :], pv_, pd[0], start=(pi == 0), stop=(pi == n - 1), reads=[pvk, pd[1]], writes=[('ps', o_b)])
                if pi == 0:
                    k.copy(dacc, pd[0], reads=[pd[1]], writes=['dacc'])
                else:
                    k.tt(dacc, dacc, pd[0], ALU.add, reads=[pd[1], 'dacc'], writes=['dacc'])
            pend = cur
        k.copy(dhi, dacc, reads=['dacc'], writes=['dhi'])
        k.tt(dlo, dacc, dhi, ALU.subtract, reads=['dacc', 'dhi'], writes=['dlo'])
        k.mm(k.ps[d_b][:, :], k.ones, dhi, start=True, stop=False, reads=['cst', 'dhi'], writes=[('ps', d_b)])
        k.mm(k.ps[d_b][:, :], k.ones, dlo, start=False, stop=True, reads=['cst', 'dlo'], writes=[('ps', d_b)])

    def finish_branch(o_b, d_b, dst):
        k.ts(rr_, k.ps[d_b][:, :], 1e-30, None, ALU.max, reads=[('ps', d_b)], writes=['rr'])
        k.recip(rr_, rr_, reads=['rr'], writes=['rr'])
        k.tt(dst[0], k.ps[o_b][:, :], rr_, ALU.mult, reads=[('ps', o_b), 'rr'], writes=[dst[1]])

    for grp in range(2):
        gs = slice(grp * 128, (grp + 1) * 128)
        k.dma(ks, ksin[gs, :], writes=['ks'])
        k.dma(vs, vsin[grp], writes=['vs'])
        k.dma(ksl, kslin[gs, :], writes=['ksl'])
        k.dma(vsl, vslin[:, gs].rearrange("(a p) d -> p a d", p=128), writes=['vsl'])
        k.dma(kwp, kwpin[gs, :, :], writes=['kwp'])
        k.dma(vwp, vwpin[grp], writes=['vwp'])
        for qt in range(NQT):
            lc, u = qt // 4, qt % 4
            Q = qsb[:, qt, grp * 4:(grp + 1) * 4, :].rearrange("p a b -> p (a b)")
            qk = [('q', grp * 4 + hg) for hg in range(4)]
            o_b, d_b = obank[br % 2], dbank[br % 2]
            br += 1
            tiles = [(kcmpT[grp][:, c * 128:(c + 1) * 128], ('kcmp', grp), [(k.ident, McT[:, qt, c, :], ['cst', 'McT'], 'hg4')],
                      vcmp[grp][:, c, :], ('vcmp', grp), (Pc[c], ('Pc', c))) for c in range(NCC)]
            run_tiles(tiles, Q, qk, o_b, d_b)
            for hg in range(4):
                for c in range(NCC):
                    first = (hg == 0 and c == 0)
                    k.mm(k.ps[6][:, hg * NSEL:(hg + 1) * NSEL], Pc[c][:, hg * 128:(hg + 1) * 128], Ov[:, c, :],
                         start=first, stop=True, skip=not first, reads=[('Pc', c), 'Ov'], writes=[('ps', 6)])
            for hg in range(4):
                for c in range(NCC):
                    first = (hg == 0 and c == 0)
                    k.mm(k.ps[7][:, hg:hg + 1], Pc[c][:, hg * 128:(hg + 1) * 128], onescol,
                         start=first, stop=True, skip=not first, reads=[('Pc', c), 'cst'], writes=[('ps', 7)])
            k.ts(rd4, k.ps[7][:, 0:4], 1e-30, None, ALU.max, reads=[('ps', 7)], writes=['rd4'])
            k.recip(rd4, rd4, reads=['rd4'], writes=['rd4'])
            k.ts(imp, k.ps[6][:, 0:NSEL], rd4[:, 0:1], None, ALU.mult, reads=[('ps', 6), 'rd4'], writes=['imp'])
            for hg in range(1, 4):
                k.stt(imp, k.ps[6][:, hg * NSEL:(hg + 1) * NSEL], rd4[:, hg:hg + 1], imp, ALU.mult, ALU.add,
                      reads=[('ps', 6), 'rd4', 'imp'], writes=['imp'])
            k.tt(imp2, imp, val[:, qt, :], ALU.mult, reads=['imp', 'val'], writes=['imp2'])
            k.tt(imp2, imp2, bon[:, qt, :], ALU.add, reads=['imp2', 'bon'], writes=['imp2'])
            k.P.op('dve', lambda e: e.max(out=m8[:, 0:8], in_=imp2), ['imp2'], ['m8a'])
            k.P.op('dve', lambda e: e.match_replace(out=imp3, in_to_replace=m8[:, 0:8], in_values=imp2, imm_value=-1e9),
                   ['imp2', 'm8a'], ['imp3'])
            k.P.op('dve', lambda e: e.max(out=m8[:, 8:16], in_=imp3), ['imp3'], ['m8b'])
            k.ts(mbs, imp2, m8[:, 15:16], -NEG, ALU.is_ge, ALU.mult, reads=['imp2', 'm8b'], writes=['mbs'])
            k.stt(mbs, mbs, NEG, pneg[:, qt, :], ALU.add, ALU.add, reads=['mbs', 'pneg'], writes=['mbs'])
            k.P.op('pe', lambda e: e.transpose(k.ps[7][0:NSEL, 128:256], mbs, identf), ['mbs', 'identf'], [('ps', 7)])
            k.copy(mT, k.ps[7][0:NSEL, 128:256], reads=[('ps', 7)], writes=['mT'])
            finish_branch(o_b, d_b, (OB[0], ('OB', 0)))
            o_b, d_b = obank[br % 2], dbank[br % 2]
            br += 1
            KM = cfg.KM[lc]
            tiles = [(ks[:, kt * 128:(kt + 1) * 128], 'ks', [(G[:, kt * 128:(kt + 1) * 128], mT, ['G', 'mT'], 'hg4')],
                      vs[:, kt, :], 'vs', None) for kt in range(KM)]
            tiles.append((ksl[:, qt * 128:(qt + 1) * 128], 'ksl', [(k.ident, triT, ['cst', 'tri'], 'hg4')], vsl[:, qt, :], 'vsl', None))
            run_tiles(tiles, Q, qk, o_b, d_b)
            finish_branch(o_b, d_b, (OB[1], ('OB', 1)))
            o_b, d_b = obank[br % 2], dbank[br % 2]
            br += 1
            tiles = []
            for i in range(5):
                w = u + i
                masks = []
                if i == 0:
                    masks.append((k.ident, tri2, ['cst', 'tri'], 'hg4'))
                if i == 4:
                    masks.append((k.ident, triT, ['cst', 'tri'], 'hg4'))
                if w < 4:
                    masks.append((ones1, pcr[:, lc, :], ['cst', 'pcr'], 'hg4'))
                tiles.append((kwp[:, lc, w * 128:(w + 1) * 128], 'kwp', masks, vwp[:, lc, w, :], 'vwp', None))
            run_tiles(tiles, Q, qk, o_b, d_b)
            finish_branch(o_b, d_b, (OB[2], ('OB', 2)))
            for b in range(3):
                for hg in range(4):
                    r0 = (b * 8 + grp * 4 + hg) * 128
                    k.mm(k.ps[6][:, hg * 128:(hg + 1) * 128], selg[:, r0:r0 + 128], gb[:, qt * 128:(qt + 1) * 128],
                         start=(hg == 0), stop=True, skip=(hg > 0), reads=['selg', 'gb'], writes=[('ps', 6)])
                k.tt(OB[b], OB[b], k.ps[6][:, :], ALU.mult, reads=[('OB', b), ('ps', 6)], writes=[('OB', b)])
            k.tt(osum, OB[0], OB[1], ALU.add, reads=[('OB', 0), ('OB', 1)], writes=['osum'])
            k.tt(ob[:, grp * 4:(grp + 1) * 4, qt * 128:(qt + 1) * 128], v3(osum, 4), v3(OB[2], 4), ALU.add,
                 reads=['osum', ('OB', 2)], writes=[('ob', grp * 4 + hg, lc) for hg in range(4)])
    k.P.barrier()
    k.reset(m)


ARENA_WORDS = 53000


def attn_tail(k, wname, m, mq, q, li, nxt):
    k.P.barrier()
    k.reset(mq)
    mo = k.mark()
    out_proj_residual(k, wname, q, lambda h, c: [('q', h, c)])
    k.P.barrier()
    k.reset(m)
    mlp(k, li)
    if nxt is not None:
        nxt(k)
    store_x(k)


def build_launch(li, cfg):
    st = ExitStack()
    k = K(cfg, st)
    k.start(ARENA_WORDS)
    load_consts(k)
    if li == 0:
        load_x(k)
        nsa_pre(k)
    elif li == 1:
        ob = v3(k.b16(H * NT), H)
        nsa_core(k, ob)
        load_x(k)
        mo = k.mark()
        k.din('w_out0', [D, D], F32)
        out_proj_residual(k, 'w_out0', ob, lambda h, c: [('ob', h, c)])
        k.P.barrier()
        k.reset(mo)
        mlp(k, 0)
        qkv_pre(k, 1, False)
        store_x(k)
    elif li == 2:
        load_x(k)
        m, mq, q = sb_core(k)
        k.din('w_out1', [D, D], F32)
        attn_tail(k, 'w_out1', m, mq, q, 1, conv_pre)
    elif li == 3:
        load_x(k)
        m, ucb = conv_core(k)
        k.din('w_out2', [D, D], F32)
        out_proj_residual(k, 'w_out2', ucb, lambda h, c: [('ucb', h, c)])
        k.P.barrier()
        k.reset(m)
        mlp(k, 2)
        qkv_pre(k, 3, True)
        store_x(k)
    elif li == 4:
        load_x(k)
        m, mq, q = moba_core(k)
        k.din('w_out3', [D, D], F32)
        attn_tail(k, 'w_out3', m, mq, q, 3, None)
    nc = k.finish()
    return nc, k, st


def pp(v, n):
    return np.ascontiguousarray(np.asarray(v, np.float32).reshape(n, 128).T)


def kernel_impl(inputs, cfg, runner, upto=5, dbg=None):
    R, B, S, NC = cfg.R, cfg.B, cfg.S, cfg.NC
    NKT = cfg.NKT
    x = np.asarray(inputs['x'], np.float32)
    gch = [cfg.gchunks(c % R) for c in range(NC)]
    bat = [c // R for c in range(NC)]
    pos = [np.concatenate([np.arange(g * CH, (g + 1) * CH) for g in gch[c]]) for c in range(NC)]
    cst = host_consts()
    P = {n: np.asarray(v, np.float32) for n, v in inputs.items() if n != 'x'}

    def run(li, in_maps):
        nc, k, st = build_launch(li, cfg)
        for im in in_maps:
            assert set(im.keys()) == set(k.in_specs.keys()), (sorted(set(im.keys()) ^ set(k.in_specs.keys())))
            for n, (shp, dt) in k.in_specs.items():
                want = bf if dt == BF16 else np.float32
                a = im[n]
                if a.dtype != want:
                    a = a.astype(want)
                assert tuple(a.shape) == tuple(shp), (n, a.shape, shp)
                im[n] = np.ascontiguousarray(a)
        res = runner(nc, in_maps)
        st.close()
        return [{n[2:]: v for n, v in r.items()} for r in res]

    def to_global(res, name, axis):
        out = []
        for b in range(B):
            parts = [None] * (NCH * R)
            for c in range(NC):
                if bat[c] != b:
                    continue
                a = np.asarray(res[c][name])
                for lc, g in enumerate(gch[c]):
                    sl = [slice(None)] * a.ndim
                    sl[axis] = slice(lc * CH, (lc + 1) * CH)
                    parts[g] = a[tuple(sl)]
            out.append(np.concatenate(parts, axis=axis))
        return out

    def vlay(vg, nh):
        return np.ascontiguousarray(vg.reshape(NKT, 128, nh, 128).transpose(2, 1, 0, 3))

    inv_freq = (np.float32(10000.0) ** (-np.arange(64, dtype=np.float32) / np.float32(64))).astype(np.float32)
    cosT, sinT = [], []
    for c in range(NC):
        ang = pos[c].astype(np.float32)[:, None] * inv_freq[None, :]
        cosT.append(np.ascontiguousarray(np.concatenate([np.cos(ang), np.cos(ang)], axis=1).T.astype(np.float32)))
        sinT.append(np.ascontiguousarray(np.concatenate([np.sin(ang), np.sin(ang)], axis=1).T.astype(np.float32)))

    def mlpw(li):
        return {'g_mlp%d' % li: pp(P['mlp_norm'][li], DC), 'w_up%d' % li: P['mlp_w_up'][li], 'w_dn%d' % li: P['mlp_w_down'][li]}

    xT = [np.ascontiguousarray(x[bat[c], pos[c], :].T) for c in range(NC)]
    ims = [dict(cst=cst, xT=xT[c], g_an0=pp(P['attn_norm'][0], DC), w_in0=P['nsa_w_in'][0], cosT=cosT[c], sinT=sinT[c],
                nsa_qn=pp(P['nsa_q_norm'][0], 1), nsa_kn=np.ascontiguousarray(P['nsa_k_norm'][0].T)) for c in range(NC)]
    r0 = run(0, ims)
    if dbg is not None:
        dbg['r0'] = r0
    if upto <= 1:
        return r0
    NSEL = S // 64
    NCMP = S // 16 - 1
    NCC = (NCMP + 127) // 128
    NQT = NT // 128
    kcf, vcf, ksf, kwf = [to_global(r0, n, 1) for n in ('kcT', 'vcT', 'ksT', 'kwT')]
    vsf, vwf = [to_global(r0, n, 0) for n in ('vs', 'vw')]
    jj = np.arange(NSEL)
    Gm = (np.arange(S)[None, :] // 64 == jj[:, None]).astype(np.float32)
    nn = np.arange(NCC * 128)
    ovl = ((16 * nn[:, None] < 64 * jj[None, :] + 64) & (16 * nn[:, None] + 32 > 64 * jj[None, :]) & (nn[:, None] < NCMP)).astype(np.float32)
    Ov = np.ascontiguousarray(ovl.reshape(NCC, 128, NSEL).transpose(1, 0, 2))
    ss, tt = np.meshgrid(np.arange(128), np.arange(128), indexing='ij')
    tri = np.concatenate([np.where(ss <= tt, 0.0, NEG), np.where(ss > tt, 0.0, NEG)], axis=1).astype(np.float32)
    selg = np.zeros((24, 24 * 128), np.float32)
    for r_ in range(24):
        selg[r_, r_ * 128:(r_ + 1) * 128] = 1.0
    ims = []
    for c in range(NC):
        b = bat[c]
        McT = np.zeros((128, NQT, NCC, 128), np.float32)
        sval = np.zeros((128, NQT, NSEL), np.float32)
        sbon = np.zeros((128, NQT, NSEL), np.float32)
        spneg = np.zeros((128, NQT, NSEL), np.float32)
        pcrow = np.zeros((1, NCH, 128), np.float32)
        kwp = np.zeros((256, NCH, 2 * CH), np.float32)
        vwp = np.zeros((2, 128, NCH, 8, 128), np.float32)
        for lc, g in enumerate(gch[c]):
            if g == 0:
                pcrow[0, lc, :] = NEG
            lo = (g - 1) * CH
            if g > 0:
                kwp[:, lc, 0:CH] = kwf[b][:, lo:lo + CH].astype(np.float32)
                vprev = vwf[b][lo:lo + CH].astype(np.float32)
            else:
                vprev = np.zeros((CH, 256), np.float32)
            kwp[:, lc, CH:] = kwf[b][:, g * CH:(g + 1) * CH].astype(np.float32)
            vboth = np.concatenate([vprev, vwf[b][g * CH:(g + 1) * CH].astype(np.float32)], axis=0)
            vwp[:, :, lc] = vboth.reshape(8, 128, 2, 128).transpose(2, 1, 0, 3)
            for u in range(4):
                qt = lc * 4 + u
                T = 4 * g + u
                tl = np.arange(128)
                tg = 128 * T + tl
                n_ = np.arange(NCC * 128).reshape(NCC, 128)
                vis = (16 * n_[:, :, None] + 31 <= tg[None, None, :]) & (n_[:, :, None] < NCMP)
                McT[:, qt, :, :] = np.where(vis, 0.0, NEG).transpose(1, 0, 2)
                cur = tg // 64
                le = jj[None, :] <= cur[:, None]
                forced = (jj[None, :] == 0) | (jj[None, :] == cur[:, None]) | (jj[None, :] == cur[:, None] - 1)
                sval[:, qt, :] = le
                sbon[:, qt, :] = np.where(le, 1000.0 * forced, -1.0)
                spneg[:, qt, :] = np.where(jj[None, :] >= 2 * T, NEG, 0.0)
        ksown = np.asarray(r0[c]['ksT'])
        ims.append(dict(cst=cst, qT=np.asarray(r0[c]['qT']), gT=np.asarray(r0[c]['gT']), selg=selg, G=Gm, Ov=Ov, tri=tri,
                        McT=McT, sval=sval, sbon=sbon, spneg=spneg, pcrow=pcrow,
                        cmp_posT=np.ascontiguousarray(P['nsa_cmp_pos'][0].transpose(2, 0, 1).reshape(128, 64)),
                        kcT_full=kcf[b], vcT_full=vcf[b], w_cmp=P['nsa_w_cmp'][0], ksT_full=ksf[b], vs_full=vlay(vsf[b], 2),
                        ksT_own=ksown, vs_own=np.asarray(r0[c]['vs']), kw_pack=kwp, vw_pack=vwp,
                        xT=xT[c], w_out0=P['nsa_w_out'][0], g_an1=pp(P['attn_norm'][1], DC), w_in1=P['sb_w_in'][0], **mlpw(0)))
    r1 = run(1, ims)
    if dbg is not None:
        dbg['r1'] = r1
    if upto <= 2:
        return r1
    ktf = to_global(r1, 'kT', 1)
    vf = to_global(r1, 'v', 0)
    js, s_ = np.meshgrid(np.arange(128), np.arange(128), indexing='ij')
    sbc = np.zeros((128, 256), np.float32)
    sbc[:, 0:128] = np.where(js >= s_, -1.0, 0.0)
    sbc[:, 128:256] = -1.0
    ims = []
    for c in range(NC):
        b = bat[c]
        sbm = np.zeros((128, NCH, 4 * R, CH), np.float32)
        for lc, g in enumerate(gch[c]):
            for i in range(4 * R):
                kt = 4 * cfg.GMIN[lc] + i
                sg_ = 128 * kt + np.arange(128)
                tg = CH * g + np.arange(CH)
                sbm[:, lc, i, :] = sg_[:, None] < tg[None, :]
        ims.append(dict(cst=cst, xT=np.asarray(r1[c]['xT_out']), qT=np.asarray(r1[c]['qT']), kT_full=ktf[b], v_full=vlay(vf[b], H),
                        sbmask=sbm, sbc=sbc, w_out1=P['sb_w_out'][0], g_an2=pp(P['attn_norm'][2], DC), conv_w_in=P['conv_w_in'][0], **mlpw(1)))
    r2 = run(2, ims)
    if dbg is not None:
        dbg['r2'] = r2
    if upto <= 3:
        return r2
    uf = to_global(r2, 'uT', 1)
    ims = []
    for c in range(NC):
        b = bat[c]
        uE = np.zeros((D, NCH, 32 + CH), np.float32)
        for lc, g in enumerate(gch[c]):
            if g > 0:
                uE[:, lc, 0:32] = uf[b][:, g * CH - 32:g * CH].astype(np.float32)
            uE[:, lc, 32:] = uf[b][:, g * CH:(g + 1) * CH].astype(np.float32)
        dw = P['conv_dw_w'][0]
        conv_dw = np.ascontiguousarray(dw.reshape(31, DC, 128).transpose(2, 1, 0).reshape(128, DC * 31))
        ims.append(dict(cst=cst, xT=np.asarray(r2[c]['xT_out']), uE=uE, conv_dw=conv_dw, conv_dwb=pp(P['conv_dw_b'][0], DC),
                        conv_lng=pp(P['conv_ln_g'][0], DC), conv_lnb=pp(P['conv_ln_b'][0], DC), w_out2=P['conv_w_out'][0],
                        g_an3=pp(P['attn_norm'][3], DC), w_in3=P['moba_w_in'][0], cosT=cosT[c], sinT=sinT[c],
                        moba_qn=pp(P['moba_q_norm'][0], 1), moba_kn=pp(P['moba_k_norm'][0], 1), **mlpw(2)))
    r3 = run(3, ims)
    if dbg is not None:
        dbg['r3'] = r3
    if upto <= 4:
        return r3
    NB = S // 256
    ktf = to_global(r3, 'kT', 1)
    vf = to_global(r3, 'v', 0)
    kmf = []
    for b in range(B):
        km = np.zeros((D, NB), np.float32)
        for c in range(NC):
            if bat[c] == b:
                a = np.asarray(r3[c]['kmean'])
                for lc, g in enumerate(gch[c]):
                    km[:, 2 * g:2 * g + 2] = a[:, 2 * lc:2 * lc + 2]
        kmf.append(km)
    esel = np.zeros((NB, NB * 128), np.float32)
    for n_ in range(NB):
        esel[n_, n_ * 128:(n_ + 1) * 128] = 1.0
    om = np.zeros((128, 4, CH), np.float32)
    for i2 in range(4):
        sl_ = 128 * i2 + np.arange(128)
        tl = np.arange(CH)
        ok = (i2 // 2 == tl[None, :] // 256) & (sl_[:, None] <= tl[None, :])
        om[:, i2, :] = np.where(ok, 0.0, NEG)
    ims = []
    for c in range(NC):
        b = bat[c]
        mv = np.zeros((128, NCH, 4 * NB), np.float32)
        for lc, g in enumerate(gch[c]):
            for u in range(4):
                cur = 2 * g + u // 2
                mv[:, lc, u * NB:(u + 1) * NB] = np.where(np.arange(NB) < cur, 0.0, -1e30)[None, :]
        ims.append(dict(cst=cst, xT=np.asarray(r3[c]['xT_out']), qT=np.asarray(r3[c]['qT']), kT_full=ktf[b], v_full=vlay(vf[b], H),
                        kT_own=np.asarray(r3[c]['kT']), v_own=np.asarray(r3[c]['v']), kmean_full=kmf[b], mvalid=mv, esel=esel,
                        ownmask=om, w_out3=P['moba_w_out'][0], **mlpw(3)))
    r4 = run(4, ims)
    if dbg is not None:
        dbg['r4'] = r4
    out = np.zeros((B, S, D), np.float32)
    for c in range(NC):
        out[bat[c], pos[c], :] = np.asarray(r4[c]['xT_out']).T
    return out


def hw_runner(nc, in_maps):
    res = run_bass_kernel_spmd(nc, in_maps, core_ids=list(range(len(in_maps))))
    return res.results


def kernel(**inputs):
    return kernel_impl(inputs, Cfg(4, 2), hw_runner)
```

```python
import numpy as np
import ml_dtypes
from contextlib import ExitStack
import concourse.bass as bass
import concourse.mybir as mybir
from concourse.bass_utils import run_bass_kernel_spmd

F32 = mybir.dt.float32
BF16 = mybir.dt.bfloat16
AF = mybir.ActivationFunctionType
ALU = mybir.AluOpType
bf = ml_dtypes.bfloat16

D = 1024
DC = 8
H = 8
DH = 128
CH = 512
NCH = 4
NT = NCH * CH
DFF = 4096
EPS = 1e-6
NEG = -30000.0
SCALE = DH ** -0.5


class Prog:
    NDMA = 24

    def __init__(self, nc):
        self.nc = nc
        self.ops = []
        self.lastw = {}
        self.readers = {}
        self.bar = set()
        self.last_eng = {}
        self.dma_since = []

    def op(self, eng, fn, reads=(), writes=(), dma=False):
        idx = len(self.ops)
        deps = set(self.bar)
        for k in reads:
            if k in self.lastw:
                deps.add(self.lastw[k])
        for k in writes:
            if k in self.lastw:
                deps.add(self.lastw[k])
            deps.update(self.readers.get(k, ()))
        for k in reads:
            self.readers.setdefault(k, []).append(idx)
        for k in writes:
            self.lastw[k] = idx
            self.readers[k] = []
        self.ops.append(dict(eng=eng, fn=fn, deps=deps, dma=dma))
        self.last_eng[eng] = idx
        if dma:
            self.dma_since.append(idx)
        return idx

    def barrier(self):
        self.bar = set(self.last_eng.values()) | set(self.dma_since)
        self.dma_since = []
        self.lastw = {}
        self.readers = {}

    def emit(self, stack):
        nc = self.nc
        ops = self.ops
        n = len(ops)
        needed = [False] * n
        for i, o in enumerate(ops):
            nd = set()
            for d in o['deps']:
                if ops[d]['eng'] == 'pe' and o['eng'] == 'pe' and not ops[d]['dma']:
                    continue
                nd.add(d)
                needed[d] = True
            o['deps'] = nd
        engs = ['pe', 'act', 'dve', 'pool', 'sp']
        esem = {e: stack.enter_context(nc.semaphore('s_' + e)) for e in engs}
        dsem = [stack.enter_context(nc.semaphore('d_%d' % i)) for i in range(self.NDMA)]
        ecount = {e: 0 for e in engs}
        dcount = [0] * self.NDMA
        rr = {'sp': 0, 'pool': 0}
        NSP = 16
        for i, o in enumerate(ops):
            if o['dma']:
                if o['eng'] == 'pool':
                    s = NSP + rr['pool'] % (self.NDMA - NSP)
                    rr['pool'] += 1
                else:
                    s = rr['sp'] % NSP
                    rr['sp'] += 1
                o['prev'] = (dsem[s], dcount[s]) if dcount[s] > 0 else None
                dcount[s] += 16
                o['sig'] = (dsem[s], dcount[s])
            else:
                if needed[i]:
                    ecount[o['eng']] += 1
                    o['sig'] = (esem[o['eng']], ecount[o['eng']])
                else:
                    o['sig'] = None
        final_d = [(dsem[s], dcount[s]) for s in range(self.NDMA) if dcount[s] > 0]
        block = stack.enter_context(nc.Block())

        def run(ename, e):
            waited = {}

            def w(sem, val):
                k = id(sem)
                if waited.get(k, 0) >= val:
                    return
                waited[k] = val
                e.wait_ge(sem, val)
            for o in ops:
                if o['eng'] != ename:
                    continue
                for d in sorted(o['deps']):
                    sg = ops[d]['sig']
                    w(sg[0], sg[1])
                if o['dma'] and o['prev'] is not None:
                    w(*o['prev'])
                ins = o['fn'](e)
                if o['dma']:
                    ins.then_inc(o['sig'][0], 16)
                elif o['sig'] is not None:
                    ins.then_inc(o['sig'][0], 1)
            if ename == 'sp':
                for sem, val in final_d:
                    w(sem, val)

        @block.tensor
        def _(e):
            run('pe', e)

        @block.scalar
        def _(e):
            run('act', e)

        @block.vector
        def _(e):
            run('dve', e)

        @block.gpsimd
        def _(e):
            run('pool', e)

        @block.sync
        def _(e):
            run('sp', e)


class Cfg:
    def __init__(self, R=4, B=2):
        self.R = R
        self.B = B
        self.S = NCH * R * CH
        self.NC = R * B
        self.NKT = self.S // 128
        self.KM = [4 * R, 8 * R, 12 * R, 16 * R]
        self.GMIN = [0, R, 2 * R, 3 * R]

    def gchunks(self, j):
        R = self.R
        return [j, 2 * R - 1 - j, 2 * R + j, 4 * R - 1 - j]


class K:
    def __init__(self, cfg, st):
        self.cfg = cfg
        self.st = st
        self.nc = bass.Bass("TRN2", target_bir_lowering=False)
        self.P = Prog(self.nc)
        self.dram = {}
        self.in_specs = {}
        self.out_specs = {}
        self.arena = None
        self.off = 0
        self.hiwater = 0
        self.uid = 0

    def start(self, words):
        self.arena = self.st.enter_context(self.nc.sbuf_tensor("arena", [128, words], F32))
        self.words = words
        self.ps = [self.st.enter_context(self.nc.psum_tensor("ps%d" % i, [128, 512], F32)) for i in range(8)]

    def din(self, name, shape, dt):
        t = self.nc.dram_tensor(name, list(shape), dt, kind="ExternalInput").ap()
        self.dram[name] = t
        self.in_specs[name] = (tuple(shape), dt)
        return t

    def dout(self, name, shape, dt):
        t = self.nc.dram_tensor('o_' + name, list(shape), dt, kind="ExternalOutput").ap()
        self.dram[name] = t
        self.out_specs[name] = (tuple(shape), dt)
        return t

    def f32(self, n, parts=128):
        a = self.arena[0:parts, self.off:self.off + n]
        self.off += n
        self.hiwater = max(self.hiwater, self.off)
        assert self.off <= self.words, ("arena overflow", self.off, self.words)
        return a

    def b16(self, n, parts=128):
        w = (n + 1) // 2
        a = self.arena[0:parts, self.off:self.off + w].bitcast(BF16)
        self.off += w
        self.hiwater = max(self.hiwater, self.off)
        assert self.off <= self.words, ("arena overflow", self.off, self.words)
        return a[:, 0:n]

    def mark(self):
        return self.off

    def reset(self, m):
        self.off = m

    def key(self, s):
        self.uid += 1
        return "%s#%d" % (s, self.uid)

    def dma(self, out, in_, reads=(), writes=(), eng='sp'):
        self.P.op(eng, lambda e, out=out, in_=in_: e.dma_start(out=out, in_=in_), reads, writes, dma=True)

    def mm(self, out, lhsT, rhs, start=True, stop=True, reads=(), writes=(), skip=False):
        self.P.op('pe', lambda e, out=out, lhsT=lhsT, rhs=rhs, start=start, stop=stop, skip=skip:
                  e.matmul(out, lhsT, rhs, start=start, stop=stop, skip_group_check=skip), reads, writes)

    def act(self, out, in_, func, reads=(), writes=(), bias=None, scale=None):
        kw = {}
        if bias is not None:
            kw['bias'] = bias
        if scale is not None:
            kw['scale'] = scale
        self.P.op('act', lambda e, out=out, in_=in_, func=func, kw=kw: e.activation(out=out, in_=in_, func=func, **kw),
                  reads, writes)

    def tt(self, out, in0, in1, op, reads=(), writes=(), eng='dve'):
        self.P.op(eng, lambda e, out=out, in0=in0, in1=in1, op=op: e.tensor_tensor(out=out, in0=in0, in1=in1, op=op),
                  reads, writes)

    def ts(self, out, in0, s1, s2, op0, op1=None, reads=(), writes=(), eng='dve'):
        if op1 is None:
            self.P.op(eng, lambda e, out=out, in0=in0, s1=s1, op0=op0:
                      e.tensor_scalar(out=out, in0=in0, scalar1=s1, scalar2=None, op0=op0), reads, writes)
        else:
            self.P.op(eng, lambda e, out=out, in0=in0, s1=s1, s2=s2, op0=op0, op1=op1:
                      e.tensor_scalar(out=out, in0=in0, scalar1=s1, scalar2=s2, op0=op0, op1=op1), reads, writes)

    def stt(self, out, in0, scalar, in1, op0, op1, reads=(), writes=(), eng='dve'):
        self.P.op(eng, lambda e, out=out, in0=in0, scalar=scalar, in1=in1, op0=op0, op1=op1:
                  e.scalar_tensor_tensor(out=out, in0=in0, scalar=scalar, in1=in1, op0=op0, op1=op1), reads, writes)

    def copy(self, out, in_, reads=(), writes=(), eng='dve'):
        self.P.op(eng, lambda e, out=out, in_=in_: e.tensor_copy(out=out, in_=in_), reads, writes)

    def recip(self, out, in_, reads=(), writes=()):
        self.P.op('dve', lambda e, out=out, in_=in_: e.reciprocal(out=out, in_=in_), reads, writes)

    def memset(self, ap, val, writes=(), eng='pool'):
        self.P.op(eng, lambda e, ap=ap, val=val: e.memset(ap, val), (), writes)

    def finish(self):
        self.P.emit(self.st)
        return self.nc


def v3(ap, a):
    return ap.rearrange("p (a b) -> p a b", a=a)


def load_consts(k):
    cin = k.din('cst', [128, 5 * 128], BF16)
    c = k.b16(5 * 128)
    k.dma(c, cin[:, :], writes=['cst'])
    k.ident = c[:, 0:128]
    k.onesD = c[:, 128:256]
    k.onesH = c[:, 256:384]
    k.ones = c[:, 384:512]
    k.Rm = c[:, 512:640]
    k.bank_rr = {}


def host_consts():
    c = np.zeros((128, 5 * 128), np.float32)
    c[:, 0:128] = np.eye(128)
    c[:, 128:256] = 1.0 / 1024
    c[:, 256:384] = 1.0 / 128
    c[:, 384:512] = 1.0
    Rm = np.zeros((128, 128), np.float32)
    for dd in range(64):
        Rm[dd + 64, dd] = -1.0
        Rm[dd, dd + 64] = 1.0
    c[:, 512:640] = Rm
    return c.astype(bf)


def nextbank(k, role, banks):
    i = k.bank_rr.get(role, 0)
    k.bank_rr[role] = i + 1
    return banks[i % len(banks)]


def alloc_x(k):
    k.xraw = k.f32(DC * NT)
    k.xT = v3(k.xraw, DC)


def load_x(k, alloc=True):
    xin = k.din('xT', [D, NT], F32)
    if alloc:
        alloc_x(k)
    for dc in range(DC):
        k.dma(k.xT[:, dc, :], xin[dc * 128:(dc + 1) * 128, :], writes=[('x', dc, c) for c in range(NCH)])


def store_x(k):
    xo = k.dout('xT_out', [D, NT], F32)
    for dc in range(DC):
        k.dma(xo[dc * 128:(dc + 1) * 128, :], k.xT[:, dc, :], reads=[('x', dc, c) for c in range(NCH)],
              writes=[('xo', dc)])


def load_vec(k, name, n):
    t = k.f32(n)
    k.dma(t, k.din(name, [128, n], F32)[:, :], writes=[name])
    return t


def rmsnorm(k, gname, hn):
    g = load_vec(k, gname, DC)
    sq = [v3(k.b16(DC * CH), DC) for _ in range(2)]
    rs = [k.f32(CH) for _ in range(2)]
    for c in range(NCH):
        s = c % 2
        cs = slice(c * CH, (c + 1) * CH)
        k.act(sq[s], k.xT[:, :, cs], AF.Square, reads=[('x', dc, c) for dc in range(DC)], writes=[('sq', s)])
        pb = 6 + s
        for dc in range(DC):
            k.mm(k.ps[pb][:, :], k.onesD, sq[s][:, dc, :], start=(dc == 0), stop=(dc == DC - 1),
                 reads=[('sq', s), 'cst'], writes=[('ps', pb)])
        k.act(rs[s], k.ps[pb][:, :], AF.Sqrt, bias=EPS, reads=[('ps', pb)], writes=[('rs', s)])
        k.recip(rs[s], rs[s], reads=[('rs', s)], writes=[('rs', s)])
        for dc in range(DC):
            k.stt(hn[:, dc, cs], k.xT[:, dc, cs], g[:, dc:dc + 1], rs[s], ALU.mult, ALU.mult,
                  reads=[('x', dc, c), ('rs', s), gname], writes=[('hn', dc, c)])


def proj_fm(k, wname, blocks, hn, consumer, banks=(0, 1, 2, 3)):
    wv = k.dram[wname].rearrange("(dc p) m -> p dc m", p=128)
    slots = [v3(k.b16(DC * 512), DC) for _ in range(2)]
    for bi, (c0, ncol) in enumerate(blocks):
        s = bi % 2
        k.dma(slots[s][:, :, 0:ncol], wv[:, :, c0:c0 + ncol], writes=[('wblk', wname, s)], eng='pool')
        for c in range(NCH):
            for mi in range((ncol + 127) // 128):
                mw = min(128, ncol - mi * 128)
                pb = nextbank(k, 'proj', banks)
                for dc in range(DC):
                    k.mm(k.ps[pb][0:mw, :], slots[s][:, dc, mi * 128:mi * 128 + mw], hn[:, dc, c * CH:(c + 1) * CH],
                         start=(dc == 0), stop=(dc == DC - 1),
                         reads=[('wblk', wname, s), ('hn', dc, c)], writes=[('ps', pb)])
                consumer(c0 + mi * 128, c, pb)


def proj_tm(k, wname, c0, ncol, hn, consumer, banks=(0, 1, 2, 3)):
    wv = k.dram[wname].rearrange("(dc p) m -> p dc m", p=128)
    wt = v3(k.b16(DC * ncol), DC)
    kk = k.key('wtm')
    k.dma(wt, wv[:, :, c0:c0 + ncol], writes=[kk], eng='pool')
    for tt in range(NT // 128):
        c = tt // 4
        pb = nextbank(k, 'proj', banks)
        for dc in range(DC):
            k.mm(k.ps[pb][:, 0:ncol], hn[:, dc, tt * 128:(tt + 1) * 128], wt[:, dc, :],
                 start=(dc == 0), stop=(dc == DC - 1), reads=[kk, ('hn', dc, c)], writes=[('ps', pb)])
        consumer(tt, pb)


def out_proj_residual(k, wname, ob, okeyf):
    wv = k.dram[wname].rearrange("(dc p) m -> p dc m", p=128)
    wt = v3(k.b16(DC * D), DC)
    kk = k.key('wout')
    k.dma(wt, wv, writes=[kk], eng='pool')
    for c in range(NCH):
        cs = slice(c * CH, (c + 1) * CH)
        for dco in range(DC):
            pb = nextbank(k, 'op', (4, 5))
            for h in range(DC):
                k.mm(k.ps[pb][:, :], wt[:, h, dco * 128:(dco + 1) * 128], ob[:, h, cs],
                     start=(h == 0), stop=(h == DC - 1), reads=[kk] + okeyf(h, c), writes=[('ps', pb)])
            k.tt(k.xT[:, dco, cs], k.xT[:, dco, cs], k.ps[pb][:, :], ALU.add,
                 reads=[('ps', pb), ('x', dco, c)], writes=[('x', dco, c)])


def mlp(k, li):
    m = k.mark()
    hn = v3(k.b16(DC * NT), DC)
    rmsnorm(k, 'g_mlp%d' % li, hn)
    wu = k.din('w_up%d' % li, [D, DFF], F32).rearrange("(dc p) m -> p dc m", p=128)
    wd = k.din('w_dn%d' % li, [DFF, D], F32).rearrange("(fc p) m -> p fc m", p=128)
    ups = [v3(k.b16(DC * 512), DC) for _ in range(2)]
    dns = [v3(k.b16(4 * D), 4) for _ in range(2)]
    rl = [k.b16(CH) for _ in range(2)]
    h1 = [v3(k.b16(4 * CH), 4) for _ in range(2)]
    it = 0
    for fb in range(DFF // 512):
        s = fb % 2
        k.dma(ups[s], wu[:, :, fb * 512:(fb + 1) * 512], writes=[('wup', s)], eng='pool')
        k.dma(dns[s], wd[:, fb * 4:(fb + 1) * 4, :], writes=[('wdn', s)], eng='pool')
        for c in range(NCH):
            cs = slice(c * CH, (c + 1) * CH)
            hs = it % 2
            it += 1
            for fc in range(4):
                pb = nextbank(k, 'mlpu', (0, 1))
                for dc in range(DC):
                    k.mm(k.ps[pb][:, :], ups[s][:, dc, fc * 128:(fc + 1) * 128], hn[:, dc, cs],
                         start=(dc == 0), stop=(dc == DC - 1), reads=[('wup', s), ('hn', dc, c)], writes=[('ps', pb)])
                r = nextbank(k, 'rl', (0, 1))
                k.act(rl[r], k.ps[pb][:, :], AF.Relu, reads=[('ps', pb)], writes=[('rl', r)])
                k.tt(h1[hs][:, fc, :], rl[r], rl[r], ALU.mult, reads=[('rl', r)], writes=[('h1', hs, fc)], eng='pool')
            for dco in range(DC):
                pb = nextbank(k, 'mlpd', (2, 3))
                for fc in range(4):
                    k.mm(k.ps[pb][:, :], dns[s][:, fc, dco * 128:(dco + 1) * 128], h1[hs][:, fc, :],
                         start=(fc == 0), stop=(fc == 3), reads=[('wdn', s), ('h1', hs, fc)], writes=[('ps', pb)])
                k.tt(k.xT[:, dco, cs], k.xT[:, dco, cs], k.ps[pb][:, :], ALU.add,
                     reads=[('ps', pb), ('x', dco, c)], writes=[('x', dco, c)])
    k.P.barrier()
    k.reset(m)


def normrope_setup(k):
    k.cosT = k.f32(NT)
    k.sinT = k.f32(NT)
    k.dma(k.cosT, k.din('cosT', [128, NT], F32)[:, :], writes=['cos'])
    k.dma(k.sinT, k.din('sinT', [128, NT], F32)[:, :], writes=['sin'])
    k.nr_xg = [k.b16(CH) for _ in range(2)]
    k.nr_sq = [k.b16(CH) for _ in range(2)]
    k.nr_rs = [k.f32(CH) for _ in range(2)]
    k.nr_t1 = [k.f32(CH) for _ in range(2)]
    k.nr_t2 = [k.f32(CH) for _ in range(2)]


def normrope(k, pb, c, gain, gkey, out, okeys):
    s = nextbank(k, 'nr', (0, 1))
    cs = slice(c * CH, (c + 1) * CH)
    xg, sq, rs, t1, t2 = k.nr_xg[s], k.nr_sq[s], k.nr_rs[s], k.nr_t1[s], k.nr_t2[s]
    k.act(xg, k.ps[pb][:, :], AF.Identity, scale=gain, reads=[('ps', pb), gkey], writes=[('nrxg', s)])
    k.act(sq, k.ps[pb][:, :], AF.Square, reads=[('ps', pb)], writes=[('nrsq', s)])
    b1 = nextbank(k, 'nrss', (4, 5))
    b2 = nextbank(k, 'nrrot', (6, 7))
    k.mm(k.ps[b1][:, :], k.onesH, sq, reads=[('nrsq', s), 'cst'], writes=[('ps', b1)])
    k.mm(k.ps[b2][:, :], k.Rm, xg, reads=[('nrxg', s), 'cst'], writes=[('ps', b2)])
    k.act(rs, k.ps[b1][:, :], AF.Sqrt, bias=EPS, reads=[('ps', b1)], writes=[('nrrs', s)])
    k.recip(rs, rs, reads=[('nrrs', s)], writes=[('nrrs', s)])
    k.tt(t1, xg, k.cosT[:, cs], ALU.mult, reads=[('nrxg', s), 'cos'], writes=[('nrt1', s)])
    k.tt(t2, k.ps[b2][:, :], k.sinT[:, cs], ALU.mult, reads=[('ps', b2), 'sin'], writes=[('nrt2', s)])
    k.tt(t1, t1, t2, ALU.add, reads=[('nrt1', s), ('nrt2', s)], writes=[('nrt1', s)])
    k.tt(out, t1, rs, ALU.mult, reads=[('nrt1', s), ('nrrs', s)], writes=okeys)


def qkv_pre(k, li, moba):
    m = k.mark()
    hn = v3(k.b16(DC * NT), DC)
    rmsnorm(k, 'g_an%d' % li, hn)
    k.din('w_in%d' % li, [D, 3 * D], F32)
    qo = k.dout('qT', [D, NT], BF16)
    ko = k.dout('kT', [D, NT], BF16)
    vo = k.dout('v', [NT, D], BF16)
    stage = [k.b16(CH) for _ in range(3)]
    if moba:
        normrope_setup(k)
        gq = load_vec(k, 'moba_qn', 1)
        gk = load_vec(k, 'moba_kn', 1)
        kmo = k.dout('kmean', [D, 2 * NCH], F32)
        kms = k.f32(H * 2 * NCH)
        kmf = [k.f32(CH) for _ in range(2)]

    def cons(col0, c, pb):
        s = nextbank(k, 'stg', (0, 1, 2))
        cs = slice(c * CH, (c + 1) * CH)
        isq = col0 < D
        m_ = (col0 % D) // 128
        dst = (qo if isq else ko)[m_ * 128:(m_ + 1) * 128, cs]
        if not moba:
            k.act(stage[s], k.ps[pb][:, :], AF.Copy, scale=(SCALE if isq else 1.0), reads=[('ps', pb)], writes=[('stg', s)])
        else:
            normrope(k, pb, c, gq[:, 0:1] if isq else gk[:, 0:1], 'moba_qn' if isq else 'moba_kn', stage[s], [('stg', s)])
            if not isq:
                f = nextbank(k, 'kmf', (0, 1))
                k.copy(kmf[f], stage[s], reads=[('stg', s)], writes=[('kmf', f)])
                for b2 in range(2):
                    col = m_ * 2 * NCH + c * 2 + b2
                    k.P.op('dve', lambda e, o=kms[:, col:col + 1], i=kmf[f][:, b2 * 256:(b2 + 1) * 256]:
                           e.reduce_sum(out=o, in_=i, axis=mybir.AxisListType.X), [('kmf', f)], [('kms', col)])
        k.dma(dst, stage[s], reads=[('stg', s)], writes=[k.key('o')])

    proj_fm(k, 'w_in%d' % li, [(0, 512), (512, 512), (1024, 512), (1536, 512)], hn, cons)
    vst = [k.b16(512) for _ in range(2)]
    for half in range(2):
        def consv(tt, pb, half=half):
            s = nextbank(k, 'vst', (0, 1))
            k.copy(vst[s], k.ps[pb][:, :], reads=[('ps', pb)], writes=[('vst', s)])
            k.dma(vo[tt * 128:(tt + 1) * 128, half * 512:(half + 1) * 512], vst[s], reads=[('vst', s)], writes=[k.key('o')])
        proj_tm(k, 'w_in%d' % li, 2 * D + half * 512, 512, hn, consv)
    if moba:
        k.ts(kms, kms, 1.0 / 256, None, ALU.mult, reads=[('kms', i) for i in range(H * 2 * NCH)], writes=['kmsall'])
        for h in range(H):
            k.dma(kmo[h * 128:(h + 1) * 128, :], kms[:, h * 2 * NCH:(h + 1) * 2 * NCH], reads=['kmsall'], writes=[k.key('o')])
    k.P.barrier()
    k.reset(m)


def conv_pre(k):
    m = k.mark()
    hn = v3(k.b16(DC * NT), DC)
    rmsnorm(k, 'g_an2', hn)
    wv = k.din('conv_w_in', [D, 2 * D], F32).rearrange("(dc p) m -> p dc m", p=128)
    uo = k.dout('uT', [D, NT], BF16)
    wa = [v3(k.b16(DC * 512), DC) for _ in range(2)]
    wb = [v3(k.b16(DC * 512), DC) for _ in range(2)]
    sg = [k.f32(CH) for _ in range(2)]
    us = [k.b16(CH) for _ in range(3)]
    for pi in range(2):
        k.dma(wa[pi], wv[:, :, pi * 512:(pi + 1) * 512], writes=[('wa', pi)], eng='pool')
        k.dma(wb[pi], wv[:, :, D + pi * 512:D + (pi + 1) * 512], writes=[('wb', pi)], eng='pool')
        for c in range(NCH):
            cs = slice(c * CH, (c + 1) * CH)
            for mi in range(4):
                dc = pi * 4 + mi
                pa = nextbank(k, 'cva', (0, 1))
                pbb = nextbank(k, 'cvb', (2, 3))
                for d2 in range(DC):
                    k.mm(k.ps[pa][:, :], wa[pi][:, d2, mi * 128:(mi + 1) * 128], hn[:, d2, cs], start=(d2 == 0), stop=(d2 == DC - 1),
                         reads=[('wa', pi), ('hn', d2, c)], writes=[('ps', pa)])
                for d2 in range(DC):
                    k.mm(k.ps[pbb][:, :], wb[pi][:, d2, mi * 128:(mi + 1) * 128], hn[:, d2, cs], start=(d2 == 0), stop=(d2 == DC - 1),
                         reads=[('wb', pi), ('hn', d2, c)], writes=[('ps', pbb)])
                s = nextbank(k, 'sg', (0, 1))
                u = nextbank(k, 'us', (0, 1, 2))
                k.act(sg[s], k.ps[pbb][:, :], AF.Sigmoid, reads=[('ps', pbb)], writes=[('sg', s)])
                k.tt(us[u], k.ps[pa][:, :], sg[s], ALU.mult, reads=[('ps', pa), ('sg', s)], writes=[('us', u)])
                k.dma(uo[dc * 128:(dc + 1) * 128, cs], us[u], reads=[('us', u)], writes=[k.key('o')])
    k.P.barrier()
    k.reset(m)


def conv_core(k):
    m = k.mark()
    HW = 32 + CH
    uin = k.din('uE', [D, NCH, HW], BF16)
    ue = v3(k.b16(DC * NCH * HW), DC * NCH)
    for dc in range(DC):
        k.dma(ue[:, dc * NCH:(dc + 1) * NCH, :], uin[dc * 128:(dc + 1) * 128, :, :], writes=[('ue', dc)])
    wdw = v3(load_vec(k, 'conv_dw', DC * 31), DC)
    bdw = load_vec(k, 'conv_dwb', DC)
    lg = load_vec(k, 'conv_lng', DC)
    lb = load_vec(k, 'conv_lnb', DC)
    ucb = v3(k.b16(DC * NT), DC)
    Dg = [v3(k.b16(31 * 128), 31) for _ in range(2)]
    for dc in range(DC):
        s = dc % 2
        for kk in range(31):
            k.ts(Dg[s][:, kk, :], k.ident, wdw[:, dc, kk:kk + 1], None, ALU.mult, reads=['cst', 'conv_dw'], writes=[('dg', s, kk)])
        for c in range(NCH):
            pb = nextbank(k, 'cv', (0, 1, 2))
            for kk in range(31):
                k.mm(k.ps[pb][:, :], Dg[s][:, kk, :], ue[:, dc * NCH + c, 2 + kk:2 + kk + CH], start=(kk == 0), stop=(kk == 30),
                     reads=[('dg', s, kk), ('ue', dc)], writes=[('ps', pb)])
            k.act(ucb[:, dc, c * CH:(c + 1) * CH], k.ps[pb][:, :], AF.Identity, bias=bdw[:, dc:dc + 1],
                  reads=[('ps', pb), 'conv_dwb'], writes=[('ucb', dc, c)])
    sq = [v3(k.b16(DC * CH), DC) for _ in range(2)]
    mu = [k.f32(CH) for _ in range(2)]
    va = [k.f32(CH) for _ in range(2)]
    nm = [k.f32(CH) for _ in range(2)]
    tn = [k.f32(CH) for _ in range(2)]
    for c in range(NCH):
        s = c % 2
        cs = slice(c * CH, (c + 1) * CH)
        k.tt(sq[s], ucb[:, :, cs], ucb[:, :, cs], ALU.mult, reads=[('ucb', dc, c) for dc in range(DC)], writes=[('csq', s)], eng='pool')
        for dc in range(DC):
            k.mm(k.ps[6][:, :], k.onesD, ucb[:, dc, cs], start=(dc == 0), stop=(dc == DC - 1), reads=[('ucb', dc, c), 'cst'], writes=[('ps', 6)])
        for dc in range(DC):
            k.mm(k.ps[7][:, :], k.onesD, sq[s][:, dc, :], start=(dc == 0), stop=(dc == DC - 1), reads=[('csq', s), 'cst'], writes=[('ps', 7)])
        k.copy(mu[s], k.ps[6][:, :], reads=[('ps', 6)], writes=[('mu', s)])
        k.tt(va[s], mu[s], mu[s], ALU.mult, reads=[('mu', s)], writes=[('va', s)])
        k.tt(va[s], k.ps[7][:, :], va[s], ALU.subtract, reads=[('ps', 7), ('va', s)], writes=[('va', s)])
        k.act(va[s], va[s], AF.Sqrt, bias=EPS, reads=[('va', s)], writes=[('va', s)])
        k.recip(va[s], va[s], reads=[('va', s)], writes=[('va', s)])
        k.stt(nm[s], mu[s], -1.0, va[s], ALU.mult, ALU.mult, reads=[('mu', s), ('va', s)], writes=[('nm', s)])
        for dc in range(DC):
            t = nextbank(k, 'tn', (0, 1))
            k.tt(tn[t], ucb[:, dc, cs], va[s], ALU.mult, reads=[('ucb', dc, c), ('va', s)], writes=[('tn', t)])
            k.tt(tn[t], tn[t], nm[s], ALU.add, reads=[('tn', t), ('nm', s)], writes=[('tn', t)])
            k.act(ucb[:, dc, cs], tn[t], AF.Silu, scale=lg[:, dc:dc + 1], bias=lb[:, dc:dc + 1],
                  reads=[('tn', t), 'conv_lng', 'conv_lnb'], writes=[('ucb', dc, c)])
    return m, ucb


def load_q(k):
    qin = k.din('qT', [D, NT], BF16)
    q = v3(k.b16(H * NT), H)
    for h in range(H):
        k.dma(q[:, h, :], qin[h * 128:(h + 1) * 128, :], writes=[('q', h, c) for c in range(NCH)])
    return q


def kv_loader(k, ktname='kT_full', vname='v_full', nslots=4):
    cfg = k.cfg
    kin = k.din(ktname, [D, cfg.S], BF16)
    vin = k.din(vname, [H, 128, cfg.NKT, 128], BF16)
    kb = [k.b16(cfg.S) for _ in range(2)]
    vb = [v3(k.b16(cfg.NKT * 128), cfg.NKT) for _ in range(2)]
    hw = cfg.S // 2
    for e in range(2):
        kb.append(k.xraw[:, (2 * e) * hw:(2 * e + 1) * hw].bitcast(BF16))
        vb.append(v3(k.xraw[:, (2 * e + 1) * hw:(2 * e + 2) * hw].bitcast(BF16), cfg.NKT))
    state = {'i': 0}

    def load(h, nkt):
        s = state['i'] % nslots
        state['i'] += 1
        k.dma(kb[s][:, 0:nkt * 128], kin[h * 128:(h + 1) * 128, 0:nkt * 128], writes=[('kb', s)])
        k.dma(vb[s][:, 0:nkt, :], vin[h, :, 0:nkt, :], writes=[('vb', s)])
        return s
    return kb, vb, load


def sb_core(k):
    cfg = k.cfg
    R = cfg.R
    m = k.mark()
    q = load_q(k)
    mq = k.mark()
    kb, vb, kvload = kv_loader(k)
    NU = 4 * R
    mkin = k.din('sbmask', [128, NCH, NU, CH], BF16)
    mk = v3(k.b16(NU * CH), NU)
    cc = k.din('sbc', [128, 256], BF16)
    cst = k.b16(256)
    k.dma(cst, cc[:, :], writes=['sbc'])
    negtri = cst[:, 0:128]
    NCHAIN = 2
    ef = [[k.f32(CH) for _ in range(2)] for _ in range(NCHAIN)]
    sp = [[k.b16(CH) for _ in range(2)] for _ in range(NCHAIN)]
    at = [[k.b16(CH) for _ in range(2)] for _ in range(NCHAIN)]
    spa = [[k.b16(CH) for _ in range(2)] for _ in range(NCHAIN)]
    ZB = [(0, 1), (2, 3)]
    OBK = [5, 6]
    CP = [0, 32]
    for lc in range(NCH):
        KM = cfg.KM[lc]
        k.dma(mk, mkin[:, lc, :, :], writes=['mk'])
        cs = slice(lc * CH, (lc + 1) * CH)
        tiles = list(range(KM - 1, -1, -1))
        n = len(tiles)
        for hp in range(H // NCHAIN):
            hs = [hp * NCHAIN + ci for ci in range(NCHAIN)]
            sl = [kvload(h, KM) for h in hs]
            zb = [[None] * n for _ in range(NCHAIN)]

            def stA(ci, i):
                h, s = hs[ci], sl[ci]
                kt = tiles[i]
                zb[ci][i] = ZB[ci][i % 2]
                z = zb[ci][i]
                k.mm(k.ps[z][:, :], kb[s][:, kt * 128:(kt + 1) * 128], q[:, h, cs],
                     reads=[('kb', s), ('q', h, lc)], writes=[('ps', z)])
                e = i % 2
                k.act(ef[ci][e], k.ps[z][:, :], AF.Exp, reads=[('ps', z)], writes=[('ef', ci, e)])
                k.act(sp[ci][e], ef[ci][e], AF.Ln, bias=1.0, reads=[('ef', ci, e)], writes=[('sp', ci, e)])
                u = kt - 4 * cfg.GMIN[lc]
                if u >= 0:
                    k.tt(sp[ci][e], sp[ci][e], mk[:, u, :], ALU.mult, reads=[('sp', ci, e), 'mk'], writes=[('sp', ci, e)])

            def stB(ci, i):
                z = zb[ci][i]
                e = i % 2
                k.mm(k.ps[z][:, :], negtri, sp[ci][e], start=False, stop=True, skip=True,
                     reads=[('sp', ci, e), 'sbc', ('ps', z)], writes=[('ps', z)])
                if i > 0:
                    c_ = (i - 1) % 2
                    k.mm(k.ps[z][:, :], cst[:, 128:256], spa[ci][c_], start=False, stop=True, skip=True,
                         reads=[('spa', ci, c_), 'sbc', ('ps', z)], writes=[('ps', z)])
                if i < n - 1:
                    if i == 0:
                        k.copy(spa[ci][0], sp[ci][e], reads=[('sp', ci, e)], writes=[('spa', ci, 0)])
                    else:
                        k.tt(spa[ci][i % 2], spa[ci][(i - 1) % 2], sp[ci][e], ALU.add,
                             reads=[('spa', ci, (i - 1) % 2), ('sp', ci, e)], writes=[('spa', ci, i % 2)])

            def stC(ci, i):
                h, s = hs[ci], sl[ci]
                kt = tiles[i]
                z = zb[ci][i]
                e = i % 2
                a = at[ci][e]
                k.act(a, k.ps[z][:, :], AF.Exp, reads=[('ps', z)], writes=[('at', ci, e)])
                u = kt - 4 * cfg.GMIN[lc]
                if u >= 0:
                    k.tt(a, a, mk[:, u, :], ALU.mult, reads=[('at', ci, e), 'mk'], writes=[('at', ci, e)])
                k.mm(k.ps[OBK[ci]][:, :], vb[s][:, kt, :], a, start=(i == 0), stop=(i == n - 1),
                     reads=[('vb', s), ('at', ci, e)], writes=[('ps', OBK[ci])])

            for i in range(n + 1):
                for ci in range(NCHAIN):
                    if i < n:
                        stA(ci, i)
                for ci in range(NCHAIN):
                    if 0 <= i - 1 < n:
                        stB(ci, i - 1)
                for ci in range(NCHAIN):
                    if 0 <= i - 1 < n:
                        stC(ci, i - 1)
            for ci in range(NCHAIN):
                k.act(q[:, hs[ci], cs], k.ps[OBK[ci]][:, :], AF.Copy, reads=[('ps', OBK[ci])], writes=[('q', hs[ci], lc)])
    return m, mq, q


def moba_core(k):
    cfg = k.cfg
    NB = cfg.S // 256
    m = k.mark()
    q = load_q(k)
    mq = k.mark()
    kb, vb, kvload = kv_loader(k)
    kown = k.din('kT_own', [D, NT], BF16)
    vown = k.din('v_own', [NT, D], BF16)
    kmin = k.din('kmean_full', [D, NB], F32)
    kmf = v3(k.f32(H * NB), H)
    kmb = v3(k.b16(H * NB), H)
    for h in range(H):
        k.dma(kmf[:, h, :], kmin[h * 128:(h + 1) * 128, :], writes=[('kmf', h)])
    k.copy(kmb, kmf, reads=[('kmf', h) for h in range(H)], writes=['kmb'])
    mvin = k.din('mvalid', [128, NCH, 4 * NB], F32)
    mv = v3(k.f32(NCH * 4 * NB), NCH)
    k.dma(mv, mvin[:, :, :], writes=['mv'])
    esin = k.din('esel', [NB, NB * 128], BF16)
    esel = k.b16(NB * 128, parts=NB)
    k.dma(esel, esin[:, :], writes=['esel'])
    omin = k.din('ownmask', [128, 4, CH], BF16)
    om = v3(k.b16(4 * CH), 4)
    k.dma(om, omin[:, :, :], writes=['om'])
    identf = k.f32(128)
    k.copy(identf, k.ident, reads=['cst'], writes=['identf'])
    gm = k.f32(4 * NB)
    m8 = k.f32(32)
    t1 = k.f32(4 * NB)
    mb = k.f32(4 * NB)
    NCHAIN = 2
    mT = [[k.b16(CH, parts=NB) for _ in range(2)] for _ in range(NCHAIN)]
    pt = [[k.b16(CH) for _ in range(3)] for _ in range(NCHAIN)]
    kl = [[k.b16(CH) for _ in range(2)] for _ in range(NCHAIN)]
    vl = [[v3(k.b16(4 * 128), 4) for _ in range(2)] for _ in range(NCHAIN)]
    rd = k.f32(CH)
    dacc = [k.f32(CH) for _ in range(NCHAIN)]
    dhi = [k.b16(CH) for _ in range(NCHAIN)]
    dlo = [k.b16(CH) for _ in range(NCHAIN)]
    ZB = [(0, 1), (2, 3)]
    OBK = [4, 5]
    DBK = [6, 7]
    it = 0
    for lc in range(NCH):
        KM = cfg.KM[lc]
        cs = slice(lc * CH, (lc + 1) * CH)
        for hp in range(H // NCHAIN):
            l = it % 2
            it += 1
            chains = []
            for ci in range(NCHAIN):
                h = hp * NCHAIN + ci
                s = kvload(h, KM)
                k.dma(kl[ci][l], kown[h * 128:(h + 1) * 128, cs], writes=[('kl', ci, l)])
                k.dma(vl[ci][l], vown[cs, h * 128:(h + 1) * 128].rearrange("(a p) d -> p a d", p=128), writes=[('vl', ci, l)])
                for u in range(4):
                    k.mm(k.ps[6][:, u * NB:(u + 1) * NB], q[:, h, lc * CH + u * 128:lc * CH + (u + 1) * 128], kmb[:, h, :],
                         start=(u == 0), stop=True, skip=(u > 0), reads=[('q', h, lc), 'kmb'], writes=[('ps', 6)])
                k.tt(gm, k.ps[6][:, 0:4 * NB], mv[:, lc, :], ALU.add, reads=[('ps', 6), 'mv'], writes=['gm'])
                for u in range(4):
                    k.P.op('dve', lambda e, o=m8[:, u * 8:(u + 1) * 8], i=gm[:, u * NB:(u + 1) * NB]: e.max(out=o, in_=i), ['gm'], [('m8', u)])
                thr = v3(m8, 4)[:, :, 2:3]
                k.ts(thr, thr, -1e29, None, ALU.max, reads=[('m8', u) for u in range(4)], writes=['thr'])
                for u in range(4):
                    k.ts(t1[:, u * NB:(u + 1) * NB], gm[:, u * NB:(u + 1) * NB], m8[:, u * 8 + 2:u * 8 + 3], None, ALU.is_ge,
                         reads=['gm', 'thr'], writes=[('t1', u)])
                k.ts(mb, t1, -NEG, NEG, ALU.mult, ALU.add, reads=[('t1', u) for u in range(4)], writes=['mb'])
                for u in range(4):
                    k.P.op('pe', lambda e, o=k.ps[7][0:NB, u * 128:(u + 1) * 128], i=mb[:, u * NB:(u + 1) * NB]:
                           e.transpose(o, i, identf), ['mb', 'identf'], [('ps', 7)])
                mt = mT[ci][l]
                mk_ = ('mT', ci, l)
                k.copy(mt, k.ps[7][0:NB, :], reads=[('ps', 7)], writes=[mk_])
                tiles = []
                for kt in range(KM):
                    nb_ = kt // 2
                    tiles.append((kb[s][:, kt * 128:(kt + 1) * 128], ('kb', s), esel[:, nb_ * 128:(nb_ + 1) * 128], mt, ['esel', mk_],
                                  vb[s][:, kt, :], ('vb', s)))
                for i2 in range(4):
                    tiles.append((kl[ci][l][:, i2 * 128:(i2 + 1) * 128], ('kl', ci, l), k.ident, om[:, i2, :], ['cst', 'om'],
                                  vl[ci][l][:, i2, :], ('vl', ci, l)))
                chains.append(dict(h=h, tiles=tiles, pend=None))
            nt_ = KM + 4
            for i in range(nt_ + 1):
                for ci, chn in enumerate(chains):
                    h = chn['h']
                    if i < nt_:
                        kT_, kkey, ml, mr, mkeys, vT_, vkey = chn['tiles'][i]
                        zb = ZB[ci][i % 2]
                        k.mm(k.ps[zb][:, :], kT_, q[:, h, cs], start=True, stop=False, reads=[kkey, ('q', h, lc)], writes=[('ps', zb)])
                        k.mm(k.ps[zb][:, :], ml, mr, start=False, stop=True, reads=mkeys, writes=[('ps', zb)])
                        p = i % 3
                        k.act(pt[ci][p], k.ps[zb][:, :], AF.Exp, scale=SCALE, reads=[('ps', zb)], writes=[('pt', ci, p)])
                        chn['cur'] = (i, p, vT_, vkey)
                    else:
                        chn['cur'] = None
                for ci, chn in enumerate(chains):
                    if chn['pend'] is not None:
                        pi, pp_, pv_, pvk = chn['pend']
                        ob = OBK[ci]
                        k.mm(k.ps[ob][:, :], pv_, pt[ci][pp_], start=(pi == 0), stop=(pi == nt_ - 1),
                             reads=[pvk, ('pt', ci, pp_)], writes=[('ps', ob)])
                        if pi == 0:
                            k.copy(dacc[ci], pt[ci][pp_], reads=[('pt', ci, pp_)], writes=[('dacc', ci)])
                        else:
                            k.tt(dacc[ci], dacc[ci], pt[ci][pp_], ALU.add, reads=[('pt', ci, pp_), ('dacc', ci)], writes=[('dacc', ci)])
                    chn['pend'] = chn['cur']
            for ci, chn in enumerate(chains):
                h = chn['h']
                ob, db = OBK[ci], DBK[ci]
                k.copy(dhi[ci], dacc[ci], reads=[('dacc', ci)], writes=[('dhi', ci)])
                k.tt(dlo[ci], dacc[ci], dhi[ci], ALU.subtract, reads=[('dacc', ci), ('dhi', ci)], writes=[('dlo', ci)])
                k.mm(k.ps[db][:, :], k.ones, dhi[ci], start=True, stop=False, reads=['cst', ('dhi', ci)], writes=[('ps', db)])
                k.mm(k.ps[db][:, :], k.ones, dlo[ci], start=False, stop=True, reads=['cst', ('dlo', ci)], writes=[('ps', db)])
                k.recip(rd, k.ps[db][:, :], reads=[('ps', db)], writes=['rd'])
                k.tt(q[:, h, cs], k.ps[ob][:, :], rd, ALU.mult, reads=[('ps', ob), 'rd'], writes=[('q', h, lc)])
    return m, mq, q


NSA_IN = 2584


def nsa_pre(k):
    m = k.mark()
    hn = v3(k.b16(DC * NT), DC)
    rmsnorm(k, 'g_an0', hn)
    k.din('w_in0', [D, NSA_IN], F32)
    qo = k.dout('qT', [D, NT], BF16)
    fm_out = {1024: k.dout('kcT', [256, NT], BF16), 1280: k.dout('vcT', [256, NT], BF16),
              1536: k.dout('ksT', [256, NT], BF16), 2048: k.dout('kwT', [256, NT], BF16)}
    vso = k.dout('vs', [NT, 256], BF16)
    vwo = k.dout('vw', [NT, 256], BF16)
    go = k.dout('gT', [24, NT], F32)
    normrope_setup(k)
    gq = load_vec(k, 'nsa_qn', 1)
    gk = load_vec(k, 'nsa_kn', 3)
    stage = [k.b16(CH) for _ in range(3)]
    gst = [k.f32(CH, parts=24) for _ in range(2)]

    def cons(col0, c, pb):
        cs = slice(c * CH, (c + 1) * CH)
        if col0 == 2560:
            s = nextbank(k, 'gst', (0, 1))
            k.act(gst[s], k.ps[pb][0:24, :], AF.Sigmoid, reads=[('ps', pb)], writes=[('gst', s)])
            k.dma(go[:, cs], gst[s], reads=[('gst', s)], writes=[k.key('o')])
            return
        s = nextbank(k, 'stg', (0, 1, 2))
        if col0 < 1024:
            normrope(k, pb, c, gq[:, 0:1], 'nsa_qn', stage[s], [('stg', s)])
            dst = qo[col0:col0 + 128, cs]
        else:
            base = max(b for b in fm_out if b <= col0)
            dst = fm_out[base][col0 - base:col0 - base + 128, cs]
            if base == 1280:
                k.act(stage[s], k.ps[pb][:, :], AF.Copy, reads=[('ps', pb)], writes=[('stg', s)])
            else:
                gi = {1024: 0, 1536: 1, 2048: 2}[base]
                normrope(k, pb, c, gk[:, gi:gi + 1], 'nsa_kn', stage[s], [('stg', s)])
        k.dma(dst, stage[s], reads=[('stg', s)], writes=[k.key('o')])

    proj_fm(k, 'w_in0', [(0, 512), (512, 512), (1024, 512), (1536, 256), (2048, 256), (2560, 24)], hn, cons)
    vst = [k.b16(256) for _ in range(2)]
    for (c0, dst) in ((1792, vso), (2304, vwo)):
        def consv(tt, pb, dst=dst):
            s = nextbank(k, 'vst', (0, 1))
            k.copy(vst[s], k.ps[pb][:, 0:256], reads=[('ps', pb)], writes=[('vst', s)])
            k.dma(dst[tt * 128:(tt + 1) * 128, :], vst[s], reads=[('vst', s)], writes=[k.key('o')])
        proj_tm(k, 'w_in0', c0, 256, hn, consv)
    k.P.barrier()
    k.reset(m)


def nsa_core(k, ob):
    cfg = k.cfg
    S = cfg.S
    NSEL = S // 64
    NCMP = S // 16 - 1
    NCC = (NCMP + 127) // 128
    NQT = NT // 128
    m = k.mark()
    qin = k.din('qT', [D, NT], BF16)
    qsb = k.b16(NQT * H * 128).rearrange("p (a b c) -> p a b c", a=NQT, b=H)
    for h in range(H):
        k.dma(qsb[:, :, h, :], qin[h * 128:(h + 1) * 128, :].rearrange("p (a t) -> p a t", t=128), writes=[('q', h)])
    gin = k.din('gT', [24, NT], F32)
    gb = k.b16(NT, parts=24)
    k.dma(gb, gin[:, :], writes=['gb'], eng='pool')
    selg = k.b16(24 * 128, parts=24)
    k.dma(selg, k.din('selg', [24, 24 * 128], BF16)[:, :], writes=['selg'])
    G = k.b16(S, parts=NSEL)
    k.dma(G, k.din('G', [NSEL, S], BF16)[:, :], writes=['G'])
    Ov = v3(k.b16(NCC * NSEL), NCC)
    k.dma(Ov, k.din('Ov', [128, NCC, NSEL], BF16)[:, :, :], writes=['Ov'])
    tri = k.b16(256)
    k.dma(tri, k.din('tri', [128, 256], BF16)[:, :], writes=['tri'])
    triT = tri[:, 0:128]
    tri2 = tri[:, 128:256]
    McT = k.b16(NQT * NCC * 128).rearrange("p (a b c) -> p a b c", a=NQT, b=NCC)
    k.dma(McT, k.din('McT', [128, NQT, NCC, 128], BF16)[:, :, :, :], writes=['McT'])
    val = v3(k.b16(NQT * NSEL), NQT)
    bon = v3(k.b16(NQT * NSEL), NQT)
    pneg = v3(k.b16(NQT * NSEL), NQT)
    k.dma(val, k.din('sval', [128, NQT, NSEL], BF16)[:, :, :], writes=['val'])
    k.dma(bon, k.din('sbon', [128, NQT, NSEL], BF16)[:, :, :], writes=['bon'])
    k.dma(pneg, k.din('spneg', [128, NQT, NSEL], BF16)[:, :, :], writes=['pneg'])
    pcr = v3(k.b16(NCH * 128, parts=1), NCH)
    k.dma(pcr, k.din('pcrow', [1, NCH, 128], BF16)[:, :, :], writes=['pcr'])
    identf = k.f32(128)
    k.copy(identf, k.ident, reads=['cst'], writes=['identf'])
    posT = v3(load_vec(k, 'cmp_posT', 2 * 32), 2)
    kcmpT = [k.b16(NCC * 128) for _ in range(2)]
    vcmp = [v3(k.b16(NCC * 128), NCC) for _ in range(2)]
    onescol = k.ones[:, 0:1]
    ones1 = k.ones[0:1, :]
    m2 = k.mark()
    rawin = [k.din('kcT_full', [256, S], BF16), k.din('vcT_full', [256, S], BF16)]
    wcin = k.din('w_cmp', [2, 32, 128, 128], F32)
    raw = [k.b16(S) for _ in range(2)]
    wc = [v3(k.b16(32 * 128), 32) for _ in range(2)]
    tmp = [k.b16(NCC * 128) for _ in range(4)]
    for t_ in range(4):
        k.memset(tmp[t_], 0.0, writes=[('tmp', t_)])
    it = 0
    for kv in range(2):
        k.dma(wc[kv], wcin[kv].rearrange("l d e -> d l e"), writes=[('wc', kv)], eng='pool')
        for grp in range(2):
            r = it % 2
            it += 1
            k.dma(raw[r], rawin[kv][grp * 128:(grp + 1) * 128, :], writes=[('raw', r)])
            pb = 2 + r
            for l in range(32):
                t_ = nextbank(k, 'tmp', (0, 1, 2, 3))
                k.ts(tmp[t_][:, 0:NCMP], raw[r][:, l:l + 16 * (NCMP - 1) + 1:16], posT[:, kv, l:l + 1], None, ALU.add,
                     reads=[('raw', r), 'cmp_posT'], writes=[('tmp', t_)])
                if kv == 0:
                    k.mm(k.ps[pb][:, 0:NCC * 128], wc[kv][:, l, :], tmp[t_], start=(l == 0), stop=(l == 31),
                         reads=[('wc', kv), ('tmp', t_)], writes=[('ps', pb)])
                else:
                    for c in range(NCC):
                        k.mm(k.ps[pb][:, c * 128:(c + 1) * 128], tmp[t_][:, c * 128:(c + 1) * 128], wc[kv][:, l, :],
                             start=(l == 0 and c == 0), stop=True, skip=not (l == 0 and c == 0),
                             reads=[('wc', kv), ('tmp', t_)], writes=[('ps', pb)])
            if kv == 0:
                k.copy(kcmpT[grp], k.ps[pb][:, 0:NCC * 128], reads=[('ps', pb)], writes=[('kcmp', grp)])
            else:
                k.copy(vcmp[grp], v3(k.ps[pb][:, 0:NCC * 128], NCC), reads=[('ps', pb)], writes=[('vcmp', grp)])
    k.P.barrier()
    k.reset(m2)
    ksin = k.din('ksT_full', [256, S], BF16)
    vsin = k.din('vs_full', [2, 128, cfg.NKT, 128], BF16)
    kslin = k.din('ksT_own', [256, NT], BF16)
    vslin = k.din('vs_own', [NT, 256], BF16)
    kwpin = k.din('kw_pack', [256, NCH, 2 * CH], BF16)
    vwpin = k.din('vw_pack', [2, 128, NCH, 8, 128], BF16)
    ks = k.b16(S)
    vs = v3(k.b16(cfg.NKT * 128), cfg.NKT)
    ksl = k.b16(NT)
    vsl = v3(k.b16(NQT * 128), NQT)
    kwp = v3(k.b16(NCH * 2 * CH), NCH)
    vwp = k.b16(NCH * 8 * 128).rearrange("p (a b c) -> p a b c", a=NCH, b=8)
    Pc = [k.b16(CH) for _ in range(NCC)]
    pt = [k.b16(CH) for _ in range(3)]
    OB = [k.f32(CH) for _ in range(3)]
    rr_ = k.f32(CH)
    rd4 = k.f32(4)
    imp = k.f32(NSEL)
    imp2 = k.f32(NSEL)
    imp3 = k.f32(NSEL)
    m8 = k.f32(16)
    mbs = k.f32(NSEL)
    mT = k.b16(128, parts=NSEL)
    osum = k.f32(CH)
    obank = [2, 3]
    dbank = [4, 5]
    br = 0

    dacc = k.f32(CH)
    dhi = k.b16(CH)
    dlo = k.b16(CH)

    def run_tiles(tiles, Q, qk, o_b, d_b):
        n = len(tiles)
        pend = None
        for i in range(n + 1):
            cur = None
            if i < n:
                kT_, kkey, masks, vT_, vkey, dst = tiles[i]
                zb = nextbank(k, 'z', (0, 1))
                k.mm(k.ps[zb][:, :], kT_, Q, start=True, stop=True, reads=[kkey] + qk, writes=[('ps', zb)])
                for (ml, mr, mkeys, mode) in masks:
                    if mode == 'hg4':
                        for hg in range(4):
                            k.mm(k.ps[zb][:, hg * 128:(hg + 1) * 128], ml, mr, start=False, stop=True, skip=True,
                                 reads=mkeys, writes=[('ps', zb)])
                    else:
                        k.mm(k.ps[zb][:, :], ml, mr, start=False, stop=True, skip=True, reads=mkeys, writes=[('ps', zb)])
                if dst is None:
                    p = nextbank(k, 'pt', (0, 1, 2))
                    dst = (pt[p], ('pt', p))
                k.act(dst[0], k.ps[zb][:, :], AF.Exp, scale=SCALE, reads=[('ps', zb)], writes=[dst[1]])
                cur = (i, dst, vT_, vkey)
            if pend is not None:
                pi, pd, pv_, pvk = pend
                k.mm(k.ps[o_b][:, :], pv_, pd[0], start=(pi == 0), stop=(pi == n - 1), reads=[pvk, pd[1]], writes=[('ps', o_b)])
                if pi == 0:
                    k.copy(dacc, pd[0], reads=[pd[1]], writes=['dacc'])
                else:
                    k.tt(dacc, dacc, pd[0], ALU.add, reads=[pd[1], 'dacc'], writes=['dacc'])
            pend = cur
        k.copy(dhi, dacc, reads=['dacc'], writes=['dhi'])
        k.tt(dlo, dacc, dhi, ALU.subtract, reads=['dacc', 'dhi'], writes=['dlo'])
        k.mm(k.ps[d_b][:, :], k.ones, dhi, start=True, stop=False, reads=['cst', 'dhi'], writes=[('ps', d_b)])
        k.mm(k.ps[d_b][:, :], k.ones, dlo, start=False, stop=True, reads=['cst', 'dlo'], writes=[('ps', d_b)])

    def finish_branch(o_b, d_b, dst):
        k.ts(rr_, k.ps[d_b][:, :], 1e-30, None, ALU.max, reads=[('ps', d_b)], writes=['rr'])
        k.recip(rr_, rr_, reads=['rr'], writes=['rr'])
        k.tt(dst[0], k.ps[o_b][:, :], rr_, ALU.mult, reads=[('ps', o_b), 'rr'], writes=[dst[1]])

    for grp in range(2):
        gs = slice(grp * 128, (grp + 1) * 128)
        k.dma(ks, ksin[gs, :], writes=['ks'])
        k.dma(vs, vsin[grp], writes=['vs'])
        k.dma(ksl, kslin[gs, :], writes=['ksl'])
        k.dma(vsl, vslin[:, gs].rearrange("(a p) d -> p a d", p=128), writes=['vsl'])
        k.dma(kwp, kwpin[gs, :, :], writes=['kwp'])
        k.dma(vwp, vwpin[grp], writes=['vwp'])
        for qt in range(NQT):
            lc, u = qt // 4, qt % 4
            Q = qsb[:, qt, grp * 4:(grp + 1) * 4, :].rearrange("p a b -> p (a b)")
            qk = [('q', grp * 4 + hg) for hg in range(4)]
            o_b, d_b = obank[br % 2], dbank[br % 2]
            br += 1
            tiles = [(kcmpT[grp][:, c * 128:(c + 1) * 128], ('kcmp', grp), [(k.ident, McT[:, qt, c, :], ['cst', 'McT'], 'hg4')],
                      vcmp[grp][:, c, :], ('vcmp', grp), (Pc[c], ('Pc', c))) for c in range(NCC)]
            run_tiles(tiles, Q, qk, o_b, d_b)
            for hg in range(4):
                for c in range(NCC):
                    first = (hg == 0 and c == 0)
                    k.mm(k.ps[6][:, hg * NSEL:(hg + 1) * NSEL], Pc[c][:, hg * 128:(hg + 1) * 128], Ov[:, c, :],
                         start=first, stop=True, skip=not first, reads=[('Pc', c), 'Ov'], writes=[('ps', 6)])
            for hg in range(4):
                for c in range(NCC):
                    first = (hg == 0 and c == 0)
                    k.mm(k.ps[7][:, hg:hg + 1], Pc[c][:, hg * 128:(hg + 1) * 128], onescol,
                         start=first, stop=True, skip=not first, reads=[('Pc', c), 'cst'], writes=[('ps', 7)])
            k.ts(rd4, k.ps[7][:, 0:4], 1e-30, None, ALU.max, reads=[('ps', 7)], writes=['rd4'])
            k.recip(rd4, rd4, reads=['rd4'], writes=['rd4'])
            k.ts(imp, k.ps[6][:, 0:NSEL], rd4[:, 0:1], None, ALU.mult, reads=[('ps', 6), 'rd4'], writes=['imp'])
            for hg in range(1, 4):
                k.stt(imp, k.ps[6][:, hg * NSEL:(hg + 1) * NSEL], rd4[:, hg:hg + 1], imp, ALU.mult, ALU.add,
                      reads=[('ps', 6), 'rd4', 'imp'], writes=['imp'])
            k.tt(imp2, imp, val[:, qt, :], ALU.mult, reads=['imp', 'val'], writes=['imp2'])
            k.tt(imp2, imp2, bon[:, qt, :], ALU.add, reads=['imp2', 'bon'], writes=['imp2'])
            k.P.op('dve', lambda e: e.max(out=m8[:, 0:8], in_=imp2), ['imp2'], ['m8a'])
            k.P.op('dve', lambda e: e.match_replace(out=imp3, in_to_replace=m8[:, 0:8], in_values=imp2, imm_value=-1e9),
                   ['imp2', 'm8a'], ['imp3'])
            k.P.op('dve', lambda e: e.max(out=m8[:, 8:16], in_=imp3), ['imp3'], ['m8b'])
            k.ts(mbs, imp2, m8[:, 15:16], -NEG, ALU.is_ge, ALU.mult, reads=['imp2', 'm8b'], writes=['mbs'])
            k.stt(mbs, mbs, NEG, pneg[:, qt, :], ALU.add, ALU.add, reads=['mbs', 'pneg'], writes=['mbs'])
            k.P.op('pe', lambda e: e.transpose(k.ps[7][0:NSEL, 128:256], mbs, identf), ['mbs', 'identf'], [('ps', 7)])
            k.copy(mT, k.ps[7][0:NSEL, 128:256], reads=[('ps', 7)], writes=['mT'])
            finish_branch(o_b, d_b, (OB[0], ('OB', 0)))
            o_b, d_b = obank[br % 2], dbank[br % 2]
            br += 1
            KM = cfg.KM[lc]
            tiles = [(ks[:, kt * 128:(kt + 1) * 128], 'ks', [(G[:, kt * 128:(kt + 1) * 128], mT, ['G', 'mT'], 'hg4')],
                      vs[:, kt, :], 'vs', None) for kt in range(KM)]
            tiles.append((ksl[:, qt * 128:(qt + 1) * 128], 'ksl', [(k.ident, triT, ['cst', 'tri'], 'hg4')], vsl[:, qt, :], 'vsl', None))
            run_tiles(tiles, Q, qk, o_b, d_b)
            finish_branch(o_b, d_b, (OB[1], ('OB', 1)))
            o_b, d_b = obank[br % 2], dbank[br % 2]
            br += 1
            tiles = []
            for i in range(5):
                w = u + i
                masks = []
                if i == 0:
                    masks.append((k.ident, tri2, ['cst', 'tri'], 'hg4'))
                if i == 4:
                    masks.append((k.ident, triT, ['cst', 'tri'], 'hg4'))
                if w < 4:
                    masks.append((ones1, pcr[:, lc, :], ['cst', 'pcr'], 'hg4'))
                tiles.append((kwp[:, lc, w * 128:(w + 1) * 128], 'kwp', masks, vwp[:, lc, w, :], 'vwp', None))
            run_tiles(tiles, Q, qk, o_b, d_b)
            finish_branch(o_b, d_b, (OB[2], ('OB', 2)))
            for b in range(3):
                for hg in range(4):
                    r0 = (b * 8 + grp * 4 + hg) * 128
                    k.mm(k.ps[6][:, hg * 128:(hg + 1) * 128], selg[:, r0:r0 + 128], gb[:, qt * 128:(qt + 1) * 128],
                         start=(hg == 0), stop=True, skip=(hg > 0), reads=['selg', 'gb'], writes=[('ps', 6)])
                k.tt(OB[b], OB[b], k.ps[6][:, :], ALU.mult, reads=[('OB', b), ('ps', 6)], writes=[('OB', b)])
            k.tt(osum, OB[0], OB[1], ALU.add, reads=[('OB', 0), ('OB', 1)], writes=['osum'])
            k.tt(ob[:, grp * 4:(grp + 1) * 4, qt * 128:(qt + 1) * 128], v3(osum, 4), v3(OB[2], 4), ALU.add,
                 reads=['osum', ('OB', 2)], writes=[('ob', grp * 4 + hg, lc) for hg in range(4)])
    k.P.barrier()
    k.reset(m)


ARENA_WORDS = 53000


def attn_tail(k, wname, m, mq, q, li, nxt):
    k.P.barrier()
    k.reset(mq)
    load_x(k, alloc=False)
    out_proj_residual(k, wname, q, lambda h, c: [('q', h, c)])
    k.P.barrier()
    k.reset(m)
    mlp(k, li)
    if nxt is not None:
        nxt(k)
    store_x(k)


def build_launch(li, cfg):
    st = ExitStack()
    k = K(cfg, st)
    k.start(ARENA_WORDS)
    load_consts(k)
    if li == 0:
        load_x(k)
        nsa_pre(k)
    elif li == 1:
        ob = v3(k.b16(H * NT), H)
        nsa_core(k, ob)
        load_x(k)
        mo = k.mark()
        k.din('w_out0', [D, D], F32)
        out_proj_residual(k, 'w_out0', ob, lambda h, c: [('ob', h, c)])
        k.P.barrier()
        k.reset(mo)
        mlp(k, 0)
        qkv_pre(k, 1, False)
        store_x(k)
    elif li == 2:
        alloc_x(k)
        m, mq, q = sb_core(k)
        k.din('w_out1', [D, D], F32)
        attn_tail(k, 'w_out1', m, mq, q, 1, conv_pre)
    elif li == 3:
        load_x(k)
        m, ucb = conv_core(k)
        k.din('w_out2', [D, D], F32)
        out_proj_residual(k, 'w_out2', ucb, lambda h, c: [('ucb', h, c)])
        k.P.barrier()
        k.reset(m)
        mlp(k, 2)
        qkv_pre(k, 3, True)
        store_x(k)
    elif li == 4:
        alloc_x(k)
        m, mq, q = moba_core(k)
        k.din('w_out3', [D, D], F32)
        attn_tail(k, 'w_out3', m, mq, q, 3, None)
    nc = k.finish()
    return nc, k, st


def pp(v, n):
    return np.ascontiguousarray(np.asarray(v, np.float32).reshape(n, 128).T)


def kernel_impl(inputs, cfg, runner, upto=5, dbg=None):
    R, B, S, NC = cfg.R, cfg.B, cfg.S, cfg.NC
    NKT = cfg.NKT
    x = np.asarray(inputs['x'], np.float32)
    gch = [cfg.gchunks(c % R) for c in range(NC)]
    bat = [c // R for c in range(NC)]
    pos = [np.concatenate([np.arange(g * CH, (g + 1) * CH) for g in gch[c]]) for c in range(NC)]
    cst = host_consts()
    P = {n: np.asarray(v, np.float32) for n, v in inputs.items() if n != 'x'}

    def run(li, in_maps):
        nc, k, st = build_launch(li, cfg)
        for im in in_maps:
            assert set(im.keys()) == set(k.in_specs.keys()), (sorted(set(im.keys()) ^ set(k.in_specs.keys())))
            for n, (shp, dt) in k.in_specs.items():
                want = bf if dt == BF16 else np.float32
                a = im[n]
                if a.dtype != want:
                    a = a.astype(want)
                assert tuple(a.shape) == tuple(shp), (n, a.shape, shp)
                im[n] = np.ascontiguousarray(a)
        res = runner(nc, in_maps)
        st.close()
        return [{n[2:]: v for n, v in r.items()} for r in res]

    def to_global(res, name, axis):
        out = []
        for b in range(B):
            parts = [None] * (NCH * R)
            for c in range(NC):
                if bat[c] != b:
                    continue
                a = np.asarray(res[c][name])
                for lc, g in enumerate(gch[c]):
                    sl = [slice(None)] * a.ndim
                    sl[axis] = slice(lc * CH, (lc + 1) * CH)
                    parts[g] = a[tuple(sl)]
            out.append(np.concatenate(parts, axis=axis))
        return out

    def vlay(vg, nh):
        return np.ascontiguousarray(vg.reshape(NKT, 128, nh, 128).transpose(2, 1, 0, 3))

    inv_freq = (np.float32(10000.0) ** (-np.arange(64, dtype=np.float32) / np.float32(64))).astype(np.float32)
    cosT, sinT = [], []
    for c in range(NC):
        ang = pos[c].astype(np.float32)[:, None] * inv_freq[None, :]
        cosT.append(np.ascontiguousarray(np.concatenate([np.cos(ang), np.cos(ang)], axis=1).T.astype(np.float32)))
        sinT.append(np.ascontiguousarray(np.concatenate([np.sin(ang), np.sin(ang)], axis=1).T.astype(np.float32)))

    def mlpw(li):
        return {'g_mlp%d' % li: pp(P['mlp_norm'][li], DC), 'w_up%d' % li: P['mlp_w_up'][li], 'w_dn%d' % li: P['mlp_w_down'][li]}

    xT = [np.ascontiguousarray(x[bat[c], pos[c], :].T) for c in range(NC)]
    ims = [dict(cst=cst, xT=xT[c], g_an0=pp(P['attn_norm'][0], DC), w_in0=P['nsa_w_in'][0], cosT=cosT[c], sinT=sinT[c],
                nsa_qn=pp(P['nsa_q_norm'][0], 1), nsa_kn=np.ascontiguousarray(P['nsa_k_norm'][0].T)) for c in range(NC)]
    r0 = run(0, ims)
    if dbg is not None:
        dbg['r0'] = r0
    if upto <= 1:
        return r0
    NSEL = S // 64
    NCMP = S // 16 - 1
    NCC = (NCMP + 127) // 128
    NQT = NT // 128
    kcf, vcf, ksf, kwf = [to_global(r0, n, 1) for n in ('kcT', 'vcT', 'ksT', 'kwT')]
    vsf, vwf = [to_global(r0, n, 0) for n in ('vs', 'vw')]
    jj = np.arange(NSEL)
    Gm = (np.arange(S)[None, :] // 64 == jj[:, None]).astype(np.float32)
    nn = np.arange(NCC * 128)
    ovl = ((16 * nn[:, None] < 64 * jj[None, :] + 64) & (16 * nn[:, None] + 32 > 64 * jj[None, :]) & (nn[:, None] < NCMP)).astype(np.float32)
    Ov = np.ascontiguousarray(ovl.reshape(NCC, 128, NSEL).transpose(1, 0, 2))
    ss, tt = np.meshgrid(np.arange(128), np.arange(128), indexing='ij')
    tri = np.concatenate([np.where(ss <= tt, 0.0, NEG), np.where(ss > tt, 0.0, NEG)], axis=1).astype(np.float32)
    selg = np.zeros((24, 24 * 128), np.float32)
    for r_ in range(24):
        selg[r_, r_ * 128:(r_ + 1) * 128] = 1.0
    ims = []
    for c in range(NC):
        b = bat[c]
        McT = np.zeros((128, NQT, NCC, 128), np.float32)
        sval = np.zeros((128, NQT, NSEL), np.float32)
        sbon = np.zeros((128, NQT, NSEL), np.float32)
        spneg = np.zeros((128, NQT, NSEL), np.float32)
        pcrow = np.zeros((1, NCH, 128), np.float32)
        kwp = np.zeros((256, NCH, 2 * CH), np.float32)
        vwp = np.zeros((2, 128, NCH, 8, 128), np.float32)
        for lc, g in enumerate(gch[c]):
            if g == 0:
                pcrow[0, lc, :] = NEG
            lo = (g - 1) * CH
            if g > 0:
                kwp[:, lc, 0:CH] = kwf[b][:, lo:lo + CH].astype(np.float32)
                vprev = vwf[b][lo:lo + CH].astype(np.float32)
            else:
                vprev = np.zeros((CH, 256), np.float32)
            kwp[:, lc, CH:] = kwf[b][:, g * CH:(g + 1) * CH].astype(np.float32)
            vboth = np.concatenate([vprev, vwf[b][g * CH:(g + 1) * CH].astype(np.float32)], axis=0)
            vwp[:, :, lc] = vboth.reshape(8, 128, 2, 128).transpose(2, 1, 0, 3)
            for u in range(4):
                qt = lc * 4 + u
                T = 4 * g + u
                tl = np.arange(128)
                tg = 128 * T + tl
                n_ = np.arange(NCC * 128).reshape(NCC, 128)
                vis = (16 * n_[:, :, None] + 31 <= tg[None, None, :]) & (n_[:, :, None] < NCMP)
                McT[:, qt, :, :] = np.where(vis, 0.0, NEG).transpose(1, 0, 2)
                cur = tg // 64
                le = jj[None, :] <= cur[:, None]
                forced = (jj[None, :] == 0) | (jj[None, :] == cur[:, None]) | (jj[None, :] == cur[:, None] - 1)
                sval[:, qt, :] = le
                sbon[:, qt, :] = np.where(le, 1000.0 * forced, -1.0)
                spneg[:, qt, :] = np.where(jj[None, :] >= 2 * T, NEG, 0.0)
        ksown = np.asarray(r0[c]['ksT'])
        ims.append(dict(cst=cst, qT=np.asarray(r0[c]['qT']), gT=np.asarray(r0[c]['gT']), selg=selg, G=Gm, Ov=Ov, tri=tri,
                        McT=McT, sval=sval, sbon=sbon, spneg=spneg, pcrow=pcrow,
                        cmp_posT=np.ascontiguousarray(P['nsa_cmp_pos'][0].transpose(2, 0, 1).reshape(128, 64)),
                        kcT_full=kcf[b], vcT_full=vcf[b], w_cmp=P['nsa_w_cmp'][0], ksT_full=ksf[b], vs_full=vlay(vsf[b], 2),
                        ksT_own=ksown, vs_own=np.asarray(r0[c]['vs']), kw_pack=kwp, vw_pack=vwp,
                        xT=xT[c], w_out0=P['nsa_w_out'][0], g_an1=pp(P['attn_norm'][1], DC), w_in1=P['sb_w_in'][0], **mlpw(0)))
    r1 = run(1, ims)
    if dbg is not None:
        dbg['r1'] = r1
    if upto <= 2:
        return r1
    ktf = to_global(r1, 'kT', 1)
    vf = to_global(r1, 'v', 0)
    js, s_ = np.meshgrid(np.arange(128), np.arange(128), indexing='ij')
    sbc = np.zeros((128, 256), np.float32)
    sbc[:, 0:128] = np.where(js >= s_, -1.0, 0.0)
    sbc[:, 128:256] = -1.0
    ims = []
    for c in range(NC):
        b = bat[c]
        sbm = np.zeros((128, NCH, 4 * R, CH), np.float32)
        for lc, g in enumerate(gch[c]):
            for i in range(4 * R):
                kt = 4 * cfg.GMIN[lc] + i
                sg_ = 128 * kt + np.arange(128)
                tg = CH * g + np.arange(CH)
                sbm[:, lc, i, :] = sg_[:, None] < tg[None, :]
        ims.append(dict(cst=cst, xT=np.asarray(r1[c]['xT_out']), qT=np.asarray(r1[c]['qT']), kT_full=ktf[b], v_full=vlay(vf[b], H),
                        sbmask=sbm, sbc=sbc, w_out1=P['sb_w_out'][0], g_an2=pp(P['attn_norm'][2], DC), conv_w_in=P['conv_w_in'][0], **mlpw(1)))
    r2 = run(2, ims)
    if dbg is not None:
        dbg['r2'] = r2
    if upto <= 3:
        return r2
    uf = to_global(r2, 'uT', 1)
    ims = []
    for c in range(NC):
        b = bat[c]
        uE = np.zeros((D, NCH, 32 + CH), np.float32)
        for lc, g in enumerate(gch[c]):
            if g > 0:
                uE[:, lc, 0:32] = uf[b][:, g * CH - 32:g * CH].astype(np.float32)
            uE[:, lc, 32:] = uf[b][:, g * CH:(g + 1) * CH].astype(np.float32)
        dw = P['conv_dw_w'][0]
        conv_dw = np.ascontiguousarray(dw.reshape(31, DC, 128).transpose(2, 1, 0).reshape(128, DC * 31))
        ims.append(dict(cst=cst, xT=np.asarray(r2[c]['xT_out']), uE=uE, conv_dw=conv_dw, conv_dwb=pp(P['conv_dw_b'][0], DC),
                        conv_lng=pp(P['conv_ln_g'][0], DC), conv_lnb=pp(P['conv_ln_b'][0], DC), w_out2=P['conv_w_out'][0],
                        g_an3=pp(P['attn_norm'][3], DC), w_in3=P['moba_w_in'][0], cosT=cosT[c], sinT=sinT[c],
                        moba_qn=pp(P['moba_q_norm'][0], 1), moba_kn=pp(P['moba_k_norm'][0], 1), **mlpw(2)))
    r3 = run(3, ims)
    if dbg is not None:
        dbg['r3'] = r3
    if upto <= 4:
        return r3
    NB = S // 256
    ktf = to_global(r3, 'kT', 1)
    vf = to_global(r3, 'v', 0)
    kmf = []
    for b in range(B):
        km = np.zeros((D, NB), np.float32)
        for c in range(NC):
            if bat[c] == b:
                a = np.asarray(r3[c]['kmean'])
                for lc, g in enumerate(gch[c]):
                    km[:, 2 * g:2 * g + 2] = a[:, 2 * lc:2 * lc + 2]
        kmf.append(km)
    esel = np.zeros((NB, NB * 128), np.float32)
    for n_ in range(NB):
        esel[n_, n_ * 128:(n_ + 1) * 128] = 1.0
    om = np.zeros((128, 4, CH), np.float32)
    for i2 in range(4):
        sl_ = 128 * i2 + np.arange(128)
        tl = np.arange(CH)
        ok = (i2 // 2 == tl[None, :] // 256) & (sl_[:, None] <= tl[None, :])
        om[:, i2, :] = np.where(ok, 0.0, NEG)
    ims = []
    for c in range(NC):
        b = bat[c]
        mv = np.zeros((128, NCH, 4 * NB), np.float32)
        for lc, g in enumerate(gch[c]):
            for u in range(4):
                cur = 2 * g + u // 2
                mv[:, lc, u * NB:(u + 1) * NB] = np.where(np.arange(NB) < cur, 0.0, -1e30)[None, :]
        ims.append(dict(cst=cst, xT=np.asarray(r3[c]['xT_out']), qT=np.asarray(r3[c]['qT']), kT_full=ktf[b], v_full=vlay(vf[b], H),
                        kT_own=np.asarray(r3[c]['kT']), v_own=np.asarray(r3[c]['v']), kmean_full=kmf[b], mvalid=mv, esel=esel,
                        ownmask=om, w_out3=P['moba_w_out'][0], **mlpw(3)))
    r4 = run(4, ims)
    if dbg is not None:
        dbg['r4'] = r4
    out = np.zeros((B, S, D), np.float32)
    for c in range(NC):
        out[bat[c], pos[c], :] = np.asarray(r4[c]['xT_out']).T
    return out


def hw_runner(nc, in_maps):
    res = run_bass_kernel_spmd(nc, in_maps, core_ids=list(range(len(in_maps))))
    return res.results


def kernel(**inputs):
    return kernel_impl(inputs, Cfg(4, 2), hw_runner)
```

```python
import numpy as np
import ml_dtypes
from contextlib import ExitStack
import concourse.bass as bass
import concourse.mybir as mybir
from concourse.bass_utils import run_bass_kernel_spmd

F32 = mybir.dt.float32
BF16 = mybir.dt.bfloat16
AF = mybir.ActivationFunctionType
ALU = mybir.AluOpType
bf = ml_dtypes.bfloat16

D = 1024
DC = 8
H = 8
DH = 128
CH = 512
NCH = 4
NT = NCH * CH
DFF = 4096
EPS = 1e-6
NEG = -30000.0
SCALE = DH ** -0.5


class Prog:
    NDMA = 24

    def __init__(self, nc):
        self.nc = nc
        self.ops = []
        self.lastw = {}
        self.readers = {}
        self.bar = set()
        self.last_eng = {}
        self.dma_since = []

    def op(self, eng, fn, reads=(), writes=(), dma=False):
        idx = len(self.ops)
        deps = set(self.bar)
        for k in reads:
            if k in self.lastw:
                deps.add(self.lastw[k])
        for k in writes:
            if k in self.lastw:
                deps.add(self.lastw[k])
            deps.update(self.readers.get(k, ()))
        for k in reads:
            self.readers.setdefault(k, []).append(idx)
        for k in writes:
            self.lastw[k] = idx
            self.readers[k] = []
        self.ops.append(dict(eng=eng, fn=fn, deps=deps, dma=dma))
        self.last_eng[eng] = idx
        if dma:
            self.dma_since.append(idx)
        return idx

    def barrier(self):
        self.bar = set(self.last_eng.values()) | set(self.dma_since)
        self.dma_since = []
        self.lastw = {}
        self.readers = {}

    def emit(self, stack):
        nc = self.nc
        ops = self.ops
        n = len(ops)
        needed = [False] * n
        for i, o in enumerate(ops):
            nd = set()
            for d in o['deps']:
                if ops[d]['eng'] == 'pe' and o['eng'] == 'pe' and not ops[d]['dma']:
                    continue
                nd.add(d)
                needed[d] = True
            o['deps'] = nd
        engs = ['pe', 'act', 'dve', 'pool', 'sp']
        esem = {e: stack.enter_context(nc.semaphore('s_' + e)) for e in engs}
        dsem = [stack.enter_context(nc.semaphore('d_%d' % i)) for i in range(self.NDMA)]
        ecount = {e: 0 for e in engs}
        dcount = [0] * self.NDMA
        rr = {'sp': 0, 'pool': 0}
        NSP = 16
        for i, o in enumerate(ops):
            if o['dma']:
                if o['eng'] == 'pool':
                    s = NSP + rr['pool'] % (self.NDMA - NSP)
                    rr['pool'] += 1
                else:
                    s = rr['sp'] % NSP
                    rr['sp'] += 1
                o['prev'] = (dsem[s], dcount[s]) if dcount[s] > 0 else None
                dcount[s] += 16
                o['sig'] = (dsem[s], dcount[s])
            else:
                if needed[i]:
                    ecount[o['eng']] += 1
                    o['sig'] = (esem[o['eng']], ecount[o['eng']])
                else:
                    o['sig'] = None
        final_d = [(dsem[s], dcount[s]) for s in range(self.NDMA) if dcount[s] > 0]
        block = stack.enter_context(nc.Block())

        def run(ename, e):
            waited = {}

            def w(sem, val):
                k = id(sem)
                if waited.get(k, 0) >= val:
                    return
                waited[k] = val
                e.wait_ge(sem, val)
            for o in ops:
                if o['eng'] != ename:
                    continue
                for d in sorted(o['deps']):
                    sg = ops[d]['sig']
                    w(sg[0], sg[1])
                if o['dma'] and o['prev'] is not None:
                    w(*o['prev'])
                ins = o['fn'](e)
                if o['dma']:
                    ins.then_inc(o['sig'][0], 16)
                elif o['sig'] is not None:
                    ins.then_inc(o['sig'][0], 1)
            if ename == 'sp':
                for sem, val in final_d:
                    w(sem, val)

        @block.tensor
        def _(e):
            run('pe', e)

        @block.scalar
        def _(e):
            run('act', e)

        @block.vector
        def _(e):
            run('dve', e)

        @block.gpsimd
        def _(e):
            run('pool', e)

        @block.sync
        def _(e):
            run('sp', e)


class Cfg:
    def __init__(self, R=4, B=2):
        self.R = R
        self.B = B
        self.S = NCH * R * CH
        self.NC = R * B
        self.NKT = self.S // 128
        self.KM = [4 * R, 8 * R, 12 * R, 16 * R]
        self.GMIN = [0, R, 2 * R, 3 * R]

    def gchunks(self, j):
        R = self.R
        return [j, 2 * R - 1 - j, 2 * R + j, 4 * R - 1 - j]


class K:
    def __init__(self, cfg, st):
        self.cfg = cfg
        self.st = st
        self.nc = bass.Bass("TRN2", target_bir_lowering=False)
        self.P = Prog(self.nc)
        self.dram = {}
        self.in_specs = {}
        self.out_specs = {}
        self.arena = None
        self.off = 0
        self.hiwater = 0
        self.uid = 0

    def start(self, words):
        self.arena = self.st.enter_context(self.nc.sbuf_tensor("arena", [128, words], F32))
        self.words = words
        self.ps = [self.st.enter_context(self.nc.psum_tensor("ps%d" % i, [128, 512], F32)) for i in range(8)]

    def din(self, name, shape, dt):
        t = self.nc.dram_tensor(name, list(shape), dt, kind="ExternalInput").ap()
        self.dram[name] = t
        self.in_specs[name] = (tuple(shape), dt)
        return t

    def dout(self, name, shape, dt):
        t = self.nc.dram_tensor('o_' + name, list(shape), dt, kind="ExternalOutput").ap()
        self.dram[name] = t
        self.out_specs[name] = (tuple(shape), dt)
        return t

    def f32(self, n, parts=128):
        a = self.arena[0:parts, self.off:self.off + n]
        self.off += n
        self.hiwater = max(self.hiwater, self.off)
        assert self.off <= self.words, ("arena overflow", self.off, self.words)
        return a

    def b16(self, n, parts=128):
        w = (n + 1) // 2
        a = self.arena[0:parts, self.off:self.off + w].bitcast(BF16)
        self.off += w
        self.hiwater = max(self.hiwater, self.off)
        assert self.off <= self.words, ("arena overflow", self.off, self.words)
        return a[:, 0:n]

    def mark(self):
        return self.off

    def reset(self, m):
        self.off = m

    def key(self, s):
        self.uid += 1
        return "%s#%d" % (s, self.uid)

    def dma(self, out, in_, reads=(), writes=(), eng='sp'):
        self.P.op(eng, lambda e, out=out, in_=in_: e.dma_start(out=out, in_=in_), reads, writes, dma=True)

    def mm(self, out, lhsT, rhs, start=True, stop=True, reads=(), writes=(), skip=False):
        self.P.op('pe', lambda e, out=out, lhsT=lhsT, rhs=rhs, start=start, stop=stop, skip=skip:
                  e.matmul(out, lhsT, rhs, start=start, stop=stop, skip_group_check=skip), reads, writes)

    def act(self, out, in_, func, reads=(), writes=(), bias=None, scale=None):
        kw = {}
        if bias is not None:
            kw['bias'] = bias
        if scale is not None:
            kw['scale'] = scale
        self.P.op('act', lambda e, out=out, in_=in_, func=func, kw=kw: e.activation(out=out, in_=in_, func=func, **kw),
                  reads, writes)

    def tt(self, out, in0, in1, op, reads=(), writes=(), eng='dve'):
        self.P.op(eng, lambda e, out=out, in0=in0, in1=in1, op=op: e.tensor_tensor(out=out, in0=in0, in1=in1, op=op),
                  reads, writes)

    def ts(self, out, in0, s1, s2, op0, op1=None, reads=(), writes=(), eng='dve'):
        if op1 is None:
            self.P.op(eng, lambda e, out=out, in0=in0, s1=s1, op0=op0:
                      e.tensor_scalar(out=out, in0=in0, scalar1=s1, scalar2=None, op0=op0), reads, writes)
        else:
            self.P.op(eng, lambda e, out=out, in0=in0, s1=s1, s2=s2, op0=op0, op1=op1:
                      e.tensor_scalar(out=out, in0=in0, scalar1=s1, scalar2=s2, op0=op0, op1=op1), reads, writes)

    def stt(self, out, in0, scalar, in1, op0, op1, reads=(), writes=(), eng='dve'):
        self.P.op(eng, lambda e, out=out, in0=in0, scalar=scalar, in1=in1, op0=op0, op1=op1:
                  e.scalar_tensor_tensor(out=out, in0=in0, scalar=scalar, in1=in1, op0=op0, op1=op1), reads, writes)

    def copy(self, out, in_, reads=(), writes=(), eng='dve'):
        self.P.op(eng, lambda e, out=out, in_=in_: e.tensor_copy(out=out, in_=in_), reads, writes)

    def recip(self, out, in_, reads=(), writes=()):
        self.P.op('dve', lambda e, out=out, in_=in_: e.reciprocal(out=out, in_=in_), reads, writes)

    def memset(self, ap, val, writes=(), eng='pool'):
        self.P.op(eng, lambda e, ap=ap, val=val: e.memset(ap, val), (), writes)

    def finish(self):
        self.P.emit(self.st)
        return self.nc


def v3(ap, a):
    return ap.rearrange("p (a b) -> p a b", a=a)


def load_consts(k):
    cin = k.din('cst', [128, 5 * 128], BF16)
    c = k.b16(5 * 128)
    k.dma(c, cin[:, :], writes=['cst'])
    k.ident = c[:, 0:128]
    k.onesD = c[:, 128:256]
    k.onesH = c[:, 256:384]
    k.ones = c[:, 384:512]
    k.Rm = c[:, 512:640]
    k.bank_rr = {}


def host_consts():
    c = np.zeros((128, 5 * 128), np.float32)
    c[:, 0:128] = np.eye(128)
    c[:, 128:256] = 1.0 / 1024
    c[:, 256:384] = 1.0 / 128
    c[:, 384:512] = 1.0
    Rm = np.zeros((128, 128), np.float32)
    for dd in range(64):
        Rm[dd + 64, dd] = -1.0
        Rm[dd, dd + 64] = 1.0
    c[:, 512:640] = Rm
    return c.astype(bf)


def nextbank(k, role, banks):
    i = k.bank_rr.get(role, 0)
    k.bank_rr[role] = i + 1
    return banks[i % len(banks)]


def alloc_x(k):
    k.xraw = k.f32(DC * NT)
    k.xT = v3(k.xraw, DC)


def load_x(k, alloc=True):
    xin = k.din('xT', [D, NT], F32)
    if alloc:
        alloc_x(k)
    for dc in range(DC):
        k.dma(k.xT[:, dc, :], xin[dc * 128:(dc + 1) * 128, :], writes=[('x', dc, c) for c in range(NCH)])


def store_x(k):
    xo = k.dout('xT_out', [D, NT], F32)
    for dc in range(DC):
        k.dma(xo[dc * 128:(dc + 1) * 128, :], k.xT[:, dc, :], reads=[('x', dc, c) for c in range(NCH)],
              writes=[('xo', dc)])


def load_vec(k, name, n):
    t = k.f32(n)
    k.dma(t, k.din(name, [128, n], F32)[:, :], writes=[name])
    return t


def rmsnorm(k, gname, hn):
    g = load_vec(k, gname, DC)
    sq = [v3(k.b16(DC * CH), DC) for _ in range(2)]
    rs = [k.f32(CH) for _ in range(2)]
    for c in range(NCH):
        s = c % 2
        cs = slice(c * CH, (c + 1) * CH)
        k.act(sq[s], k.xT[:, :, cs], AF.Square, reads=[('x', dc, c) for dc in range(DC)], writes=[('sq', s)])
        pb = 6 + s
        for dc in range(DC):
            k.mm(k.ps[pb][:, :], k.onesD, sq[s][:, dc, :], start=(dc == 0), stop=(dc == DC - 1),
                 reads=[('sq', s), 'cst'], writes=[('ps', pb)])
        k.act(rs[s], k.ps[pb][:, :], AF.Sqrt, bias=EPS, reads=[('ps', pb)], writes=[('rs', s)])
        k.recip(rs[s], rs[s], reads=[('rs', s)], writes=[('rs', s)])
        for dc in range(DC):
            k.stt(hn[:, dc, cs], k.xT[:, dc, cs], g[:, dc:dc + 1], rs[s], ALU.mult, ALU.mult,
                  reads=[('x', dc, c), ('rs', s), gname], writes=[('hn', dc, c)])


def proj_fm(k, wname, blocks, hn, consumer, banks=(0, 1, 2, 3)):
    wv = k.dram[wname].rearrange("(dc p) m -> p dc m", p=128)
    slots = [v3(k.b16(DC * 512), DC) for _ in range(2)]
    for bi, (c0, ncol) in enumerate(blocks):
        s = bi % 2
        k.dma(slots[s][:, :, 0:ncol], wv[:, :, c0:c0 + ncol], writes=[('wblk', wname, s)], eng='pool')
        for c in range(NCH):
            for mi in range((ncol + 127) // 128):
                mw = min(128, ncol - mi * 128)
                pb = nextbank(k, 'proj', banks)
                for dc in range(DC):
                    k.mm(k.ps[pb][0:mw, :], slots[s][:, dc, mi * 128:mi * 128 + mw], hn[:, dc, c * CH:(c + 1) * CH],
                         start=(dc == 0), stop=(dc == DC - 1),
                         reads=[('wblk', wname, s), ('hn', dc, c)], writes=[('ps', pb)])
                consumer(c0 + mi * 128, c, pb)


def proj_tm(k, wname, c0, ncol, hn, consumer, banks=(0, 1, 2, 3)):
    wv = k.dram[wname].rearrange("(dc p) m -> p dc m", p=128)
    wt = v3(k.b16(DC * ncol), DC)
    kk = k.key('wtm')
    k.dma(wt, wv[:, :, c0:c0 + ncol], writes=[kk], eng='pool')
    for tt in range(NT // 128):
        c = tt // 4
        pb = nextbank(k, 'proj', banks)
        for dc in range(DC):
            k.mm(k.ps[pb][:, 0:ncol], hn[:, dc, tt * 128:(tt + 1) * 128], wt[:, dc, :],
                 start=(dc == 0), stop=(dc == DC - 1), reads=[kk, ('hn', dc, c)], writes=[('ps', pb)])
        consumer(tt, pb)


def out_proj_residual(k, wname, ob, okeyf):
    wv = k.dram[wname].rearrange("(dc p) m -> p dc m", p=128)
    wt = v3(k.b16(DC * D), DC)
    kk = k.key('wout')
    k.dma(wt, wv, writes=[kk], eng='pool')
    for c in range(NCH):
        cs = slice(c * CH, (c + 1) * CH)
        for dco in range(DC):
            pb = nextbank(k, 'op', (4, 5))
            for h in range(DC):
                k.mm(k.ps[pb][:, :], wt[:, h, dco * 128:(dco + 1) * 128], ob[:, h, cs],
                     start=(h == 0), stop=(h == DC - 1), reads=[kk] + okeyf(h, c), writes=[('ps', pb)])
            k.tt(k.xT[:, dco, cs], k.xT[:, dco, cs], k.ps[pb][:, :], ALU.add,
                 reads=[('ps', pb), ('x', dco, c)], writes=[('x', dco, c)])


def mlp(k, li):
    m = k.mark()
    hn = v3(k.b16(DC * NT), DC)
    rmsnorm(k, 'g_mlp%d' % li, hn)
    wu = k.din('w_up%d' % li, [D, DFF], F32).rearrange("(dc p) m -> p dc m", p=128)
    wd = k.din('w_dn%d' % li, [DFF, D], F32).rearrange("(fc p) m -> p fc m", p=128)
    ups = [v3(k.b16(DC * 512), DC) for _ in range(2)]
    dns = [v3(k.b16(4 * D), 4) for _ in range(2)]
    rl = [k.b16(CH) for _ in range(2)]
    h1 = [v3(k.b16(4 * CH), 4) for _ in range(2)]
    it = 0
    for fb in range(DFF // 512):
        s = fb % 2
        k.dma(ups[s], wu[:, :, fb * 512:(fb + 1) * 512], writes=[('wup', s)], eng='pool')
        k.dma(dns[s], wd[:, fb * 4:(fb + 1) * 4, :], writes=[('wdn', s)], eng='pool')
        for c in range(NCH):
            cs = slice(c * CH, (c + 1) * CH)
            hs = it % 2
            it += 1
            for fc in range(4):
                pb = nextbank(k, 'mlpu', (0, 1))
                for dc in range(DC):
                    k.mm(k.ps[pb][:, :], ups[s][:, dc, fc * 128:(fc + 1) * 128], hn[:, dc, cs],
                         start=(dc == 0), stop=(dc == DC - 1), reads=[('wup', s), ('hn', dc, c)], writes=[('ps', pb)])
                r = nextbank(k, 'rl', (0, 1))
                k.act(rl[r], k.ps[pb][:, :], AF.Relu, reads=[('ps', pb)], writes=[('rl', r)])
                k.tt(h1[hs][:, fc, :], rl[r], rl[r], ALU.mult, reads=[('rl', r)], writes=[('h1', hs, fc)], eng='pool')
            for dco in range(DC):
                pb = nextbank(k, 'mlpd', (2, 3))
                for fc in range(4):
                    k.mm(k.ps[pb][:, :], dns[s][:, fc, dco * 128:(dco + 1) * 128], h1[hs][:, fc, :],
                         start=(fc == 0), stop=(fc == 3), reads=[('wdn', s), ('h1', hs, fc)], writes=[('ps', pb)])
                k.tt(k.xT[:, dco, cs], k.xT[:, dco, cs], k.ps[pb][:, :], ALU.add,
                     reads=[('ps', pb), ('x', dco, c)], writes=[('x', dco, c)])
    k.P.barrier()
    k.reset(m)


def normrope_setup(k):
    k.cosT = k.f32(NT)
    k.sinT = k.f32(NT)
    k.dma(k.cosT, k.din('cosT', [128, NT], F32)[:, :], writes=['cos'])
    k.dma(k.sinT, k.din('sinT', [128, NT], F32)[:, :], writes=['sin'])
    k.nr_xg = [k.b16(CH) for _ in range(2)]
    k.nr_sq = [k.b16(CH) for _ in range(2)]
    k.nr_rs = [k.f32(CH) for _ in range(2)]
    k.nr_t1 = [k.f32(CH) for _ in range(2)]
    k.nr_t2 = [k.f32(CH) for _ in range(2)]


def normrope(k, pb, c, gain, gkey, out, okeys):
    s = nextbank(k, 'nr', (0, 1))
    cs = slice(c * CH, (c + 1) * CH)
    xg, sq, rs, t1, t2 = k.nr_xg[s], k.nr_sq[s], k.nr_rs[s], k.nr_t1[s], k.nr_t2[s]
    k.act(xg, k.ps[pb][:, :], AF.Identity, scale=gain, reads=[('ps', pb), gkey], writes=[('nrxg', s)])
    k.act(sq, k.ps[pb][:, :], AF.Square, reads=[('ps', pb)], writes=[('nrsq', s)])
    b1 = nextbank(k, 'nrss', (4, 5))
    b2 = nextbank(k, 'nrrot', (6, 7))
    k.mm(k.ps[b1][:, :], k.onesH, sq, reads=[('nrsq', s), 'cst'], writes=[('ps', b1)])
    k.mm(k.ps[b2][:, :], k.Rm, xg, reads=[('nrxg', s), 'cst'], writes=[('ps', b2)])
    k.act(rs, k.ps[b1][:, :], AF.Sqrt, bias=EPS, reads=[('ps', b1)], writes=[('nrrs', s)])
    k.recip(rs, rs, reads=[('nrrs', s)], writes=[('nrrs', s)])
    k.tt(t1, xg, k.cosT[:, cs], ALU.mult, reads=[('nrxg', s), 'cos'], writes=[('nrt1', s)])
    k.tt(t2, k.ps[b2][:, :], k.sinT[:, cs], ALU.mult, reads=[('ps', b2), 'sin'], writes=[('nrt2', s)])
    k.tt(t1, t1, t2, ALU.add, reads=[('nrt1', s), ('nrt2', s)], writes=[('nrt1', s)])
    k.tt(out, t1, rs, ALU.mult, reads=[('nrt1', s), ('nrrs', s)], writes=okeys)


def qkv_pre(k, li, moba):
    m = k.mark()
    hn = v3(k.b16(DC * NT), DC)
    rmsnorm(k, 'g_an%d' % li, hn)
    k.din('w_in%d' % li, [D, 3 * D], F32)
    qo = k.dout('qT', [D, NT], BF16)
    ko = k.dout('kT', [D, NT], BF16)
    vo = k.dout('v', [NT, D], BF16)
    stage = [k.b16(CH) for _ in range(3)]
    if moba:
        normrope_setup(k)
        gq = load_vec(k, 'moba_qn', 1)
        gk = load_vec(k, 'moba_kn', 1)
        kmo = k.dout('kmean', [D, 2 * NCH], F32)
        kms = k.f32(H * 2 * NCH)
        kmf = [k.f32(CH) for _ in range(2)]

    def cons(col0, c, pb):
        s = nextbank(k, 'stg', (0, 1, 2))
        cs = slice(c * CH, (c + 1) * CH)
        isq = col0 < D
        m_ = (col0 % D) // 128
        dst = (qo if isq else ko)[m_ * 128:(m_ + 1) * 128, cs]
        if not moba:
            k.act(stage[s], k.ps[pb][:, :], AF.Copy, scale=(SCALE if isq else 1.0), reads=[('ps', pb)], writes=[('stg', s)])
        else:
            normrope(k, pb, c, gq[:, 0:1] if isq else gk[:, 0:1], 'moba_qn' if isq else 'moba_kn', stage[s], [('stg', s)])
            if not isq:
                f = nextbank(k, 'kmf', (0, 1))
                k.copy(kmf[f], stage[s], reads=[('stg', s)], writes=[('kmf', f)])
                for b2 in range(2):
                    col = m_ * 2 * NCH + c * 2 + b2
                    k.P.op('dve', lambda e, o=kms[:, col:col + 1], i=kmf[f][:, b2 * 256:(b2 + 1) * 256]:
                           e.reduce_sum(out=o, in_=i, axis=mybir.AxisListType.X), [('kmf', f)], [('kms', col)])
        k.dma(dst, stage[s], reads=[('stg', s)], writes=[k.key('o')])

    proj_fm(k, 'w_in%d' % li, [(0, 512), (512, 512), (1024, 512), (1536, 512)], hn, cons)
    vst = [k.b16(512) for _ in range(2)]
    for half in range(2):
        def consv(tt, pb, half=half):
            s = nextbank(k, 'vst', (0, 1))
            k.copy(vst[s], k.ps[pb][:, :], reads=[('ps', pb)], writes=[('vst', s)])
            k.dma(vo[tt * 128:(tt + 1) * 128, half * 512:(half + 1) * 512], vst[s], reads=[('vst', s)], writes=[k.key('o')])
        proj_tm(k, 'w_in%d' % li, 2 * D + half * 512, 512, hn, consv)
    if moba:
        k.ts(kms, kms, 1.0 / 256, None, ALU.mult, reads=[('kms', i) for i in range(H * 2 * NCH)], writes=['kmsall'])
        for h in range(H):
            k.dma(kmo[h * 128:(h + 1) * 128, :], kms[:, h * 2 * NCH:(h + 1) * 2 * NCH], reads=['kmsall'], writes=[k.key('o')])
    k.P.barrier()
    k.reset(m)


def conv_pre(k):
    m = k.mark()
    hn = v3(k.b16(DC * NT), DC)
    rmsnorm(k, 'g_an2', hn)
    wv = k.din('conv_w_in', [D, 2 * D], F32).rearrange("(dc p) m -> p dc m", p=128)
    uo = k.dout('uT', [D, NT], BF16)
    wa = [v3(k.b16(DC * 512), DC) for _ in range(2)]
    wb = [v3(k.b16(DC * 512), DC) for _ in range(2)]
    sg = [k.f32(CH) for _ in range(2)]
    us = [k.b16(CH) for _ in range(3)]
    for pi in range(2):
        k.dma(wa[pi], wv[:, :, pi * 512:(pi + 1) * 512], writes=[('wa', pi)], eng='pool')
        k.dma(wb[pi], wv[:, :, D + pi * 512:D + (pi + 1) * 512], writes=[('wb', pi)], eng='pool')
        for c in range(NCH):
            cs = slice(c * CH, (c + 1) * CH)
            for mi in range(4):
                dc = pi * 4 + mi
                pa = nextbank(k, 'cva', (0, 1))
                pbb = nextbank(k, 'cvb', (2, 3))
                for d2 in range(DC):
                    k.mm(k.ps[pa][:, :], wa[pi][:, d2, mi * 128:(mi + 1) * 128], hn[:, d2, cs], start=(d2 == 0), stop=(d2 == DC - 1),
                         reads=[('wa', pi), ('hn', d2, c)], writes=[('ps', pa)])
                for d2 in range(DC):
                    k.mm(k.ps[pbb][:, :], wb[pi][:, d2, mi * 128:(mi + 1) * 128], hn[:, d2, cs], start=(d2 == 0), stop=(d2 == DC - 1),
                         reads=[('wb', pi), ('hn', d2, c)], writes=[('ps', pbb)])
                s = nextbank(k, 'sg', (0, 1))
                u = nextbank(k, 'us', (0, 1, 2))
                k.act(sg[s], k.ps[pbb][:, :], AF.Sigmoid, reads=[('ps', pbb)], writes=[('sg', s)])
                k.tt(us[u], k.ps[pa][:, :], sg[s], ALU.mult, reads=[('ps', pa), ('sg', s)], writes=[('us', u)])
                k.dma(uo[dc * 128:(dc + 1) * 128, cs], us[u], reads=[('us', u)], writes=[k.key('o')])
    k.P.barrier()
    k.reset(m)


def conv_core(k):
    m = k.mark()
    HW = 32 + CH
    uin = k.din('uE', [D, NCH, HW], BF16)
    ue = v3(k.b16(DC * NCH * HW), DC * NCH)
    for dc in range(DC):
        k.dma(ue[:, dc * NCH:(dc + 1) * NCH, :], uin[dc * 128:(dc + 1) * 128, :, :], writes=[('ue', dc)])
    wdw = v3(load_vec(k, 'conv_dw', DC * 31), DC)
    bdw = load_vec(k, 'conv_dwb', DC)
    lg = load_vec(k, 'conv_lng', DC)
    lb = load_vec(k, 'conv_lnb', DC)
    ucb = v3(k.b16(DC * NT), DC)
    Dg = [v3(k.b16(31 * 128), 31) for _ in range(2)]
    for dc in range(DC):
        s = dc % 2
        for kk in range(31):
            k.ts(Dg[s][:, kk, :], k.ident, wdw[:, dc, kk:kk + 1], None, ALU.mult, reads=['cst', 'conv_dw'], writes=[('dg', s, kk)])
        for c in range(NCH):
            pb = nextbank(k, 'cv', (0, 1, 2))
            for kk in range(31):
                k.mm(k.ps[pb][:, :], Dg[s][:, kk, :], ue[:, dc * NCH + c, 2 + kk:2 + kk + CH], start=(kk == 0), stop=(kk == 30),
                     reads=[('dg', s, kk), ('ue', dc)], writes=[('ps', pb)])
            k.act(ucb[:, dc, c * CH:(c + 1) * CH], k.ps[pb][:, :], AF.Identity, bias=bdw[:, dc:dc + 1],
                  reads=[('ps', pb), 'conv_dwb'], writes=[('ucb', dc, c)])
    sq = [v3(k.b16(DC * CH), DC) for _ in range(2)]
    mu = [k.f32(CH) for _ in range(2)]
    va = [k.f32(CH) for _ in range(2)]
    nm = [k.f32(CH) for _ in range(2)]
    tn = [k.f32(CH) for _ in range(2)]
    for c in range(NCH):
        s = c % 2
        cs = slice(c * CH, (c + 1) * CH)
        k.tt(sq[s], ucb[:, :, cs], ucb[:, :, cs], ALU.mult, reads=[('ucb', dc, c) for dc in range(DC)], writes=[('csq', s)], eng='pool')
        for dc in range(DC):
            k.mm(k.ps[6][:, :], k.onesD, ucb[:, dc, cs], start=(dc == 0), stop=(dc == DC - 1), reads=[('ucb', dc, c), 'cst'], writes=[('ps', 6)])
        for dc in range(DC):
            k.mm(k.ps[7][:, :], k.onesD, sq[s][:, dc, :], start=(dc == 0), stop=(dc == DC - 1), reads=[('csq', s), 'cst'], writes=[('ps', 7)])
        k.copy(mu[s], k.ps[6][:, :], reads=[('ps', 6)], writes=[('mu', s)])
        k.tt(va[s], mu[s], mu[s], ALU.mult, reads=[('mu', s)], writes=[('va', s)])
        k.tt(va[s], k.ps[7][:, :], va[s], ALU.subtract, reads=[('ps', 7), ('va', s)], writes=[('va', s)])
        k.act(va[s], va[s], AF.Sqrt, bias=EPS, reads=[('va', s)], writes=[('va', s)])
        k.recip(va[s], va[s], reads=[('va', s)], writes=[('va', s)])
        k.stt(nm[s], mu[s], -1.0, va[s], ALU.mult, ALU.mult, reads=[('mu', s), ('va', s)], writes=[('nm', s)])
        for dc in range(DC):
            t = nextbank(k, 'tn', (0, 1))
            k.tt(tn[t], ucb[:, dc, cs], va[s], ALU.mult, reads=[('ucb', dc, c), ('va', s)], writes=[('tn', t)])
            k.tt(tn[t], tn[t], nm[s], ALU.add, reads=[('tn', t), ('nm', s)], writes=[('tn', t)])
            k.act(ucb[:, dc, cs], tn[t], AF.Silu, scale=lg[:, dc:dc + 1], bias=lb[:, dc:dc + 1],
                  reads=[('tn', t), 'conv_lng', 'conv_lnb'], writes=[('ucb', dc, c)])
    return m, ucb


def load_q(k):
    qin = k.din('qT', [D, NT], BF16)
    q = v3(k.b16(H * NT), H)
    for h in range(H):
        k.dma(q[:, h, :], qin[h * 128:(h + 1) * 128, :], writes=[('q', h, c) for c in range(NCH)])
    return q


def kv_loader(k, ktname='kT_full', vname='v_full', nslots=4):
    cfg = k.cfg
    kin = k.din(ktname, [D, cfg.S], BF16)
    vin = k.din(vname, [H, 128, cfg.NKT, 128], BF16)
    kb = [k.b16(cfg.S) for _ in range(2)]
    vb = [v3(k.b16(cfg.NKT * 128), cfg.NKT) for _ in range(2)]
    hw = cfg.S // 2
    for e in range(2):
        kb.append(k.xraw[:, (2 * e) * hw:(2 * e + 1) * hw].bitcast(BF16))
        vb.append(v3(k.xraw[:, (2 * e + 1) * hw:(2 * e + 2) * hw].bitcast(BF16), cfg.NKT))
    state = {'i': 0}

    def load(h, nkt):
        s = state['i'] % nslots
        state['i'] += 1
        k.dma(kb[s][:, 0:nkt * 128], kin[h * 128:(h + 1) * 128, 0:nkt * 128], writes=[('kb', s)])
        k.dma(vb[s][:, 0:nkt, :], vin[h, :, 0:nkt, :], writes=[('vb', s)])
        return s
    return kb, vb, load


def sb_core(k):
    cfg = k.cfg
    R = cfg.R
    m = k.mark()
    q = load_q(k)
    mq = k.mark()
    kb, vb, kvload = kv_loader(k)
    NU = 4 * R
    mkin = k.din('sbmask', [128, NCH, NU, CH], BF16)
    mk = v3(k.b16(NU * CH), NU)
    cc = k.din('sbc', [128, 256], BF16)
    cst = k.b16(256)
    k.dma(cst, cc[:, :], writes=['sbc'])
    negtri = cst[:, 0:128]
    NCHAIN = 2
    ef = [[k.f32(CH) for _ in range(2)] for _ in range(NCHAIN)]
    sp = [[k.b16(CH) for _ in range(2)] for _ in range(NCHAIN)]
    at = [[k.b16(CH) for _ in range(2)] for _ in range(NCHAIN)]
    spa = [[k.b16(CH) for _ in range(2)] for _ in range(NCHAIN)]
    ZB = [(0, 1), (2, 3)]
    OBK = [5, 6]
    CP = [0, 32]
    for lc in range(NCH):
        KM = cfg.KM[lc]
        k.dma(mk, mkin[:, lc, :, :], writes=['mk'])
        cs = slice(lc * CH, (lc + 1) * CH)
        tiles = list(range(KM - 1, -1, -1))
        n = len(tiles)
        for hp in range(H // NCHAIN):
            hs = [hp * NCHAIN + ci for ci in range(NCHAIN)]
            sl = [kvload(h, KM) for h in hs]
            zb = [[None] * n for _ in range(NCHAIN)]

            def stA(ci, i):
                h, s = hs[ci], sl[ci]
                kt = tiles[i]
                zb[ci][i] = ZB[ci][i % 2]
                z = zb[ci][i]
                k.mm(k.ps[z][:, :], kb[s][:, kt * 128:(kt + 1) * 128], q[:, h, cs],
                     reads=[('kb', s), ('q', h, lc)], writes=[('ps', z)])
                e = i % 2
                k.act(ef[ci][e], k.ps[z][:, :], AF.Exp, reads=[('ps', z)], writes=[('ef', ci, e)])
                k.act(sp[ci][e], ef[ci][e], AF.Ln, bias=1.0, reads=[('ef', ci, e)], writes=[('sp', ci, e)])
                u = kt - 4 * cfg.GMIN[lc]
                if u >= 0:
                    k.tt(sp[ci][e], sp[ci][e], mk[:, u, :], ALU.mult, reads=[('sp', ci, e), 'mk'], writes=[('sp', ci, e)])

            def stB(ci, i):
                z = zb[ci][i]
                e = i % 2
                k.mm(k.ps[z][:, :], negtri, sp[ci][e], start=False, stop=True, skip=True,
                     reads=[('sp', ci, e), 'sbc', ('ps', z)], writes=[('ps', z)])
                if i > 0:
                    c_ = (i - 1) % 2
                    k.mm(k.ps[z][:, :], cst[:, 128:256], spa[ci][c_], start=False, stop=True, skip=True,
                         reads=[('spa', ci, c_), 'sbc', ('ps', z)], writes=[('ps', z)])
                if i < n - 1:
                    if i == 0:
                        k.copy(spa[ci][0], sp[ci][e], reads=[('sp', ci, e)], writes=[('spa', ci, 0)])
                    else:
                        k.tt(spa[ci][i % 2], spa[ci][(i - 1) % 2], sp[ci][e], ALU.add,
                             reads=[('spa', ci, (i - 1) % 2), ('sp', ci, e)], writes=[('spa', ci, i % 2)])

            def stC(ci, i):
                h, s = hs[ci], sl[ci]
                kt = tiles[i]
                z = zb[ci][i]
                e = i % 2
                a = at[ci][e]
                k.act(a, k.ps[z][:, :], AF.Exp, reads=[('ps', z)], writes=[('at', ci, e)])
                u = kt - 4 * cfg.GMIN[lc]
                if u >= 0:
                    k.tt(a, a, mk[:, u, :], ALU.mult, reads=[('at', ci, e), 'mk'], writes=[('at', ci, e)])
                k.mm(k.ps[OBK[ci]][:, :], vb[s][:, kt, :], a, start=(i == 0), stop=(i == n - 1),
                     reads=[('vb', s), ('at', ci, e)], writes=[('ps', OBK[ci])])

            for i in range(n + 1):
                for ci in range(NCHAIN):
                    if i < n:
                        stA(ci, i)
                for ci in range(NCHAIN):
                    if 0 <= i - 1 < n:
                        stB(ci, i - 1)
                for ci in range(NCHAIN):
                    if 0 <= i - 1 < n:
                        stC(ci, i - 1)
            for ci in range(NCHAIN):
                k.act(q[:, hs[ci], cs], k.ps[OBK[ci]][:, :], AF.Copy, reads=[('ps', OBK[ci])], writes=[('q', hs[ci], lc)])
    return m, mq, q


def moba_core(k):
    cfg = k.cfg
    NB = cfg.S // 256
    m = k.mark()
    q = load_q(k)
    mq = k.mark()
    kb, vb, kvload = kv_loader(k)
    kown = k.din('kT_own', [D, NT], BF16)
    vown = k.din('v_own', [NT, D], BF16)
    kmin = k.din('kmean_full', [D, NB], F32)
    kmf = v3(k.f32(H * NB), H)
    kmb = v3(k.b16(H * NB), H)
    for h in range(H):
        k.dma(kmf[:, h, :], kmin[h * 128:(h + 1) * 128, :], writes=[('kmf', h)])
    k.copy(kmb, kmf, reads=[('kmf', h) for h in range(H)], writes=['kmb'])
    mvin = k.din('mvalid', [128, NCH, 4 * NB], F32)
    mv = v3(k.f32(NCH * 4 * NB), NCH)
    k.dma(mv, mvin[:, :, :], writes=['mv'])
    esin = k.din('esel', [128, NB * 128], BF16)
    esel = k.b16(NB * 128)
    k.dma(esel, esin[:, :], writes=['esel'])
    omin = k.din('ownmask', [128, 4, CH], BF16)
    om = v3(k.b16(4 * CH), 4)
    k.dma(om, omin[:, :, :], writes=['om'])
    identf = k.f32(128)
    k.copy(identf, k.ident, reads=['cst'], writes=['identf'])
    gm = k.f32(4 * NB)
    m8 = k.f32(32)
    t1 = k.f32(4 * NB)
    mb = k.f32(4 * NB)
    NCHAIN = 2
    mT = [[k.b16(CH) for _ in range(2)] for _ in range(NCHAIN)]
    for ci_ in range(NCHAIN):
        for l_ in range(2):
            k.memset(mT[ci_][l_], 0.0, writes=[('mT', ci_, l_)])
    pt = [[k.b16(CH) for _ in range(3)] for _ in range(NCHAIN)]
    kl = [[k.b16(CH) for _ in range(2)] for _ in range(NCHAIN)]
    vl = [[v3(k.b16(4 * 128), 4) for _ in range(2)] for _ in range(NCHAIN)]
    rd = k.f32(CH)
    dacc = [k.f32(CH) for _ in range(NCHAIN)]
    dhi = [k.b16(CH) for _ in range(NCHAIN)]
    dlo = [k.b16(CH) for _ in range(NCHAIN)]
    ZB = [(0, 1), (2, 3)]
    OBK = [4, 5]
    DBK = [6, 7]
    it = 0
    for lc in range(NCH):
        KM = cfg.KM[lc]
        cs = slice(lc * CH, (lc + 1) * CH)
        for hp in range(H // NCHAIN):
            l = it % 2
            it += 1
            chains = []
            for ci in range(NCHAIN):
                h = hp * NCHAIN + ci
                s = kvload(h, KM)
                k.dma(kl[ci][l], kown[h * 128:(h + 1) * 128, cs], writes=[('kl', ci, l)])
                k.dma(vl[ci][l], vown[cs, h * 128:(h + 1) * 128].rearrange("(a p) d -> p a d", p=128), writes=[('vl', ci, l)])
                for u in range(4):
                    k.mm(k.ps[6][:, u * NB:(u + 1) * NB], q[:, h, lc * CH + u * 128:lc * CH + (u + 1) * 128], kmb[:, h, :],
                         start=(u == 0), stop=True, skip=(u > 0), reads=[('q', h, lc), 'kmb'], writes=[('ps', 6)])
                k.tt(gm, k.ps[6][:, 0:4 * NB], mv[:, lc, :], ALU.add, reads=[('ps', 6), 'mv'], writes=['gm'])
                for u in range(4):
                    k.P.op('dve', lambda e, o=m8[:, u * 8:(u + 1) * 8], i=gm[:, u * NB:(u + 1) * NB]: e.max(out=o, in_=i), ['gm'], [('m8', u)])
                thr = v3(m8, 4)[:, :, 2:3]
                k.ts(thr, thr, -1e29, None, ALU.max, reads=[('m8', u) for u in range(4)], writes=['thr'])
                for u in range(4):
                    k.ts(t1[:, u * NB:(u + 1) * NB], gm[:, u * NB:(u + 1) * NB], m8[:, u * 8 + 2:u * 8 + 3], None, ALU.is_ge,
                         reads=['gm', 'thr'], writes=[('t1', u)])
                k.ts(mb, t1, -NEG, NEG, ALU.mult, ALU.add, reads=[('t1', u) for u in range(4)], writes=['mb'])
                for u in range(4):
                    k.P.op('pe', lambda e, o=k.ps[7][0:NB, u * 128:(u + 1) * 128], i=mb[:, u * NB:(u + 1) * NB]:
                           e.transpose(o, i, identf), ['mb', 'identf'], [('ps', 7)])
                mt = mT[ci][l]
                mk_ = ('mT', ci, l)
                k.copy(mt[0:NB, :], k.ps[7][0:NB, :], reads=[('ps', 7), mk_], writes=[mk_])
                tiles = []
                for kt in range(KM):
                    nb_ = kt // 2
                    tiles.append((kb[s][:, kt * 128:(kt + 1) * 128], ('kb', s), esel[:, nb_ * 128:(nb_ + 1) * 128], mt, ['esel', mk_],
                                  vb[s][:, kt, :], ('vb', s)))
                for i2 in range(4):
                    tiles.append((kl[ci][l][:, i2 * 128:(i2 + 1) * 128], ('kl', ci, l), k.ident, om[:, i2, :], ['cst', 'om'],
                                  vl[ci][l][:, i2, :], ('vl', ci, l)))
                chains.append(dict(h=h, tiles=tiles, pend=None))
            nt_ = KM + 4
            for i in range(nt_ + 1):
                for ci, chn in enumerate(chains):
                    h = chn['h']
                    if i < nt_:
                        kT_, kkey, ml, mr, mkeys, vT_, vkey = chn['tiles'][i]
                        zb = ZB[ci][i % 2]
                        k.mm(k.ps[zb][:, :], kT_, q[:, h, cs], start=True, stop=False, reads=[kkey, ('q', h, lc)], writes=[('ps', zb)])
                        k.mm(k.ps[zb][:, :], ml, mr, start=False, stop=True, reads=mkeys, writes=[('ps', zb)])
                        p = i % 3
                        k.act(pt[ci][p], k.ps[zb][:, :], AF.Exp, scale=SCALE, reads=[('ps', zb)], writes=[('pt', ci, p)])
                        chn['cur'] = (i, p, vT_, vkey)
                    else:
                        chn['cur'] = None
                for ci, chn in enumerate(chains):
                    if chn['pend'] is not None:
                        pi, pp_, pv_, pvk = chn['pend']
                        ob = OBK[ci]
                        k.mm(k.ps[ob][:, :], pv_, pt[ci][pp_], start=(pi == 0), stop=(pi == nt_ - 1),
                             reads=[pvk, ('pt', ci, pp_)], writes=[('ps', ob)])
                        if pi == 0:
                            k.copy(dacc[ci], pt[ci][pp_], reads=[('pt', ci, pp_)], writes=[('dacc', ci)])
                        else:
                            k.tt(dacc[ci], dacc[ci], pt[ci][pp_], ALU.add, reads=[('pt', ci, pp_), ('dacc', ci)], writes=[('dacc', ci)])
                    chn['pend'] = chn['cur']
            for ci, chn in enumerate(chains):
                h = chn['h']
                ob, db = OBK[ci], DBK[ci]
                k.copy(dhi[ci], dacc[ci], reads=[('dacc', ci)], writes=[('dhi', ci)])
                k.tt(dlo[ci], dacc[ci], dhi[ci], ALU.subtract, reads=[('dacc', ci), ('dhi', ci)], writes=[('dlo', ci)])
                k.mm(k.ps[db][:, :], k.ones, dhi[ci], start=True, stop=False, reads=['cst', ('dhi', ci)], writes=[('ps', db)])
                k.mm(k.ps[db][:, :], k.ones, dlo[ci], start=False, stop=True, reads=['cst', ('dlo', ci)], writes=[('ps', db)])
                k.recip(rd, k.ps[db][:, :], reads=[('ps', db)], writes=['rd'])
                k.tt(q[:, h, cs], k.ps[ob][:, :], rd, ALU.mult, reads=[('ps', ob), 'rd'], writes=[('q', h, lc)])
    return m, mq, q


NSA_IN = 2584


def nsa_pre(k):
    m = k.mark()
    hn = v3(k.b16(DC * NT), DC)
    rmsnorm(k, 'g_an0', hn)
    k.din('w_in0', [D, NSA_IN], F32)
    qo = k.dout('qT', [D, NT], BF16)
    fm_out = {1024: k.dout('kcT', [256, NT], BF16), 1280: k.dout('vcT', [256, NT], BF16),
              1536: k.dout('ksT', [256, NT], BF16), 2048: k.dout('kwT', [256, NT], BF16)}
    vso = k.dout('vs', [NT, 256], BF16)
    vwo = k.dout('vw', [NT, 256], BF16)
    go = k.dout('gT', [24, NT], F32)
    normrope_setup(k)
    gq = load_vec(k, 'nsa_qn', 1)
    gk = load_vec(k, 'nsa_kn', 3)
    stage = [k.b16(CH) for _ in range(3)]
    gst = [k.f32(CH, parts=24) for _ in range(2)]

    def cons(col0, c, pb):
        cs = slice(c * CH, (c + 1) * CH)
        if col0 == 2560:
            s = nextbank(k, 'gst', (0, 1))
            k.act(gst[s], k.ps[pb][0:24, :], AF.Sigmoid, reads=[('ps', pb)], writes=[('gst', s)])
            k.dma(go[:, cs], gst[s], reads=[('gst', s)], writes=[k.key('o')])
            return
        s = nextbank(k, 'stg', (0, 1, 2))
        if col0 < 1024:
            normrope(k, pb, c, gq[:, 0:1], 'nsa_qn', stage[s], [('stg', s)])
            dst = qo[col0:col0 + 128, cs]
        else:
            base = max(b for b in fm_out if b <= col0)
            dst = fm_out[base][col0 - base:col0 - base + 128, cs]
            if base == 1280:
                k.act(stage[s], k.ps[pb][:, :], AF.Copy, reads=[('ps', pb)], writes=[('stg', s)])
            else:
                gi = {1024: 0, 1536: 1, 2048: 2}[base]
                normrope(k, pb, c, gk[:, gi:gi + 1], 'nsa_kn', stage[s], [('stg', s)])
        k.dma(dst, stage[s], reads=[('stg', s)], writes=[k.key('o')])

    proj_fm(k, 'w_in0', [(0, 512), (512, 512), (1024, 512), (1536, 256), (2048, 256), (2560, 24)], hn, cons)
    vst = [k.b16(256) for _ in range(2)]
    for (c0, dst) in ((1792, vso), (2304, vwo)):
        def consv(tt, pb, dst=dst):
            s = nextbank(k, 'vst', (0, 1))
            k.copy(vst[s], k.ps[pb][:, 0:256], reads=[('ps', pb)], writes=[('vst', s)])
            k.dma(dst[tt * 128:(tt + 1) * 128, :], vst[s], reads=[('vst', s)], writes=[k.key('o')])
        proj_tm(k, 'w_in0', c0, 256, hn, consv)
    k.P.barrier()
    k.reset(m)


def nsa_core(k, ob):
    cfg = k.cfg
    S = cfg.S
    NSEL = S // 64
    NCMP = S // 16 - 1
    NCC = (NCMP + 127) // 128
    NQT = NT // 128
    m = k.mark()
    qin = k.din('qT', [D, NT], BF16)
    qsb = k.b16(NQT * H * 128).rearrange("p (a b c) -> p a b c", a=NQT, b=H)
    for h in range(H):
        k.dma(qsb[:, :, h, :], qin[h * 128:(h + 1) * 128, :].rearrange("p (a t) -> p a t", t=128), writes=[('q', h)])
    gin = k.din('gT', [24, NT], F32)
    gb = k.b16(NT, parts=24)
    k.dma(gb, gin[:, :], writes=['gb'], eng='pool')
    selg = k.b16(24 * 128, parts=24)
    k.dma(selg, k.din('selg', [24, 24 * 128], BF16)[:, :], writes=['selg'])
    G = k.b16(S, parts=NSEL)
    k.dma(G, k.din('G', [NSEL, S], BF16)[:, :], writes=['G'])
    Ov = v3(k.b16(NCC * NSEL), NCC)
    k.dma(Ov, k.din('Ov', [128, NCC, NSEL], BF16)[:, :, :], writes=['Ov'])
    tri = k.b16(1024)
    k.dma(tri, k.din('tri', [128, 1024], BF16)[:, :], writes=['tri'])
    triT = tri[:, 0:512]
    tri2 = tri[:, 512:1024]
    McT = k.b16(NQT * NCC * 128).rearrange("p (a b c) -> p a b c", a=NQT, b=NCC)
    k.dma(McT, k.din('McT', [128, NQT, NCC, 128], BF16)[:, :, :, :], writes=['McT'])
    val = v3(k.b16(NQT * NSEL), NQT)
    bon = v3(k.b16(NQT * NSEL), NQT)
    pneg = v3(k.b16(NQT * NSEL), NQT)
    k.dma(val, k.din('sval', [128, NQT, NSEL], BF16)[:, :, :], writes=['val'])
    k.dma(bon, k.din('sbon', [128, NQT, NSEL], BF16)[:, :, :], writes=['bon'])
    k.dma(pneg, k.din('spneg', [128, NQT, NSEL], BF16)[:, :, :], writes=['pneg'])
    pcr = v3(k.b16(NCH * 128, parts=1), NCH)
    k.dma(pcr, k.din('pcrow', [1, NCH, 128], BF16)[:, :, :], writes=['pcr'])
    identf = k.f32(128)
    k.copy(identf, k.ident, reads=['cst'], writes=['identf'])
    posT = v3(load_vec(k, 'cmp_posT', 2 * 32), 2)
    kcmpT = [k.b16(NCC * 128) for _ in range(2)]
    vcmp = [v3(k.b16(NCC * 128), NCC) for _ in range(2)]
    onescol = k.ones[:, 0:1]
    ones1 = k.ones[0:1, :]
    m2 = k.mark()
    rawin = [k.din('kcT_full', [256, S], BF16), k.din('vcT_full', [256, S], BF16)]
    wcin = k.din('w_cmp', [2, 32, 128, 128], F32)
    raw = [k.b16(S) for _ in range(2)]
    wc = [v3(k.b16(32 * 128), 32) for _ in range(2)]
    tmp = [k.b16(NCC * 128) for _ in range(4)]
    for t_ in range(4):
        k.memset(tmp[t_], 0.0, writes=[('tmp', t_)])
    it = 0
    for kv in range(2):
        k.dma(wc[kv], wcin[kv].rearrange("l d e -> d l e"), writes=[('wc', kv)], eng='pool')
        for grp in range(2):
            r = it % 2
            it += 1
            k.dma(raw[r], rawin[kv][grp * 128:(grp + 1) * 128, :], writes=[('raw', r)])
            pb = 2 + r
            for l in range(32):
                t_ = nextbank(k, 'tmp', (0, 1, 2, 3))
                k.ts(tmp[t_][:, 0:NCMP], raw[r][:, l:l + 16 * (NCMP - 1) + 1:16], posT[:, kv, l:l + 1], None, ALU.add,
                     reads=[('raw', r), 'cmp_posT'], writes=[('tmp', t_)])
                if kv == 0:
                    k.mm(k.ps[pb][:, 0:NCC * 128], wc[kv][:, l, :], tmp[t_], start=(l == 0), stop=(l == 31),
                         reads=[('wc', kv), ('tmp', t_)], writes=[('ps', pb)])
                else:
                    for c in range(NCC):
                        k.mm(k.ps[pb][:, c * 128:(c + 1) * 128], tmp[t_][:, c * 128:(c + 1) * 128], wc[kv][:, l, :],
                             start=(l == 0 and c == 0), stop=True, skip=not (l == 0 and c == 0),
                             reads=[('wc', kv), ('tmp', t_)], writes=[('ps', pb)])
            if kv == 0:
                k.copy(kcmpT[grp], k.ps[pb][:, 0:NCC * 128], reads=[('ps', pb)], writes=[('kcmp', grp)])
            else:
                k.copy(vcmp[grp], v3(k.ps[pb][:, 0:NCC * 128], NCC), reads=[('ps', pb)], writes=[('vcmp', grp)])
    k.P.barrier()
    k.reset(m2)
    ksin = k.din('ksT_full', [256, S], BF16)
    vsin = k.din('vs_full', [2, 128, cfg.NKT, 128], BF16)
    kslin = k.din('ksT_own', [256, NT], BF16)
    vslin = k.din('vs_own', [NT, 256], BF16)
    kwpin = k.din('kw_pack', [256, NCH, 2 * CH], BF16)
    vwpin = k.din('vw_pack', [2, 128, NCH, 8, 128], BF16)
    ks = k.b16(S)
    vs = v3(k.b16(cfg.NKT * 128), cfg.NKT)
    ksl = k.b16(NT)
    vsl = v3(k.b16(NQT * 128), NQT)
    kwp = v3(k.b16(NCH * 2 * CH), NCH)
    vwp = k.b16(NCH * 8 * 128).rearrange("p (a b c) -> p a b c", a=NCH, b=8)
    Pc = [k.b16(CH) for _ in range(NCC)]
    pt = [k.b16(CH) for _ in range(3)]
    OB = [k.f32(CH) for _ in range(3)]
    rr_ = k.f32(CH)
    rd4 = k.f32(4)
    imp = k.f32(NSEL)
    imp2 = k.f32(NSEL)
    imp3 = k.f32(NSEL)
    m8 = k.f32(16)
    mbs = k.f32(NSEL)
    mT = k.b16(512, parts=NSEL)
    obank = [2, 3]
    dbank = [4, 5]
    br = 0

    dacc = k.f32(CH)
    dhi = k.b16(CH)
    dlo = k.b16(CH)

    def run_tiles(tiles, Q, qk, o_b, d_b):
        n = len(tiles)
        pend = None
        for i in range(n + 1):
            cur = None
            if i < n:
                kT_, kkey, masks, vT_, vkey, dst = tiles[i]
                zb = nextbank(k, 'z', (0, 1))
                k.mm(k.ps[zb][:, :], kT_, Q, start=True, stop=True, reads=[kkey] + qk, writes=[('ps', zb)])
                for (ml, mr, mkeys, mode) in masks:
                    if mode == 'hg4':
                        for hg in range(4):
                            k.mm(k.ps[zb][:, hg * 128:(hg + 1) * 128], ml, mr, start=False, stop=True, skip=True,
                                 reads=mkeys, writes=[('ps', zb)])
                    else:
                        k.mm(k.ps[zb][:, :], ml, mr, start=False, stop=True, skip=True, reads=mkeys, writes=[('ps', zb)])
                if dst is None:
                    p = nextbank(k, 'pt', (0, 1, 2))
                    dst = (pt[p], ('pt', p))
                k.act(dst[0], k.ps[zb][:, :], AF.Exp, scale=SCALE, reads=[('ps', zb)], writes=[dst[1]])
                cur = (i, dst, vT_, vkey)
            if pend is not None:
                pi, pd, pv_, pvk = pend
                k.mm(k.ps[o_b][:, :], pv_, pd[0], start=(pi == 0), stop=(pi == n - 1), reads=[pvk, pd[1]], writes=[('ps', o_b)])
                if pi == 0:
                    k.copy(dacc, pd[0], reads=[pd[1]], writes=['dacc'])
                else:
                    k.tt(dacc, dacc, pd[0], ALU.add, reads=[pd[1], 'dacc'], writes=['dacc'])
            pend = cur
        k.copy(dhi, dacc, reads=['dacc'], writes=['dhi'])
        k.tt(dlo, dacc, dhi, ALU.subtract, reads=['dacc', 'dhi'], writes=['dlo'])
        k.mm(k.ps[d_b][:, :], k.ones, dhi, start=True, stop=False, reads=['cst', 'dhi'], writes=[('ps', d_b)])
        k.mm(k.ps[d_b][:, :], k.ones, dlo, start=False, stop=True, reads=['cst', 'dlo'], writes=[('ps', d_b)])

    def finish_branch(o_b, d_b, dst):
        k.ts(rr_, k.ps[d_b][:, :], 1e-30, None, ALU.max, reads=[('ps', d_b)], writes=['rr'])
        k.recip(rr_, rr_, reads=['rr'], writes=['rr'])
        k.tt(dst[0], k.ps[o_b][:, :], rr_, ALU.mult, reads=[('ps', o_b), 'rr'], writes=[dst[1]])

    for grp in range(2):
        gs = slice(grp * 128, (grp + 1) * 128)
        k.dma(ks, ksin[gs, :], writes=['ks'])
        k.dma(vs, vsin[grp], writes=['vs'])
        k.dma(ksl, kslin[gs, :], writes=['ksl'])
        k.dma(vsl, vslin[:, gs].rearrange("(a p) d -> p a d", p=128), writes=['vsl'])
        k.dma(kwp, kwpin[gs, :, :], writes=['kwp'])
        k.dma(vwp, vwpin[grp], writes=['vwp'])
        for qt in range(NQT):
            lc, u = qt // 4, qt % 4
            Q = qsb[:, qt, grp * 4:(grp + 1) * 4, :].rearrange("p a b -> p (a b)")
            qk = [('q', grp * 4 + hg) for hg in range(4)]
            o_b, d_b = obank[br % 2], dbank[br % 2]
            br += 1
            tiles = [(kcmpT[grp][:, c * 128:(c + 1) * 128], ('kcmp', grp), [(k.ident, McT[:, qt, c, :], ['cst', 'McT'], 'hg4')],
                      vcmp[grp][:, c, :], ('vcmp', grp), (Pc[c], ('Pc', c))) for c in range(NCC)]
            run_tiles(tiles, Q, qk, o_b, d_b)
            for hg in range(4):
                for c in range(NCC):
                    first = (hg == 0 and c == 0)
                    k.mm(k.ps[6][:, hg * NSEL:(hg + 1) * NSEL], Pc[c][:, hg * 128:(hg + 1) * 128], Ov[:, c, :],
                         start=first, stop=True, skip=not first, reads=[('Pc', c), 'Ov'], writes=[('ps', 6)])
            for hg in range(4):
                for c in range(NCC):
                    first = (hg == 0 and c == 0)
                    k.mm(k.ps[7][:, hg:hg + 1], Pc[c][:, hg * 128:(hg + 1) * 128], onescol,
                         start=first, stop=True, skip=not first, reads=[('Pc', c), 'cst'], writes=[('ps', 7)])
            k.ts(rd4, k.ps[7][:, 0:4], 1e-30, None, ALU.max, reads=[('ps', 7)], writes=['rd4'])
            k.recip(rd4, rd4, reads=['rd4'], writes=['rd4'])
            k.ts(imp, k.ps[6][:, 0:NSEL], rd4[:, 0:1], None, ALU.mult, reads=[('ps', 6), 'rd4'], writes=['imp'])
            for hg in range(1, 4):
                k.stt(imp, k.ps[6][:, hg * NSEL:(hg + 1) * NSEL], rd4[:, hg:hg + 1], imp, ALU.mult, ALU.add,
                      reads=[('ps', 6), 'rd4', 'imp'], writes=['imp'])
            k.tt(imp2, imp, val[:, qt, :], ALU.mult, reads=['imp', 'val'], writes=['imp2'])
            k.tt(imp2, imp2, bon[:, qt, :], ALU.add, reads=['imp2', 'bon'], writes=['imp2'])
            k.P.op('dve', lambda e: e.max(out=m8[:, 0:8], in_=imp2), ['imp2'], ['m8a'])
            k.P.op('dve', lambda e: e.match_replace(out=imp3, in_to_replace=m8[:, 0:8], in_values=imp2, imm_value=-1e9),
                   ['imp2', 'm8a'], ['imp3'])
            k.P.op('dve', lambda e: e.max(out=m8[:, 8:16], in_=imp3), ['imp3'], ['m8b'])
            k.ts(mbs, imp2, m8[:, 15:16], -NEG, ALU.is_ge, ALU.mult, reads=['imp2', 'm8b'], writes=['mbs'])
            k.stt(mbs, mbs, NEG, pneg[:, qt, :], ALU.add, ALU.add, reads=['mbs', 'pneg'], writes=['mbs'])
            k.P.op('pe', lambda e: e.transpose(k.ps[7][0:NSEL, 128:256], mbs, identf), ['mbs', 'identf'], [('ps', 7)])
            k.copy(v3(mT, 4), k.ps[7][0:NSEL, 128:256].unsqueeze(1).to_broadcast([NSEL, 4, 128]), reads=[('ps', 7)], writes=['mT'])
            finish_branch(o_b, d_b, (OB[0], ('OB', 0)))
            o_b, d_b = obank[br % 2], dbank[br % 2]
            br += 1
            KM = cfg.KM[lc]
            tiles = [(ks[:, kt * 128:(kt + 1) * 128], 'ks', [(G[:, kt * 128:(kt + 1) * 128], mT, ['G', 'mT'], 'row')],
                      vs[:, kt, :], 'vs', None) for kt in range(KM)]
            tiles.append((ksl[:, qt * 128:(qt + 1) * 128], 'ksl', [(k.ident, triT, ['cst', 'tri'], 'row')], vsl[:, qt, :], 'vsl', None))
            run_tiles(tiles, Q, qk, o_b, d_b)
            finish_branch(o_b, d_b, (OB[1], ('OB', 1)))
            o_b, d_b = obank[br % 2], dbank[br % 2]
            br += 1
            tiles = []
            for i in range(5):
                w = u + i
                masks = []
                if i == 0:
                    masks.append((k.ident, tri2, ['cst', 'tri'], 'row'))
                if i == 4:
                    masks.append((k.ident, triT, ['cst', 'tri'], 'row'))
                if w < 4:
                    masks.append((ones1, pcr[:, lc, :], ['cst', 'pcr'], 'hg4'))
                tiles.append((kwp[:, lc, w * 128:(w + 1) * 128], 'kwp', masks, vwp[:, lc, w, :], 'vwp', None))
            run_tiles(tiles, Q, qk, o_b, d_b)
            finish_branch(o_b, d_b, (OB[2], ('OB', 2)))
            for b in range(3):
                for hg in range(4):
                    r0 = (b * 8 + grp * 4 + hg) * 128
                    k.mm(k.ps[6][:, hg * 128:(hg + 1) * 128], selg[:, r0:r0 + 128], gb[:, qt * 128:(qt + 1) * 128],
                         start=(hg == 0), stop=True, skip=(hg > 0), reads=['selg', 'gb'], writes=[('ps', 6)])
                k.tt(OB[b], OB[b], k.ps[6][:, :], ALU.mult, reads=[('OB', b), ('ps', 6)], writes=[('OB', b)])
            k.tt(OB[0], OB[0], OB[1], ALU.add, reads=[('OB', 0), ('OB', 1)], writes=[('OB', 0)])
            k.tt(ob[:, grp * 4:(grp + 1) * 4, qt * 128:(qt + 1) * 128], v3(OB[0], 4), v3(OB[2], 4), ALU.add,
                 reads=[('OB', 0), ('OB', 2)], writes=[('ob', grp * 4 + hg, lc) for hg in range(4)])
    k.P.barrier()
    k.reset(m)


ARENA_WORDS = 53200


def attn_tail(k, wname, m, mq, q, li, nxt):
    k.P.barrier()
    k.reset(mq)
    load_x(k, alloc=False)
    out_proj_residual(k, wname, q, lambda h, c: [('q', h, c)])
    k.P.barrier()
    k.reset(m)
    mlp(k, li)
    if nxt is not None:
        nxt(k)
    store_x(k)


def build_launch(li, cfg):
    st = ExitStack()
    k = K(cfg, st)
    k.start(ARENA_WORDS)
    load_consts(k)
    if li == 0:
        load_x(k)
        nsa_pre(k)
    elif li == 1:
        ob = v3(k.b16(H * NT), H)
        nsa_core(k, ob)
        load_x(k)
        mo = k.mark()
        k.din('w_out0', [D, D], F32)
        out_proj_residual(k, 'w_out0', ob, lambda h, c: [('ob', h, c)])
        k.P.barrier()
        k.reset(mo)
        mlp(k, 0)
        qkv_pre(k, 1, False)
        store_x(k)
    elif li == 2:
        alloc_x(k)
        m, mq, q = sb_core(k)
        k.din('w_out1', [D, D], F32)
        attn_tail(k, 'w_out1', m, mq, q, 1, conv_pre)
    elif li == 3:
        load_x(k)
        m, ucb = conv_core(k)
        k.din('w_out2', [D, D], F32)
        out_proj_residual(k, 'w_out2', ucb, lambda h, c: [('ucb', h, c)])
        k.P.barrier()
        k.reset(m)
        mlp(k, 2)
        qkv_pre(k, 3, True)
        store_x(k)
    elif li == 4:
        alloc_x(k)
        m, mq, q = moba_core(k)
        k.din('w_out3', [D, D], F32)
        attn_tail(k, 'w_out3', m, mq, q, 3, None)
    nc = k.finish()
    return nc, k, st


def pp(v, n):
    return np.ascontiguousarray(np.asarray(v, np.float32).reshape(n, 128).T)


def kernel_impl(inputs, cfg, runner, upto=5, dbg=None):
    R, B, S, NC = cfg.R, cfg.B, cfg.S, cfg.NC
    NKT = cfg.NKT
    x = np.asarray(inputs['x'], np.float32)
    gch = [cfg.gchunks(c % R) for c in range(NC)]
    bat = [c // R for c in range(NC)]
    pos = [np.concatenate([np.arange(g * CH, (g + 1) * CH) for g in gch[c]]) for c in range(NC)]
    cst = host_consts()
    P = {n: np.asarray(v, np.float32) for n, v in inputs.items() if n != 'x'}

    def run(li, in_maps):
        nc, k, st = build_launch(li, cfg)
        for im in in_maps:
            assert set(im.keys()) == set(k.in_specs.keys()), (sorted(set(im.keys()) ^ set(k.in_specs.keys())))
            for n, (shp, dt) in k.in_specs.items():
                want = bf if dt == BF16 else np.float32
                a = im[n]
                if a.dtype != want:
                    a = a.astype(want)
                assert tuple(a.shape) == tuple(shp), (n, a.shape, shp)
                im[n] = np.ascontiguousarray(a)
        res = runner(nc, in_maps)
        st.close()
        return [{n[2:]: v for n, v in r.items()} for r in res]

    def to_global(res, name, axis):
        out = []
        for b in range(B):
            parts = [None] * (NCH * R)
            for c in range(NC):
                if bat[c] != b:
                    continue
                a = np.asarray(res[c][name])
                for lc, g in enumerate(gch[c]):
                    sl = [slice(None)] * a.ndim
                    sl[axis] = slice(lc * CH, (lc + 1) * CH)
                    parts[g] = a[tuple(sl)]
            out.append(np.concatenate(parts, axis=axis))
        return out

    def vlay(vg, nh):
        return np.ascontiguousarray(vg.reshape(NKT, 128, nh, 128).transpose(2, 1, 0, 3))

    inv_freq = (np.float32(10000.0) ** (-np.arange(64, dtype=np.float32) / np.float32(64))).astype(np.float32)
    cosT, sinT = [], []
    for c in range(NC):
        ang = pos[c].astype(np.float32)[:, None] * inv_freq[None, :]
        cosT.append(np.ascontiguousarray(np.concatenate([np.cos(ang), np.cos(ang)], axis=1).T.astype(np.float32)))
        sinT.append(np.ascontiguousarray(np.concatenate([np.sin(ang), np.sin(ang)], axis=1).T.astype(np.float32)))

    def mlpw(li):
        return {'g_mlp%d' % li: pp(P['mlp_norm'][li], DC), 'w_up%d' % li: P['mlp_w_up'][li], 'w_dn%d' % li: P['mlp_w_down'][li]}

    xT = [np.ascontiguousarray(x[bat[c], pos[c], :].T) for c in range(NC)]
    ims = [dict(cst=cst, xT=xT[c], g_an0=pp(P['attn_norm'][0], DC), w_in0=P['nsa_w_in'][0], cosT=cosT[c], sinT=sinT[c],
                nsa_qn=pp(P['nsa_q_norm'][0], 1), nsa_kn=np.ascontiguousarray(P['nsa_k_norm'][0].T)) for c in range(NC)]
    r0 = run(0, ims)
    if dbg is not None:
        dbg['r0'] = r0
    if upto <= 1:
        return r0
    NSEL = S // 64
    NCMP = S // 16 - 1
    NCC = (NCMP + 127) // 128
    NQT = NT // 128
    kcf, vcf, ksf, kwf = [to_global(r0, n, 1) for n in ('kcT', 'vcT', 'ksT', 'kwT')]
    vsf, vwf = [to_global(r0, n, 0) for n in ('vs', 'vw')]
    jj = np.arange(NSEL)
    Gm = (np.arange(S)[None, :] // 64 == jj[:, None]).astype(np.float32)
    nn = np.arange(NCC * 128)
    ovl = ((16 * nn[:, None] < 64 * jj[None, :] + 64) & (16 * nn[:, None] + 32 > 64 * jj[None, :]) & (nn[:, None] < NCMP)).astype(np.float32)
    Ov = np.ascontiguousarray(ovl.reshape(NCC, 128, NSEL).transpose(1, 0, 2))
    ss, tt = np.meshgrid(np.arange(128), np.arange(128), indexing='ij')
    tri = np.concatenate([np.where(ss <= tt, 0.0, NEG)] * 4 + [np.where(ss > tt, 0.0, NEG)] * 4, axis=1).astype(np.float32)
    selg = np.zeros((24, 24 * 128), np.float32)
    for r_ in range(24):
        selg[r_, r_ * 128:(r_ + 1) * 128] = 1.0
    ims = []
    for c in range(NC):
        b = bat[c]
        McT = np.zeros((128, NQT, NCC, 128), np.float32)
        sval = np.zeros((128, NQT, NSEL), np.float32)
        sbon = np.zeros((128, NQT, NSEL), np.float32)
        spneg = np.zeros((128, NQT, NSEL), np.float32)
        pcrow = np.zeros((1, NCH, 128), np.float32)
        kwp = np.zeros((256, NCH, 2 * CH), np.float32)
        vwp = np.zeros((2, 128, NCH, 8, 128), np.float32)
        for lc, g in enumerate(gch[c]):
            if g == 0:
                pcrow[0, lc, :] = NEG
            lo = (g - 1) * CH
            if g > 0:
                kwp[:, lc, 0:CH] = kwf[b][:, lo:lo + CH].astype(np.float32)
                vprev = vwf[b][lo:lo + CH].astype(np.float32)
            else:
                vprev = np.zeros((CH, 256), np.float32)
            kwp[:, lc, CH:] = kwf[b][:, g * CH:(g + 1) * CH].astype(np.float32)
            vboth = np.concatenate([vprev, vwf[b][g * CH:(g + 1) * CH].astype(np.float32)], axis=0)
            vwp[:, :, lc] = vboth.reshape(8, 128, 2, 128).transpose(2, 1, 0, 3)
            for u in range(4):
                qt = lc * 4 + u
                T = 4 * g + u
                tl = np.arange(128)
                tg = 128 * T + tl
                n_ = np.arange(NCC * 128).reshape(NCC, 128)
                vis = (16 * n_[:, :, None] + 31 <= tg[None, None, :]) & (n_[:, :, None] < NCMP)
                McT[:, qt, :, :] = np.where(vis, 0.0, NEG).transpose(1, 0, 2)
                cur = tg // 64
                le = jj[None, :] <= cur[:, None]
                forced = (jj[None, :] == 0) | (jj[None, :] == cur[:, None]) | (jj[None, :] == cur[:, None] - 1)
                sval[:, qt, :] = le
                sbon[:, qt, :] = np.where(le, 1000.0 * forced, -1.0)
                spneg[:, qt, :] = np.where(jj[None, :] >= 2 * T, NEG, 0.0)
        ksown = np.asarray(r0[c]['ksT'])
        ims.append(dict(cst=cst, qT=np.asarray(r0[c]['qT']), gT=np.asarray(r0[c]['gT']), selg=selg, G=Gm, Ov=Ov, tri=tri,
                        McT=McT, sval=sval, sbon=sbon, spneg=spneg, pcrow=pcrow,
                        cmp_posT=np.ascontiguousarray(P['nsa_cmp_pos'][0].transpose(2, 0, 1).reshape(128, 64)),
                        kcT_full=kcf[b], vcT_full=vcf[b], w_cmp=P['nsa_w_cmp'][0], ksT_full=ksf[b], vs_full=vlay(vsf[b], 2),
                        ksT_own=ksown, vs_own=np.asarray(r0[c]['vs']), kw_pack=kwp, vw_pack=vwp,
                        xT=xT[c], w_out0=P['nsa_w_out'][0], g_an1=pp(P['attn_norm'][1], DC), w_in1=P['sb_w_in'][0], **mlpw(0)))
    r1 = run(1, ims)
    if dbg is not None:
        dbg['r1'] = r1
    if upto <= 2:
        return r1
    ktf = to_global(r1, 'kT', 1)
    vf = to_global(r1, 'v', 0)
    js, s_ = np.meshgrid(np.arange(128), np.arange(128), indexing='ij')
    sbc = np.zeros((128, 256), np.float32)
    sbc[:, 0:128] = np.where(js >= s_, -1.0, 0.0)
    sbc[:, 128:256] = -1.0
    ims = []
    for c in range(NC):
        b = bat[c]
        sbm = np.zeros((128, NCH, 4 * R, CH), np.float32)
        for lc, g in enumerate(gch[c]):
            for i in range(4 * R):
                kt = 4 * cfg.GMIN[lc] + i
                sg_ = 128 * kt + np.arange(128)
                tg = CH * g + np.arange(CH)
                sbm[:, lc, i, :] = sg_[:, None] < tg[None, :]
        ims.append(dict(cst=cst, xT=np.asarray(r1[c]['xT_out']), qT=np.asarray(r1[c]['qT']), kT_full=ktf[b], v_full=vlay(vf[b], H),
                        sbmask=sbm, sbc=sbc, w_out1=P['sb_w_out'][0], g_an2=pp(P['attn_norm'][2], DC), conv_w_in=P['conv_w_in'][0], **mlpw(1)))
    r2 = run(2, ims)
    if dbg is not None:
        dbg['r2'] = r2
    if upto <= 3:
        return r2
    uf = to_global(r2, 'uT', 1)
    ims = []
    for c in range(NC):
        b = bat[c]
        uE = np.zeros((D, NCH, 32 + CH), np.float32)
        for lc, g in enumerate(gch[c]):
            if g > 0:
                uE[:, lc, 0:32] = uf[b][:, g * CH - 32:g * CH].astype(np.float32)
            uE[:, lc, 32:] = uf[b][:, g * CH:(g + 1) * CH].astype(np.float32)
        dw = P['conv_dw_w'][0]
        conv_dw = np.ascontiguousarray(dw.reshape(31, DC, 128).transpose(2, 1, 0).reshape(128, DC * 31))
        ims.append(dict(cst=cst, xT=np.asarray(r2[c]['xT_out']), uE=uE, conv_dw=conv_dw, conv_dwb=pp(P['conv_dw_b'][0], DC),
                        conv_lng=pp(P['conv_ln_g'][0], DC), conv_lnb=pp(P['conv_ln_b'][0], DC), w_out2=P['conv_w_out'][0],
                        g_an3=pp(P['attn_norm'][3], DC), w_in3=P['moba_w_in'][0], cosT=cosT[c], sinT=sinT[c],
                        moba_qn=pp(P['moba_q_norm'][0], 1), moba_kn=pp(P['moba_k_norm'][0], 1), **mlpw(2)))
    r3 = run(3, ims)
    if dbg is not None:
        dbg['r3'] = r3
    if upto <= 4:
        return r3
    NB = S // 256
    ktf = to_global(r3, 'kT', 1)
    vf = to_global(r3, 'v', 0)
    kmf = []
    for b in range(B):
        km = np.zeros((D, NB), np.float32)
        for c in range(NC):
            if bat[c] == b:
                a = np.asarray(r3[c]['kmean'])
                for lc, g in enumerate(gch[c]):
                    km[:, 2 * g:2 * g + 2] = a[:, 2 * lc:2 * lc + 2]
        kmf.append(km)
    esel = np.zeros((128, NB * 128), np.float32)
    for n_ in range(NB):
        esel[n_, n_ * 128:(n_ + 1) * 128] = 1.0
    om = np.zeros((128, 4, CH), np.float32)
    for i2 in range(4):
        sl_ = 128 * i2 + np.arange(128)
        tl = np.arange(CH)
        ok = (i2 // 2 == tl[None, :] // 256) & (sl_[:, None] <= tl[None, :])
        om[:, i2, :] = np.where(ok, 0.0, NEG)
    ims = []
    for c in range(NC):
        b = bat[c]
        mv = np.zeros((128, NCH, 4 * NB), np.float32)
        for lc, g in enumerate(gch[c]):
            for u in range(4):
                cur = 2 * g + u // 2
                mv[:, lc, u * NB:(u + 1) * NB] = np.where(np.arange(NB) < cur, 0.0, -1e30)[None, :]
        ims.append(dict(cst=cst, xT=np.asarray(r3[c]['xT_out']), qT=np.asarray(r3[c]['qT']), kT_full=ktf[b], v_full=vlay(vf[b], H),
                        kT_own=np.asarray(r3[c]['kT']), v_own=np.asarray(r3[c]['v']), kmean_full=kmf[b], mvalid=mv, esel=esel,
                        ownmask=om, w_out3=P['moba_w_out'][0], **mlpw(3)))
    r4 = run(4, ims)
    if dbg is not None:
        dbg['r4'] = r4
    out = np.zeros((B, S, D), np.float32)
    for c in range(NC):
        out[bat[c], pos[c], :] = np.asarray(r4[c]['xT_out']).T
    return out


def hw_runner(nc, in_maps):
    res = run_bass_kernel_spmd(nc, in_maps, core_ids=list(range(len(in_maps))))
    return res.results


def kernel(**inputs):
    return kernel_impl(inputs, Cfg(4, 2), hw_runner)
```
